# Optimizing a Trainium2 kernel written in Bass

```python
import math
import jax
import jax.numpy as jnp
from jax import lax
import numpy as np

D_MODEL = 4096
BATCH = 4
SEQ = 2048
DEPTH = 2
DEC_BATCH = 8
DEC_SEQ = 8
PAST_LEN = 16384
PAGE_SIZE = 128

GROUP_W = D_MODEL // 4
MIX_W = 4 * GROUP_W
GDN_HEADS = 8
GDN_DK = GROUP_W // GDN_HEADS
GDN_DV = GROUP_W // GDN_HEADS
GDN_QKV = 3 * GROUP_W
GDN_CONV = 4
GDN_CHUNK = 64
RWKV_HS = 64
RWKV_HEADS = GROUP_W // RWKV_HS
RWKV_W_LORA = 64
RWKV_A_LORA = 64
RWKV_G_LORA = 160
RWKV_PROJ = 3 * GROUP_W + RWKV_W_LORA + RWKV_A_LORA + RWKV_G_LORA
RWKV_GN_EPS = 64e-5
SSM_HEADDIM = 64
SSM_HEADS = GROUP_W // SSM_HEADDIM
SSM_GROUPS = 2
SSM_STATE = 128
SSM_CONV = 4
SSM_CHUNK = 128
SSM_XBC = GROUP_W + 2 * SSM_GROUPS * SSM_STATE
SWA_HEADS = 8
SWA_HD = GROUP_W // SWA_HEADS
SWA_PATTERNS = ((128, 1), (512, 4), (2048, 16))
SWA_MAX_WINDOW = 2048
D_FF = 256 * ((8 * D_MODEL // 3 + 255) // 256)
FFN_CONV = 3
NORM_EPS = 1e-6
NEG_INF = -1e30
IN_SIZES = (GDN_QKV, GROUP_W, GDN_HEADS, GDN_HEADS, RWKV_PROJ, GROUP_W, SSM_XBC, SSM_HEADS, 3 * GROUP_W)
N_IN = sum(IN_SIZES)

kernel_name = 'hybrid_parallel_heads_decode_step'


def rmsnorm(x, w):
    x32 = x.astype(jnp.float32)
    y = x32 * lax.rsqrt(jnp.mean(x32 * x32, axis=-1, keepdims=True) + NORM_EPS)
    return (y * w.astype(jnp.float32)).astype(x.dtype)


def l2norm(x):
    return x * lax.rsqrt(jnp.sum(x * x, axis=-1, keepdims=True) + 1e-6)


def split_cols(h, sizes):
    out, o = [], 0
    for s in sizes:
        out.append(h[..., o:o + s])
        o += s
    return out


def causal_dwconv(x, buf, w):
    n_tap, l = w.shape[0], x.shape[1]
    xp = jnp.concatenate([buf.astype(x.dtype), x], axis=1)
    y = xp[:, 0:l] * w[0]
    for i in range(1, n_tap):
        y = y + xp[:, i:i + l] * w[i]
    return y, xp[:, l:]


def alibi_slopes():
    return jnp.asarray(2.0 ** (-8.0 * np.arange(1, SWA_HEADS + 1) / SWA_HEADS), dtype=jnp.float32)


def gated_delta_chunked(q, k, v, g, beta, S0):
    bsz, l, H, dk = q.shape
    dv = v.shape[-1]
    C = GDN_CHUNK
    n = -(-l // C)
    pad = n * C - l

    def blocks(t):
        t = jnp.pad(t, [(0, 0), (0, pad)] + [(0, 0)] * (t.ndim - 2))
        t = jnp.moveaxis(t.reshape(bsz, n, C, *t.shape[2:]), 2, 3)
        return jnp.moveaxis(t, 1, 0)

    qc, kc, vc, gc, bc = [blocks(t) for t in (q, k, v, g, beta)]
    gcum = jnp.cumsum(gc, axis=-1)
    tri = jnp.tril(jnp.ones((C, C), bool))
    strict = jnp.tril(jnp.ones((C, C), bool), -1)
    gam = jnp.exp(jnp.where(tri, gcum[..., :, None] - gcum[..., None, :], -jnp.inf))
    kbeta = kc * bc[..., None]
    M = jnp.where(strict, jnp.einsum('nbhid,nbhjd->nbhij', kbeta, kc) * gam, 0.0)
    A = M + jnp.eye(C, dtype=jnp.float32)
    W = lax.linalg.triangular_solve(A, kbeta * jnp.exp(gcum)[..., None], left_side=True, lower=True, unit_diagonal=True)
    U = lax.linalg.triangular_solve(A, vc * bc[..., None], left_side=True, lower=True, unit_diagonal=True)
    Aqk = jnp.einsum('nbhid,nbhjd->nbhij', qc, kc) * gam
    qd = qc * jnp.exp(gcum)[..., None]
    kd = kc * jnp.exp(gcum[..., -1:] - gcum)[..., None]
    glast = jnp.exp(gcum[..., -1])

    def step(S, inp):
        W_i, U_i, Aqk_i, qd_i, kd_i, gl_i = inp
        v_new = U_i - jnp.einsum('bhcd,bhde->bhce', W_i, S)
        o = jnp.einsum('bhcd,bhde->bhce', qd_i, S) + jnp.einsum('bhij,bhje->bhie', Aqk_i, v_new)
        S = S * gl_i[..., None, None] + jnp.einsum('bhcd,bhce->bhde', kd_i, v_new)
        return S, o

    S, o = lax.scan(step, S0.astype(jnp.float32), (W, U, Aqk, qd, kd, glast))
    o = jnp.moveaxis(jnp.moveaxis(o, 0, 1), 2, 3).reshape(bsz, n * C, H, dv)[:, :l]
    return o, S


def gdn_mixer(qkv, z, b_raw, a_raw, conv_buf, S0, conv_w, A_log, dt_bias, norm_w):
    bsz, l, _ = qkv.shape
    y, new_buf = causal_dwconv(qkv, conv_buf, conv_w)
    y = jax.nn.silu(y.astype(jnp.float32))
    q, k, v = split_cols(y, (GROUP_W, GROUP_W, GROUP_W))
    q = l2norm(q.reshape(bsz, l, GDN_HEADS, GDN_DK)) * (GDN_DK ** -0.5)
    k = l2norm(k.reshape(bsz, l, GDN_HEADS, GDN_DK))
    v = v.reshape(bsz, l, GDN_HEADS, GDN_DV)
    beta = jax.nn.sigmoid(b_raw.astype(jnp.float32))
    g = -jnp.exp(A_log.astype(jnp.float32)) * jax.nn.softplus(a_raw.astype(jnp.float32) + dt_bias)
    o, S = gated_delta_chunked(q, k, v, g, beta, S0)
    o = o * lax.rsqrt(jnp.mean(o * o, -1, keepdims=True) + NORM_EPS) * norm_w
    o = o * jax.nn.silu(z.astype(jnp.float32).reshape(bsz, l, GDN_HEADS, GDN_DV))
    return o.reshape(bsz, l, GROUP_W), new_buf, S


def rwkv7_scan(r, w, k, v, a, b, S0):
    def step(S, inp):
        r_t, w_t, k_t, v_t, a_t, b_t = inp
        sa = jnp.einsum('bhij,bhj->bhi', S, a_t)
        S = S * w_t[:, :, None, :] + sa[..., None] * b_t[:, :, None, :] + v_t[..., None] * k_t[:, :, None, :]
        return S, jnp.einsum('bhij,bhj->bhi', S, r_t)

    xs = tuple(jnp.moveaxis(t, 1, 0) for t in (r, w, k, v, a, b))
    S, ys = lax.scan(step, S0.astype(jnp.float32), xs)
    return jnp.moveaxis(ys, 0, 1), S


def rwkv7_mixer(zb, shift_prev, S0, mu, w0, w2, a0, a2, g2, k_k, k_a, r_k, ln_w, ln_b):
    bsz, l, _ = zb.shape
    z32 = zb.astype(jnp.float32)
    prev = jnp.concatenate([shift_prev[:, None].astype(jnp.float32), z32[:, :-1]], axis=1)
    zm = z32 + (prev - z32) * mu
    r, k, v, wd, ad, gd = split_cols(zm, (GROUP_W, GROUP_W, GROUP_W, RWKV_W_LORA, RWKV_A_LORA, RWKV_G_LORA))
    w_log = -jax.nn.softplus(-(w0 + jnp.tanh(wd) @ w2)) - 0.5
    decay = jnp.exp(-jnp.exp(w_log))
    a = jax.nn.sigmoid(a0 + ad @ a2)
    gate = jax.nn.sigmoid(gd) @ g2

    def hs(t):
        return t.reshape(bsz, l, RWKV_HEADS, RWKV_HS)

    kk = l2norm(hs(k * k_k))
    k = k * (1.0 + (a - 1.0) * k_a)
    r_h, k_h, v_h, a_h = hs(r), hs(k), hs(v), hs(a)
    y, S = rwkv7_scan(r_h, hs(decay), k_h, v_h, -kk, kk * a_h, S0)
    mean = jnp.mean(y, -1, keepdims=True)
    var = jnp.mean(jnp.square(y - mean), -1, keepdims=True)
    yn = ((y - mean) * lax.rsqrt(var + RWKV_GN_EPS)).reshape(bsz, l, GROUP_W) * ln_w + ln_b
    bonus = jnp.sum(r_h * k_h * r_k, -1, keepdims=True) * v_h
    out = (yn + bonus.reshape(bsz, l, GROUP_W)) * gate
    return out, zb[:, -1], S


def ssd_chunked(X, dA, Bm, Cm, S0):
    bsz, l, H, P = X.shape
    G, N = Bm.shape[2], Bm.shape[3]
    R = H // G
    Q = SSM_CHUNK
    n = -(-l // Q)
    pad = n * Q - l

    def padl(t):
        return jnp.pad(t, [(0, 0), (0, pad)] + [(0, 0)] * (t.ndim - 2))

    Xc = padl(X).reshape(bsz, n, Q, G, R, P)
    Ac = padl(dA).reshape(bsz, n, Q, G, R)
    Bc = padl(Bm).reshape(bsz, n, Q, G, N)
    Cc = padl(Cm).reshape(bsz, n, Q, G, N)
    Acs = jnp.cumsum(Ac, axis=2)
    Acs_t = jnp.moveaxis(Acs, 2, -1)
    tri = jnp.tril(jnp.ones((Q, Q), bool))
    Lmat = jnp.exp(jnp.where(tri, Acs_t[..., :, None] - Acs_t[..., None, :], -jnp.inf))
    CB = jnp.einsum('bclgn,bcsgn->bcgls', Cc, Bc)
    y_diag = jnp.einsum('bcgls,bcgrls,bcsgrp->bclgrp', CB, Lmat, Xc)
    decay_st = jnp.exp(Acs[:, :, -1:] - Acs)
    chunk_states = jnp.einsum('bclgn,bclgr,bclgrp->bcgrpn', Bc, decay_st, Xc)
    chunk_decay = jnp.exp(Acs[:, :, -1])

    def step(S, inp):
        st, dec = inp
        return S * dec[..., None, None] + st, S

    S_fin, S_in = lax.scan(step, S0.reshape(bsz, G, R, P, N).astype(jnp.float32),
                           (jnp.moveaxis(chunk_states, 1, 0), jnp.moveaxis(chunk_decay, 1, 0)))
    S_in = jnp.moveaxis(S_in, 0, 1)
    y_off = jnp.einsum('bclgn,bcgrpn,bclgr->bclgrp', Cc, S_in, jnp.exp(Acs))
    y = (y_diag + y_off).reshape(bsz, n * Q, H, P)[:, :l]
    return y, S_fin.reshape(bsz, H, P, N)


def mamba2_mixer(z, xbc, dt_raw, conv_buf, S0, conv_w, conv_b, dt_bias, A_log, d_skip, norm_w):
    bsz, l, _ = xbc.shape
    y, new_buf = causal_dwconv(xbc, conv_buf, conv_w)
    y = jax.nn.silu((y + conv_b).astype(jnp.float32))
    xs, Bm, Cm = split_cols(y, (GROUP_W, SSM_GROUPS * SSM_STATE, SSM_GROUPS * SSM_STATE))
    xs = xs.reshape(bsz, l, SSM_HEADS, SSM_HEADDIM)
    Bm = Bm.reshape(bsz, l, SSM_GROUPS, SSM_STATE)
    Cm = Cm.reshape(bsz, l, SSM_GROUPS, SSM_STATE)
    dt = jax.nn.softplus(dt_raw.astype(jnp.float32) + dt_bias)
    A = -jnp.exp(A_log.astype(jnp.float32))
    yh, S = ssd_chunked(xs * dt[..., None], dt * A, Bm, Cm, S0)
    yh = yh + xs * d_skip[:, None]
    yg = yh.reshape(bsz, l, GROUP_W) * jax.nn.silu(z.astype(jnp.float32))
    yg = yg.reshape(bsz, l, SSM_GROUPS, GROUP_W // SSM_GROUPS)
    yg = yg * lax.rsqrt(jnp.mean(yg * yg, -1, keepdims=True) + NORM_EPS)
    return yg.reshape(bsz, l, GROUP_W) * norm_w, new_buf, S


def dilated_band(q, k, v, slopes, window, dil):
    bsz, S, H, E = q.shape
    band = window // dil
    L = S // dil
    nb = -(-L // band)
    Lp = nb * band
    q, k, v = [t.astype(jnp.float32) for t in (q, k, v)]

    def sub(t):
        return jnp.moveaxis(t.reshape(bsz, L, dil, H, E), 2, 1)

    def pad_seq(t, front, back):
        return jnp.pad(t, ((0, 0), (0, 0), (front, back), (0, 0), (0, 0)))

    qb = pad_seq(sub(q), 0, Lp - L).reshape(bsz, dil, nb, band, H, E)

    def key_blocks(t):
        tp = pad_seq(sub(t), band, Lp - L).reshape(bsz, dil, nb + 1, band, H, E)
        return jnp.concatenate([tp[:, :, :-1], tp[:, :, 1:]], axis=3)

    kb, vb = key_blocks(k), key_blocks(v)
    s = jnp.einsum('bdnqhe,bdnkhe->bdnhqk', qb, kb) * (E ** -0.5)
    qi = jnp.arange(band)[:, None]
    ki = jnp.arange(2 * band)[None, :]
    dist = qi + band - ki
    blk = jnp.arange(nb)[:, None, None]
    valid = (dist >= 0) & (dist <= band) & (blk * band - band + ki >= 0)
    s = s - slopes[:, None, None] * (dist * dil).astype(jnp.float32)
    s = jnp.where(valid[:, None], s, NEG_INF)
    m = jnp.max(s, axis=-1)
    p = jnp.exp(s - m[..., None])
    den = jnp.sum(p, axis=-1)
    num = jnp.einsum('bdnhqk,bdnkhe->bdnqhe', p, vb)

    def unblock(t):
        t = t.reshape(bsz, dil, Lp, *t.shape[4:])[:, :, :L]
        return jnp.moveaxis(t, 1, 2).reshape(bsz, S, *t.shape[3:])

    return unblock(jnp.moveaxis(m, 3, 4)), unblock(jnp.moveaxis(den, 3, 4)), unblock(num)


def dilated_sample(q, k_all, v_all, slopes, wb):
    bsz, T, H, E = q.shape
    q = q.astype(jnp.float32)
    parts = []
    for window, dil in SWA_PATTERNS:
        steps = np.arange(window // dil + 1)
        idx = wb + np.arange(T)[:, None] - steps[None, :] * dil
        valid = jnp.asarray(idx >= 0)
        idx = np.maximum(idx, 0)
        kg = k_all[:, idx].astype(jnp.float32)
        vg = v_all[:, idx].astype(jnp.float32)
        s = jnp.einsum('bthe,btnhe->bthn', q, kg) * (E ** -0.5)
        s = s - slopes[:, None] * jnp.asarray(steps * dil, jnp.float32)
        s = jnp.where(valid[:, None, :], s, NEG_INF)
        m = jnp.max(s, axis=-1)
        p = jnp.exp(s - m[..., None])
        parts.append((m, jnp.sum(p, axis=-1), jnp.einsum('bthn,btnhe->bthe', p, vg)))
    return parts


def combine_groups(parts):
    M = parts[0][0]
    for m, _, _ in parts[1:]:
        M = jnp.maximum(M, m)
    num, den = 0.0, 0.0
    for m, s, n in parts:
        c = jnp.exp(m - M)
        num = num + c[..., None] * n
        den = den + c * s
    return num / den[..., None]


def zero_past(bsz, dtype):
    f32 = jnp.float32
    return {
        'gdn': jnp.zeros((bsz, GDN_HEADS, GDN_DK, GDN_DV), f32),
        'gdn_conv': jnp.zeros((bsz, GDN_CONV - 1, GDN_QKV), dtype),
        'rwkv': jnp.zeros((bsz, RWKV_HEADS, RWKV_HS, RWKV_HS), f32),
        'rwkv_shift': jnp.zeros((bsz, RWKV_PROJ), dtype),
        'ssm': jnp.zeros((bsz, SSM_HEADS, SSM_HEADDIM, SSM_STATE), f32),
        'ssm_conv': jnp.zeros((bsz, SSM_CONV - 1, SSM_XBC), dtype),
        'ffn_conv': jnp.zeros((bsz, FFN_CONV - 1, D_FF), dtype),
    }


def decoder_layer(x, prm, slopes, past, swa_cache):
    bsz, l, _ = x.shape
    h = rmsnorm(x, prm['norm_mix_pre'])
    proj = h @ prm['w_in']
    gdn_qkv, gdn_z, gdn_b, gdn_a, rwkv_in, ssm_z, ssm_xbc, ssm_dt, swa_qkv = split_cols(proj, IN_SIZES)

    o_a, gdn_conv, gdn_S = gdn_mixer(gdn_qkv, gdn_z, gdn_b, gdn_a, past['gdn_conv'], past['gdn'],
                                     prm['gdn_conv_w'], prm['gdn_A_log'], prm['gdn_dt_bias'], prm['gdn_norm_w'])
    o_b, rwkv_shift, rwkv_S = rwkv7_mixer(rwkv_in, past['rwkv_shift'], past['rwkv'], prm['rwkv_mu'],
                                          prm['rwkv_w0'], prm['rwkv_w2'], prm['rwkv_a0'], prm['rwkv_a2'],
                                          prm['rwkv_g2'], prm['rwkv_k_k'], prm['rwkv_k_a'], prm['rwkv_r_k'],
                                          prm['rwkv_ln_w'], prm['rwkv_ln_b'])
    o_c, ssm_conv, ssm_S = mamba2_mixer(ssm_z, ssm_xbc, ssm_dt, past['ssm_conv'], past['ssm'],
                                        prm['ssm_conv_w'], prm['ssm_conv_b'], prm['ssm_dt_bias'],
                                        prm['ssm_A_log'], prm['ssm_D'], prm['ssm_norm_w'])
    q, k, v = [t.reshape(bsz, l, SWA_HEADS, SWA_HD) for t in split_cols(swa_qkv, (GROUP_W, GROUP_W, GROUP_W))]
    if swa_cache is None:
        o_d = combine_groups([dilated_band(q, k, v, slopes, w, d) for (w, d) in SWA_PATTERNS])
        keep = min(SWA_MAX_WINDOW, l)
        k_rows, v_rows = k[:, l - keep:], v[:, l - keep:]
    else:
        ck, cv = swa_cache
        wb = ck.shape[1]
        k_all = jnp.concatenate([ck.astype(k.dtype), k], axis=1)
        v_all = jnp.concatenate([cv.astype(v.dtype), v], axis=1)
        o_d = combine_groups(dilated_sample(q, k_all, v_all, slopes, wb))
        k_rows, v_rows = k, v

    mix = jnp.concatenate([o_a, o_b, o_c, o_d.reshape(bsz, l, GROUP_W)], axis=-1).astype(x.dtype)
    x = x + rmsnorm(mix @ prm['w_out'], prm['norm_mix_post'])

    h2 = rmsnorm(x, prm['norm_ffn_pre'])
    up = h2 @ prm['ffn_w_up']
    gate, val = up[..., :D_FF], up[..., D_FF:]
    gconv, ffn_conv = causal_dwconv(gate, past['ffn_conv'], prm['ffn_conv_w'])
    act = jax.nn.silu((gconv + prm['ffn_conv_b']).astype(jnp.float32)) * val.astype(jnp.float32)
    x = x + rmsnorm(act.astype(x.dtype) @ prm['ffn_w_down'], prm['norm_ffn_post'])
    return x, (gdn_S, gdn_conv, rwkv_S, rwkv_shift, ssm_S, ssm_conv, k_rows, v_rows, ffn_conv)


def setup_inputs(seed: int = 0) -> dict:
    key = jax.random.key(seed)
    keys = iter(jax.random.split(key, 64))
    f32 = jnp.float32

    def nrm(shape, scale):
        return scale * jax.random.normal(next(keys), shape, f32)

    def gain(shape):
        return 1.0 + 0.02 * jax.random.normal(next(keys), shape, f32)

    def unif(shape, lo, hi):
        return jax.random.uniform(next(keys), shape, f32, lo, hi)

    def inv_softplus_dt(shape):
        dt = jnp.exp(unif(shape, math.log(1e-3), math.log(1e-1)))
        return dt + jnp.log(-jnp.expm1(-dt))

    L, DB = DEPTH, DEC_BATCH
    wb = min(SWA_MAX_WINDOW, PAST_LEN)
    return {
        'x_prompt': nrm((BATCH, SEQ, D_MODEL), 1.0),
        'x_sample': nrm((DEC_BATCH, DEC_SEQ, D_MODEL), 1.0),
        'state_gdn': nrm((L, DB, GDN_HEADS, GDN_DK, GDN_DV), 0.1),
        'state_gdn_conv': nrm((L, DB, GDN_CONV - 1, GDN_QKV), 1.0),
        'state_rwkv': nrm((L, DB, RWKV_HEADS, RWKV_HS, RWKV_HS), 0.1),
        'state_rwkv_shift': nrm((L, DB, RWKV_PROJ), 1.0),
        'state_ssm': nrm((L, DB, SSM_HEADS, SSM_HEADDIM, SSM_STATE), 0.1),
        'state_ssm_conv': nrm((L, DB, SSM_CONV - 1, SSM_XBC), 1.0),
        'cache_swa_k': nrm((L, DB, wb, SWA_HEADS, SWA_HD), 1.0),
        'cache_swa_v': nrm((L, DB, wb, SWA_HEADS, SWA_HD), 1.0),
        'state_ffn_conv': nrm((L, DB, FFN_CONV - 1, D_FF), 1.0),
        'norm_mix_pre': gain((L, D_MODEL)),
        'norm_mix_post': gain((L, D_MODEL)),
        'norm_ffn_pre': gain((L, D_MODEL)),
        'norm_ffn_post': gain((L, D_MODEL)),
        'w_in': nrm((L, D_MODEL, N_IN), D_MODEL ** -0.5),
        'w_out': nrm((L, MIX_W, D_MODEL), MIX_W ** -0.5),
        'gdn_conv_w': nrm((L, GDN_CONV, GDN_QKV), GDN_CONV ** -0.5),
        'gdn_A_log': jnp.log(unif((L, GDN_HEADS), 1.0, 16.0)),
        'gdn_dt_bias': inv_softplus_dt((L, GDN_HEADS)),
        'gdn_norm_w': gain((L, GDN_DV)),
        'rwkv_mu': unif((L, RWKV_PROJ), 0.0, 1.0),
        'rwkv_w0': unif((L, GROUP_W), -6.0, -1.0),
        'rwkv_w2': nrm((L, RWKV_W_LORA, GROUP_W), 0.1 * RWKV_W_LORA ** -0.5),
        'rwkv_a0': nrm((L, GROUP_W), 0.1),
        'rwkv_a2': nrm((L, RWKV_A_LORA, GROUP_W), RWKV_A_LORA ** -0.5),
        'rwkv_g2': nrm((L, RWKV_G_LORA, GROUP_W), RWKV_G_LORA ** -0.5),
        'rwkv_k_k': 0.85 + nrm((L, GROUP_W), 0.02),
        'rwkv_k_a': gain((L, GROUP_W)),
        'rwkv_r_k': nrm((L, RWKV_HEADS, RWKV_HS), 0.1),
        'rwkv_ln_w': gain((L, GROUP_W)),
        'rwkv_ln_b': nrm((L, GROUP_W), 0.02),
        'ssm_conv_w': nrm((L, SSM_CONV, SSM_XBC), SSM_CONV ** -0.5),
        'ssm_conv_b': nrm((L, SSM_XBC), 0.02),
        'ssm_dt_bias': inv_softplus_dt((L, SSM_HEADS)),
        'ssm_A_log': jnp.log(unif((L, SSM_HEADS), 1.0, 16.0)),
        'ssm_D': gain((L, SSM_HEADS)),
        'ssm_norm_w': gain((L, GROUP_W)),
        'ffn_w_up': nrm((L, D_MODEL, 2 * D_FF), D_MODEL ** -0.5),
        'ffn_conv_w': nrm((L, FFN_CONV, D_FF), FFN_CONV ** -0.5),
        'ffn_conv_b': nrm((L, D_FF), 0.02),
        'ffn_w_down': nrm((L, D_FF, D_MODEL), D_FF ** -0.5),
    }


def reference(x_prompt, x_sample, state_gdn, state_gdn_conv, state_rwkv, state_rwkv_shift, state_ssm,
              state_ssm_conv, cache_swa_k, cache_swa_v, state_ffn_conv, norm_mix_pre, norm_mix_post,
              norm_ffn_pre, norm_ffn_post, w_in, w_out, gdn_conv_w, gdn_A_log, gdn_dt_bias, gdn_norm_w,
              rwkv_mu, rwkv_w0, rwkv_w2, rwkv_a0, rwkv_a2, rwkv_g2, rwkv_k_k, rwkv_k_a, rwkv_r_k,
              rwkv_ln_w, rwkv_ln_b, ssm_conv_w, ssm_conv_b, ssm_dt_bias, ssm_A_log, ssm_D, ssm_norm_w,
              ffn_w_up, ffn_conv_w, ffn_conv_b, ffn_w_down):
    slopes = alibi_slopes()
    xp, xs = x_prompt, x_sample
    prompt_states, sample_states = [], []
    for li in range(DEPTH):
        prm = {
            'norm_mix_pre': norm_mix_pre[li], 'norm_mix_post': norm_mix_post[li],
            'norm_ffn_pre': norm_ffn_pre[li], 'norm_ffn_post': norm_ffn_post[li],
            'w_in': w_in[li], 'w_out': w_out[li],
            'gdn_conv_w': gdn_conv_w[li], 'gdn_A_log': gdn_A_log[li], 'gdn_dt_bias': gdn_dt_bias[li],
            'gdn_norm_w': gdn_norm_w[li],
            'rwkv_mu': rwkv_mu[li], 'rwkv_w0': rwkv_w0[li], 'rwkv_w2': rwkv_w2[li], 'rwkv_a0': rwkv_a0[li],
            'rwkv_a2': rwkv_a2[li], 'rwkv_g2': rwkv_g2[li], 'rwkv_k_k': rwkv_k_k[li], 'rwkv_k_a': rwkv_k_a[li],
            'rwkv_r_k': rwkv_r_k[li], 'rwkv_ln_w': rwkv_ln_w[li], 'rwkv_ln_b': rwkv_ln_b[li],
            'ssm_conv_w': ssm_conv_w[li], 'ssm_conv_b': ssm_conv_b[li], 'ssm_dt_bias': ssm_dt_bias[li],
            'ssm_A_log': ssm_A_log[li], 'ssm_D': ssm_D[li], 'ssm_norm_w': ssm_norm_w[li],
            'ffn_w_up': ffn_w_up[li], 'ffn_conv_w': ffn_conv_w[li], 'ffn_conv_b': ffn_conv_b[li],
            'ffn_w_down': ffn_w_down[li],
        }
        xp, stp = decoder_layer(xp, prm, slopes, zero_past(xp.shape[0], xp.dtype), None)
        past = {
            'gdn': state_gdn[li], 'gdn_conv': state_gdn_conv[li], 'rwkv': state_rwkv[li],
            'rwkv_shift': state_rwkv_shift[li], 'ssm': state_ssm[li], 'ssm_conv': state_ssm_conv[li],
            'ffn_conv': state_ffn_conv[li],
        }
        xs, sts = decoder_layer(xs, prm, slopes, past, (cache_swa_k[li], cache_swa_v[li]))
        prompt_states.append(stp)
        sample_states.append(sts)
    (p_gdn, p_gdn_conv, p_rwkv, p_rwkv_shift, p_ssm, p_ssm_conv, p_swa_k, p_swa_v,
     p_ffn_conv) = [jnp.stack(t) for t in zip(*prompt_states)]
    (s_gdn, s_gdn_conv, s_rwkv, s_rwkv_shift, s_ssm, s_ssm_conv, s_swa_k, s_swa_v,
     s_ffn_conv) = [jnp.stack(t) for t in zip(*sample_states)]
    return (xp, xs, p_gdn, p_gdn_conv, p_rwkv, p_rwkv_shift, p_ssm, p_ssm_conv, p_swa_k, p_swa_v, p_ffn_conv,
            s_gdn, s_gdn_conv, s_rwkv, s_rwkv_shift, s_ssm, s_ssm_conv, s_swa_k, s_swa_v, s_ffn_conv)
```

```python
import math
from concourse.bass_utils import run_bass_kernel_spmd
import numpy as np
from contextlib import ExitStack
import concourse.bass as bass
import concourse.mybir as mybir
F32 = mybir.dt.float32; BF16 = mybir.dt.bfloat16; I32 = mybir.dt.int32
AF = mybir.ActivationFunctionType; ALU = mybir.AluOpType
AX = mybir.AxisListType

SEM_LIMIT = 20000
SAME_ENG_RAW = True

class Tl:
    def __init__(self, k, name, h, space):
        self.k = k; self.name = name; self.h = h; self.space = space
        self.w = {}; self.r = {}
        self.ds = None
    def __getitem__(self, idx):
        return self.h[idx]
    def ap(self):
        return self.h.ap() if self.space == 'dram' else self.h[:]

class DS:
    def __init__(self, sem, name):
        self.sem = sem; self.cnt = 0; self.name = name

class KB:
    def __init__(self, nc, es):
        self.nc = nc; self.es = es
        self.E = {'pe': nc.tensor, 'act': nc.scalar, 'dve': nc.vector, 'pool': nc.gpsimd, 'sp': nc.sync}
        self.esem = {}; self.ecnt = {}; self.eep = {}
        for e in self.E:
            self.eep[e] = 0; self.ecnt[e] = 0
            self.esem[e] = self._sem(f"e_{e}_0")
        self.waited = {}
        self.ds_free = []; self.ds_all = []
        self.nsem = 0
        self.phase_stack = None
        self.phase_tiles = []
        self.persist_tiles = []
        self.phase_ds = []
        self.ninst = 0
    def _sem(self, name):
        s = self.es.enter_context(self.nc.semaphore(name))
        self.nsem = getattr(self, 'nsem', 0) + 1
        return s
    def dram(self, name, shape, dtype, kind="Internal"):
        h = self.nc.dram_tensor(name, list(shape), dtype, kind=kind)
        t = Tl(self, name, h, 'dram')
        self.persist_tiles.append(t)
        return t
    def sb(self, name, shape, dtype, persist=False):
        self.uid = getattr(self, 'uid', 0) + 1; name = f"{name}_u{self.uid}"
        base = self.scope_stack if getattr(self, 'scope_stack', None) is not None else self.es
        st = base if (persist or self.phase_stack is None) else self.phase_stack
        h = st.enter_context(self.nc.sbuf_tensor(name, list(shape), dtype))
        t = Tl(self, name, h, 'sb')
        (self.persist_tiles if st is not self.phase_stack else self.phase_tiles).append(t)
        return t
    def scope_begin(self):
        assert self.phase_stack is None and getattr(self, 'scope_stack', None) is None
        self.scope_stack = ExitStack(); self.scope_stack.__enter__()
    def scope_end(self):
        assert self.phase_stack is None
        self.barrier()
        self.scope_stack.__exit__(None, None, None)
        self.scope_stack = None
    def ps(self, name, shape, dtype=F32, persist=False):
        self.uid = getattr(self, 'uid', 0) + 1; name = f"{name}_u{self.uid}"
        st = self.es if (persist or self.phase_stack is None) else self.phase_stack
        h = st.enter_context(self.nc.psum_tensor(name, list(shape), dtype))
        t = Tl(self, name, h, 'ps')
        (self.persist_tiles if st is self.es else self.phase_tiles).append(t)
        return t
    def get_ds(self, t):
        if t.ds is None:
            if self.ds_free:
                t.ds = self.ds_free.pop()
            else:
                t.ds = DS(self._sem(f"d{len(self.ds_all)}"), f"d{len(self.ds_all)}")
                self.ds_all.append(t.ds)
            if t in self.phase_tiles:
                self.phase_ds.append(t.ds)
        return t.ds
    def _wait(self, eng, deps, skip_self_for=None):
        for key, (sem, val) in deps.items():
            wk = (eng, key)
            if self.waited.get(wk, 0) >= val:
                continue
            self.E[eng].wait_ge(sem, val)
            self.waited[wk] = val
    def _deps(self, eng, reads, writes):
        deps = {}
        own = f"e_{eng}_"
        def add(d, raw):
            for key, (sem, val) in d.items():
                if key.startswith(own):
                    if eng == 'pe' or not (raw and SAME_ENG_RAW):
                        continue
                if key not in deps or deps[key][1] < val:
                    deps[key] = (sem, val)
        for t in reads:
            add(t.w, True)
            if t.space == 'ps':
                add(t.r, False)
        for t in writes:
            add(t.w, False); add(t.r, False)
        return deps
    def _tok(self, eng):
        if self.ecnt[eng] >= SEM_LIMIT:
            self.eep[eng] += 1; self.ecnt[eng] = 0
            self.esem[eng] = self._sem(f"e_{eng}_{self.eep[eng]}")
        self.ecnt[eng] += 1
        return f"e_{eng}_{self.eep[eng]}", self.esem[eng], self.ecnt[eng]
    def op(self, eng, fn, reads=(), writes=()):
        deps = self._deps(eng, reads, writes)
        self._wait(eng, deps)
        key, sem, val = self._tok(eng)
        inst = fn(self.E[eng])
        inst.then_inc(sem, 1)
        self.ninst += 1
        for t in reads:
            t.r[key] = (sem, val)
        for t in writes:
            t.w = {key: (sem, val)}; t.r = {}
        return inst
    def dma(self, q, out, in_, reads, writes, sbt, **kw):
        deps = self._deps(q, reads, writes)
        self._wait(q, deps)
        ds = self.get_ds(sbt)
        inst = self.E[q].dma_start(out=out, in_=in_, **kw)
        ds.cnt += 1
        inst.then_inc(ds.sem, 16)
        self.ninst += 1
        tok = (ds.sem, 16 * ds.cnt)
        for t in reads:
            t.r[ds.name] = tok
        for t in writes:
            if t.space == 'dram':
                t.w[ds.name] = tok
            else:
                t.w = {ds.name: tok}; t.r = {}
        return inst
    def barrier(self, engines=None):
        engines = engines or list(self.E)
        toks = {}
        for e in self.E:
            if self.ecnt[e] > 0:
                toks[f"e_{e}_{self.eep[e]}"] = (self.esem[e], self.ecnt[e])
        for ds in self.ds_all:
            if ds.cnt > 0:
                toks[ds.name] = (ds.sem, 16 * ds.cnt)
        for e in engines:
            own = f"e_{e}_"
            self._wait(e, {k: v for k, v in toks.items() if not k.startswith(own)})
        for t in self.persist_tiles + self.phase_tiles:
            t.w = {}; t.r = {}
    def phase_begin(self):
        assert self.phase_stack is None
        self.phase_stack = ExitStack()
        self.phase_stack.__enter__()
        self.phase_tiles = []; self.phase_ds = []
    def phase_end(self):
        self.barrier()
        for ds in self.phase_ds:
            self.ds_free.append(ds)
        self.phase_stack.__exit__(None, None, None)
        self.phase_stack = None
        self.phase_tiles = []; self.phase_ds = []

def bcast_rows(ap_row, nparts):
    return ap_row.partition_broadcast(nparts) if hasattr(ap_row, 'partition_broadcast') else ap_row

class Ctx:
    pass

def make_ident(k):
    idf = k.sb('ident_f', [128, 128], F32, persist=True)
    idb = k.sb('ident_b', [128, 128], BF16, persist=True)
    k.op('pool', lambda e: e.memset(idf[:], 1.0), [], [idf])
    k.op('pool', lambda e: e.affine_select(out=idf[:], in_=idf[:], pattern=[[-1, 128]], compare_op=ALU.is_equal,
                                             fill=0.0, base=0, channel_multiplier=1), [idf], [idf])
    k.op('dve', lambda e: e.tensor_copy(out=idb[:], in_=idf[:]), [idf], [idb])
    return idf, idb

def norm_phase(k, c, name, NTOK, D, x_src, o_src, wpost_row, wpre_row, x_dst, hT_dst, eps=1e-6):
    KC = D // 128
    k.phase_begin()
    wB = {}
    for nm, wr in (('post', wpost_row), ('pre', wpre_row)):
        if wr is None: continue
        t = k.sb(f'{name}_w{nm}', [128, D], F32)
        k.dma('sp', t[:], wr[1].partition_broadcast(128), [wr[0]], [t], t)
        wB[nm] = t
    NS = 2
    xs = [k.sb(f'{name}_x{i}', [128, D], F32) for i in range(NS)]
    os_ = [k.sb(f'{name}_o{i}', [128, D], F32) for i in range(NS)] if o_src else None
    junk = k.sb(f'{name}_junk', [128, D], BF16)
    xn = [k.sb(f'{name}_xn{i}', [128, D], BF16) for i in range(NS)]
    st = [k.sb(f'{name}_st{i}', [128, 8], F32) for i in range(NS)]
    GT = 512
    hst = [k.sb(f'{name}_h{i}', [128, KC, GT], BF16) for i in range(2)] if hT_dst else None
    pst = [k.ps(f'{name}_p{i}', [128, 2048], BF16) for i in range(2)] if hT_dst else None
    def row_ap(pieces, r0, n):
        off = 0
        for (tl, ap, nr) in pieces:
            if r0 >= off and r0 + n <= off + nr:
                return tl, ap[r0 - off:r0 - off + n, :]
            off += nr
        raise ValueError((r0, n))
    ntile = (NTOK + 127) // 128
    it = 0
    pcount = 0
    for g0 in range(0, NTOK, GT):
        gn = min(GT, NTOK - g0)
        gi = (g0 // GT) % 2
        for t0 in range(g0, g0 + gn, 128):
            n = min(128, NTOK - t0)
            s = it % NS; it += 1
            X = xs[s]
            tl, ap = row_ap(x_src, t0, n)
            k.dma('sp', X[0:n, :], ap, [tl], [X], X)
            S = st[s]
            if o_src:
                O = os_[s]
                tl, ap = row_ap(o_src, t0, n)
                k.dma('sp', O[0:n, :], ap, [tl], [O], O)
                k.op('act', lambda e: e.activation(out=junk[0:n, :], in_=O[0:n, :], func=AF.Square, accum_out=S[0:n, 0:1]), [O], [junk, S])
                k.op('dve', lambda e: e.tensor_scalar(out=S[0:n, 1:2], in0=S[0:n, 0:1], scalar1=1.0 / D, scalar2=eps, op0=ALU.mult, op1=ALU.add), [S], [S])
                k.op('act', lambda e: e.activation(out=S[0:n, 2:3], in_=S[0:n, 1:2], func=AF.Sqrt), [S], [S])
                k.op('dve', lambda e: e.reciprocal(out=S[0:n, 3:4], in_=S[0:n, 2:3]), [S], [S])
                k.op('dve', lambda e: e.scalar_tensor_tensor(out=O[0:n, :], in0=O[0:n, :], scalar=S[0:n, 3:4], in1=wB['post'][0:n, :], op0=ALU.mult, op1=ALU.mult), [O, S, wB['post']], [O])
                k.op('pool', lambda e: e.tensor_tensor(out=X[0:n, :], in0=X[0:n, :], in1=O[0:n, :], op=ALU.add), [X, O], [X])
            if x_dst:
                tl, ap = row_ap(x_dst, t0, n)
                k.dma('sp', ap, X[0:n, :], [X], [tl], X)
            if hT_dst:
                XN = xn[s]
                k.op('act', lambda e: e.activation(out=junk[0:n, :], in_=X[0:n, :], func=AF.Square, accum_out=S[0:n, 4:5]), [X], [junk, S])
                k.op('dve', lambda e: e.tensor_scalar(out=S[0:n, 5:6], in0=S[0:n, 4:5], scalar1=1.0 / D, scalar2=eps, op0=ALU.mult, op1=ALU.add), [S], [S])
                k.op('act', lambda e: e.activation(out=S[0:n, 6:7], in_=S[0:n, 5:6], func=AF.Sqrt), [S], [S])
                k.op('dve', lambda e: e.reciprocal(out=S[0:n, 7:8], in_=S[0:n, 6:7]), [S], [S])
                k.op('dve', lambda e: e.scalar_tensor_tensor(out=XN[0:n, :], in0=X[0:n, :], scalar=S[0:n, 7:8], in1=wB['pre'][0:n, :], op0=ALU.mult, op1=ALU.mult), [X, S, wB['pre']], [XN])
                H = hst[gi]
                tt = (t0 - g0)
                for half in range(KC // 16 if KC >= 16 else 1):
                    nk = min(16, KC)
                    P = pst[pcount % 2]; pcount += 1
                    for j in range(nk):
                        kc = half * 16 + j
                        k.op('pe', lambda e: e.transpose(out=P[:, j * 128:j * 128 + n], in_=XN[0:n, kc * 128:(kc + 1) * 128], identity=c.idb[0:n, 0:n]), [XN, c.idb], [P])
                    eng = 'act' if (pcount % 2) else 'dve'
                    src = P[:, 0:nk * 128].rearrange("p (k t) -> p k t", t=128)[:, :, 0:n]
                    dst = H[:, half * 16:half * 16 + nk, tt:tt + n]
                    if eng == 'act':
                        k.op('act', lambda e: e.copy(out=dst, in_=src), [P], [H])
                    else:
                        k.op('dve', lambda e: e.tensor_copy(out=dst, in_=src), [P], [H])
        if hT_dst:
            H = hst[gi]
            k.dma('sp', hT_dst.h.ap().rearrange("k p t -> p k t")[:, :, g0:g0 + gn], H[:, :, 0:gn], [H], [hT_dst], H)
    k.phase_end()

def gemm(k, c, name, XT, NTOK, KC, wsegs_blocks, wtile, orient, epilogue, TG=512, xbufs=2, wbufs=2, WMAX=512):
    Wt = [k.sb(f'{name}_w{i}', [128, KC, WMAX], BF16) for i in range(wbufs)]
    Xt = [k.sb(f'{name}_x{i}', [128, KC, TG], BF16) for i in range(xbufs)]
    NPS = 8
    PS = [k.ps(f'{name}_ps{i}', [128, 512], F32) for i in range(NPS)]
    psi = 0
    xi = 0
    groups = [(g0, min(TG, NTOK - g0)) for g0 in range(0, NTOK, TG)]
    single_x = (len(groups) <= xbufs)
    xloaded = {}
    if getattr(c, 'limit_blocks', None):
        wsegs_blocks = wsegs_blocks[:c.limit_blocks]
    for bi, segs in enumerate(wsegs_blocks):
        W = Wt[bi % wbufs]
        off = 0
        first = True
        for (wtl, wap) in segs:
            wd = wap.shape[1]
            wv = wap.rearrange("(k p) c -> p k c", p=128)
            for k0 in range(0, KC, 16):
                k1 = min(KC, k0 + 16)
                k.dma('pool', W[:, k0:k1, off:off + wd], wv[:, k0:k1, :], [wtl], [W], W)
            off += wd
        width = off
        for (g0, gn) in groups:
            if single_x and g0 in xloaded:
                X = xloaded[g0]
            else:
                X = Xt[xi % xbufs]; xi += 1
                xv = XT.h.ap().rearrange("k p t -> p k t")
                for k0 in range(0, KC, 16):
                    k1 = min(KC, k0 + 16)
                    k.dma('sp', X[:, k0:k1, 0:gn], xv[:, k0:k1, g0:g0 + gn], [XT], [X], X)
                if single_x: xloaded[g0] = X
            outs = []
            if orient == 'f':
                for c0 in range(0, width, 128):
                    m = min(128, width - c0)
                    P = PS[psi % NPS]; psi += 1
                    for kc in range(KC):
                        k.op('pe', lambda e: e.matmul(P[0:m, 0:gn], lhsT=W[:, kc, c0:c0 + m], rhs=X[:, kc, 0:gn], start=(kc == 0), stop=(kc == KC - 1)), [W, X], [P])
                    outs.append((P, m, gn, c0))
            else:
                for t0 in range(0, gn, 128):
                    m = min(128, gn - t0)
                    P = PS[psi % NPS]; psi += 1
                    for kc in range(KC):
                        k.op('pe', lambda e: e.matmul(P[0:m, 0:width], lhsT=X[:, kc, t0:t0 + m], rhs=W[:, kc, 0:width], start=(kc == 0), stop=(kc == KC - 1)), [W, X], [P])
                    outs.append((P, m, width, g0 + t0))
            epilogue(bi, g0, gn, outs)
D_MODEL = 4096; SEQ = 2048; DEPTH = 2; NB_S = 2; DEC_SEQ = 8
NTOK = SEQ + NB_S * DEC_SEQ
GROUP_W = 1024
N_IN = 13120; D_FF = 11008
OFF_GDN_QKV = 0; OFF_GDN_Z = 3072; OFF_GDN_B = 4096; OFF_GDN_A = 4104; OFF_RWKV = 4112
OFF_SSM_Z = 7472; OFF_SSM_XBC = 8496; OFF_SSM_DT = 10032; OFF_SWA = 10048
RWKV_PROJ = 3360; SSM_XBC = 1536

def rows_to_fm(k, c, dst, rows, NF, name):
    R = len(rows)
    Rt = k.sb(f'{name}_rt', [R, NF * 128], F32)
    for r, (tl, ap) in enumerate(rows):
        k.dma('sp', Rt[r:r + 1, :], ap.unsqueeze(0), [tl], [Rt], Rt)
    per = 512 // R
    for f0 in range(0, NF, per):
        nf = min(per, NF - f0)
        P = k.ps(f'{name}_p{f0}', [128, 512], F32)
        for f in range(nf):
            k.op('pe', lambda e: e.transpose(out=P[:, f * R:(f + 1) * R], in_=Rt[0:R, (f0 + f) * 128:(f0 + f + 1) * 128], identity=c.idf[0:R, 0:R]), [Rt, c.idf], [P])
        k.op('dve', lambda e: e.tensor_copy(out=dst[:, f0:f0 + nf, :], in_=P[:, 0:nf * R].rearrange("p (f r) -> p f r", r=R)), [P], [dst])

def fm_to_rows(k, c, src, R, NF, outs, name):
    Rt = k.sb(f'{name}_rt', [R, NF * 128], F32)
    PP = [k.ps(f'{name}_p{i}', [128, 512], F32) for i in range(4)]
    for f0 in range(0, NF, 4):
        nf = min(4, NF - f0)
        P = PP[(f0 // 4) % 4]
        for f in range(nf):
            k.op('pe', lambda e: e.transpose(out=P[0:R, f * 128:(f + 1) * 128], in_=src[:, f0 + f, :], identity=c.idf[:, :]), [src, c.idf], [P])
        k.op('dve', lambda e: e.tensor_copy(out=Rt[0:R, f0 * 128:(f0 + nf) * 128], in_=P[0:R, 0:nf * 128]), [P], [Rt])
    for r, (tl, ap) in enumerate(outs):
        k.dma('sp', ap.unsqueeze(0), Rt[r:r + 1, :], [Rt], [tl], Rt)

class StopBuild(Exception):
    pass

def build_program(nc, es, mixers=True, limit_blocks=None, depth=DEPTH, stop=None):
    k = KB(nc, es); c = Ctx(); c.limit_blocks = limit_blocks
    try:
        _build_body(k, c, mixers, depth, stop)
    except StopBuild:
        pass
    k.barrier()
    k.c = c
    return k

def _build_body(k, c, mixers, depth, stop):
    f = F32
    IN = {}
    def inp(name, shape):
        IN[name] = k.dram(name, shape, f, kind="ExternalInput"); return IN[name]
    OUT = {}
    def outp(name, shape):
        OUT[name] = k.dram(name, shape, f, kind="ExternalOutput"); return OUT[name]
    L = DEPTH
    xin = inp('xin', [NTOK, D_MODEL])
    inp('st_gdn', [L, NB_S, 8, 128, 128]); inp('st_gdn_conv', [L, NB_S, 3, 3072]); inp('st_rwkv', [L, NB_S, 16, 64, 64])
    inp('st_rwkv_shift', [L, NB_S, 3360]); inp('st_ssm', [L, NB_S, 16, 64, 128]); inp('st_ssm_conv', [L, NB_S, 3, 1536])
    inp('c_swa_k', [L, NB_S, 2048, 1024]); inp('c_swa_v', [L, NB_S, 2048, 1024]); inp('st_ffn_conv', [L, NB_S, 2, D_FF])
    for nm in ['norm_mix_pre', 'norm_mix_post', 'norm_ffn_pre', 'norm_ffn_post']:
        inp(nm, [L, D_MODEL])
    inp('w_in', [L, D_MODEL, N_IN]); inp('w_out', [L, D_MODEL, D_MODEL])
    inp('gdn_conv_w', [L, 4, 3072]); inp('gdn_A_log', [L, 8]); inp('gdn_dt_bias', [L, 8]); inp('gdn_norm_w', [L, 128])
    inp('rwkv_mu', [L, 3360]); inp('rwkv_w0', [L, 1024]); inp('rwkv_w2', [L, 64, 1024]); inp('rwkv_a0', [L, 1024])
    inp('rwkv_a2', [L, 64, 1024]); inp('rwkv_g2', [L, 160, 1024]); inp('rwkv_k_k', [L, 1024]); inp('rwkv_k_a', [L, 1024])
    inp('rwkv_r_k', [L, 1024]); inp('rwkv_ln_w', [L, 1024]); inp('rwkv_ln_b', [L, 1024])
    inp('ssm_conv_w', [L, 4, 1536]); inp('ssm_conv_b', [L, 1536]); inp('ssm_dt_bias', [L, 16]); inp('ssm_A_log', [L, 16])
    inp('ssm_D', [L, 16]); inp('ssm_norm_w', [L, 1024])
    inp('ffn_w_up', [L, D_MODEL, 2 * D_FF]); inp('ffn_conv_w', [L, 3, D_FF]); inp('ffn_conv_b', [L, D_FF]); inp('ffn_w_down', [L, D_FF, D_MODEL])
    y = outp('y', [NTOK, D_MODEL])
    outp('p_gdn', [L, 8, 128, 128]); outp('p_gdn_conv', [L, 3, 3072]); outp('p_rwkv', [L, 16, 64, 64]); outp('p_rwkv_shift', [L, 3360])
    outp('p_ssm', [L, 16, 64, 128]); outp('p_ssm_conv', [L, 3, 1536]); outp('p_swa_k', [L, 2048, 1024]); outp('p_swa_v', [L, 2048, 1024])
    outp('p_ffn_conv', [L, 2, D_FF])
    outp('s_gdn', [L, NB_S, 8, 128, 128]); outp('s_gdn_conv', [L, NB_S, 3, 3072]); outp('s_rwkv', [L, NB_S, 16, 64, 64]); outp('s_rwkv_shift', [L, NB_S, 3360])
    outp('s_ssm', [L, NB_S, 16, 64, 128]); outp('s_ssm_conv', [L, NB_S, 3, 1536]); outp('s_swa_k', [L, NB_S, 8, 1024]); outp('s_swa_v', [L, NB_S, 8, 1024])
    outp('s_ffn_conv', [L, NB_S, 2, D_FF])
    c.IN = IN; c.OUT = OUT
    c.idf, c.idb = make_ident(k)
    if mixers:
        make_consts(k, c)
        make_consts2(k, c)
    KC = D_MODEL // 128; KF = D_FF // 128
    xcur = xin
    for li in range(depth):
        hT = k.dram(f'hT{li}', [KC, 128, NTOK], BF16) if li == 0 else None
        projT = k.dram(f'projT{li}', [N_IN, NTOK], F32)
        mixT = k.dram(f'mixT{li}', [KC, 128, NTOK], BF16)
        o1 = k.dram(f'o1_{li}', [NTOK, D_MODEL], F32)
        x1 = k.dram(f'x1_{li}', [NTOK, D_MODEL], F32)
        h2T = k.dram(f'h2T{li}', [KC, 128, NTOK], BF16)
        actT = k.dram(f'actT{li}', [KF, 128, NTOK], BF16)
        o2 = k.dram(f'o2_{li}', [NTOK, D_MODEL], F32)
        if li == 0:
            norm_phase(k, c, f'n1_{li}', NTOK, D_MODEL, [(xcur, xcur.h.ap(), NTOK)], None, None,
                       (IN['norm_mix_pre'], IN['norm_mix_pre'].h.ap()[li]), None, hT)
        else:
            hT = c.next_hT
        k.phase_begin()
        stg = [k.sb(f'g1stg{i}', [128, 512], F32) for i in range(4)]
        cnt = [0]
        w_in = IN['w_in']
        blocks = []; bstart = []
        for c0 in range(0, N_IN, 512):
            wd = min(512, N_IN - c0)
            blocks.append([(w_in, w_in.h.ap()[li, :, c0:c0 + wd])]); bstart.append(c0)
        def epi1(bi, g0, gn, outs):
            for (P, m, n, c0) in outs:
                S = stg[cnt[0] % 4]; cnt[0] += 1
                if cnt[0] % 2:
                    k.op('act', lambda e: e.copy(out=S[0:m, 0:n], in_=P[0:m, 0:n]), [P], [S])
                else:
                    k.op('dve', lambda e: e.tensor_copy(out=S[0:m, 0:n], in_=P[0:m, 0:n]), [P], [S])
                col = bstart[bi] + c0
                k.dma('sp', projT.h.ap()[col:col + m, g0:g0 + n], S[0:m, 0:n], [S], [projT], S)
        gemm(k, c, f'g1_{li}', hT, NTOK, KC, blocks, None, 'f', epi1)
        k.phase_end()
        if stop == 'g1': raise StopBuild()
        if mixers:
            run_mixers(k, c, li, projT, mixT)
        else:
            k.phase_begin()
            Z = k.sb('zeros', [128, KC, 512], BF16)
            k.op('pool', lambda e: e.memset(Z[:], 0.0), [], [Z])
            for g0 in range(0, NTOK, 512):
                gn = min(512, NTOK - g0)
                k.dma('sp', mixT.h.ap().rearrange("k p t -> p k t")[:, :, g0:g0 + gn], Z[:, :, 0:gn], [Z], [mixT], Z)
            k.phase_end()
        k.phase_begin()
        stg = [k.sb(f'g2stg{i}', [128, 512], F32) for i in range(4)]
        w_out = IN['w_out']
        blocks = [[(w_out, w_out.h.ap()[li, :, c0:c0 + 512])] for c0 in range(0, D_MODEL, 512)]
        def epi2(bi, g0, gn, outs):
            for (P, m, n, t0) in outs:
                S = stg[cnt[0] % 4]; cnt[0] += 1
                if cnt[0] % 2:
                    k.op('act', lambda e: e.copy(out=S[0:m, 0:n], in_=P[0:m, 0:n]), [P], [S])
                else:
                    k.op('dve', lambda e: e.tensor_copy(out=S[0:m, 0:n], in_=P[0:m, 0:n]), [P], [S])
                k.dma('sp', o1.h.ap()[t0:t0 + m, bi * 512:bi * 512 + n], S[0:m, 0:n], [S], [o1], S)
        gemm(k, c, f'g2_{li}', mixT, NTOK, KC, blocks, None, 't', epi2)
        k.phase_end()
        if stop == 'g2': raise StopBuild()
        norm_phase(k, c, f'n2_{li}', NTOK, D_MODEL, [(xcur, xcur.h.ap(), NTOK)], [(o1, o1.h.ap(), NTOK)],
                   (IN['norm_mix_post'], IN['norm_mix_post'].h.ap()[li]), (IN['norm_ffn_pre'], IN['norm_ffn_pre'].h.ap()[li]),
                   [(x1, x1.h.ap(), NTOK)], h2T)
        k.scope_begin()
        cw = k.sb(f'cw{li}', [128, KF, 4], F32, persist=True)
        fst = k.sb(f'fst{li}', [128, KF, 4], F32, persist=True)
        fco = k.sb(f'fco{li}', [128, KF, 6], F32, persist=True)
        k.phase_begin()
        fw = IN['ffn_conv_w']; fb = IN['ffn_conv_b']; fs = IN['st_ffn_conv']
        rows_to_fm(k, c, cw, [(fw, fw.h.ap()[li, 0]), (fw, fw.h.ap()[li, 1]), (fw, fw.h.ap()[li, 2]), (fb, fb.h.ap()[li])], KF, f'cwl{li}')
        k.phase_end()
        k.phase_begin()
        rows_to_fm(k, c, fst, [(fs, fs.h.ap()[li, 0, 0]), (fs, fs.h.ap()[li, 0, 1]), (fs, fs.h.ap()[li, 1, 0]), (fs, fs.h.ap()[li, 1, 1])], KF, f'fstl{li}')
        k.phase_end()
        k.phase_begin()
        Gt = [k.sb(f'g3G{j}', [128, 516], F32) for j in range(2)]
        At = [k.sb(f'g3A{j}', [128, 512], F32) for j in range(2)]
        ATs = [k.sb(f'g3AT{i}', [128, 2, 512], BF16) for i in range(2)]
        wu = IN['ffn_w_up']
        blocks = [[(wu, wu.h.ap()[li, :, j * 256:(j + 1) * 256]), (wu, wu.h.ap()[li, :, D_FF + j * 256:D_FF + (j + 1) * 256])] for j in range(KF // 2)]
        acnt = [0]
        def epi3(bi, g0, gn, outs):
            AT = ATs[acnt[0] % 2]; acnt[0] += 1
            for j in range(2):
                ft = bi * 2 + j
                Pg = outs[j][0]; Pv = outs[2 + j][0]
                G = Gt[j]; A = At[j]
                w0 = cw[:, ft, 0:1]; w1 = cw[:, ft, 1:2]; w2 = cw[:, ft, 2:3]; bb = cw[:, ft, 3:4]
                if gn == 512:
                    if g0 == 0:
                        k.op('pool', lambda e: e.memset(G[:, 0:2], 0.0), [], [G])
                    else:
                        k.op('pool', lambda e: e.tensor_copy(out=G[:, 0:2], in_=G[:, 512:514]), [G], [G])
                    k.op('act', lambda e: e.copy(out=G[:, 2:2 + gn], in_=Pg[:, 0:gn]), [Pg], [G])
                    nn = gn
                    if g0 + gn == SEQ:
                        k.op('pool', lambda e: e.tensor_copy(out=fco[:, ft, 0:2], in_=G[:, 512:514]), [G], [fco])
                else:
                    k.op('pool', lambda e: e.tensor_copy(out=G[:, 0:2], in_=fst[:, ft, 0:2]), [fst], [G])
                    k.op('pool', lambda e: e.tensor_copy(out=G[:, 10:12], in_=fst[:, ft, 2:4]), [fst], [G])
                    k.op('act', lambda e: e.copy(out=G[:, 2:10], in_=Pg[:, 0:8]), [Pg], [G])
                    k.op('act', lambda e: e.copy(out=G[:, 12:20], in_=Pg[:, 8:16]), [Pg], [G])
                    k.op('pool', lambda e: e.tensor_copy(out=fco[:, ft, 2:4], in_=G[:, 8:10]), [G], [fco])
                    k.op('pool', lambda e: e.tensor_copy(out=fco[:, ft, 4:6], in_=G[:, 18:20]), [G], [fco])
                    nn = 18
                k.op('dve', lambda e: e.tensor_scalar(out=A[:, 0:nn], in0=G[:, 2:2 + nn], scalar1=w2, scalar2=bb, op0=ALU.mult, op1=ALU.add), [G, cw], [A])
                k.op('dve', lambda e: e.scalar_tensor_tensor(out=A[:, 0:nn], in0=G[:, 1:1 + nn], scalar=w1, in1=A[:, 0:nn], op0=ALU.mult, op1=ALU.add), [G, cw, A], [A])
                k.op('dve', lambda e: e.scalar_tensor_tensor(out=A[:, 0:nn], in0=G[:, 0:nn], scalar=w0, in1=A[:, 0:nn], op0=ALU.mult, op1=ALU.add), [G, cw, A], [A])
                k.op('act', lambda e: e.activation(out=A[:, 0:nn], in_=A[:, 0:nn], func=AF.Silu), [A], [A])
                if gn == 512:
                    k.op('dve', lambda e: e.tensor_tensor(out=AT[:, j, 0:gn], in0=A[:, 0:gn], in1=Pv[:, 0:gn], op=ALU.mult), [A, Pv], [AT])
                else:
                    k.op('dve', lambda e: e.tensor_tensor(out=AT[:, j, 0:8], in0=A[:, 0:8], in1=Pv[:, 0:8], op=ALU.mult), [A, Pv], [AT])
                    k.op('dve', lambda e: e.tensor_tensor(out=AT[:, j, 8:16], in0=A[:, 10:18], in1=Pv[:, 8:16], op=ALU.mult), [A, Pv], [AT])
            k.dma('sp', actT.h.ap().rearrange("k p t -> p k t")[:, bi * 2:bi * 2 + 2, g0:g0 + gn], AT[:, :, 0:gn], [AT], [actT], AT)
        gemm(k, c, f'g3_{li}', h2T, NTOK, KC, blocks, None, 'f', epi3)
        k.phase_end()
        k.phase_begin()
        po = OUT['p_ffn_conv']; so = OUT['s_ffn_conv']
        fm_to_rows(k, c, fco, 6, KF, [(po, po.h.ap()[li, 0]), (po, po.h.ap()[li, 1]), (so, so.h.ap()[li, 0, 0]), (so, so.h.ap()[li, 0, 1]),
                                      (so, so.h.ap()[li, 1, 0]), (so, so.h.ap()[li, 1, 1])], f'fco{li}')
        k.phase_end()
        k.scope_end()
        k.phase_begin()
        stg = [k.sb(f'g4stg{i}', [128, 256], F32) for i in range(4)]
        wd_ = IN['ffn_w_down']
        blocks = [[(wd_, wd_.h.ap()[li, :, c0:c0 + 256])] for c0 in range(0, D_MODEL, 256)]
        def epi4(bi, g0, gn, outs):
            for (P, m, n, t0) in outs:
                S = stg[cnt[0] % 4]; cnt[0] += 1
                if cnt[0] % 2:
                    k.op('act', lambda e: e.copy(out=S[0:m, 0:n], in_=P[0:m, 0:n]), [P], [S])
                else:
                    k.op('dve', lambda e: e.tensor_copy(out=S[0:m, 0:n], in_=P[0:m, 0:n]), [P], [S])
                k.dma('sp', o2.h.ap()[t0:t0 + m, bi * 256:bi * 256 + n], S[0:m, 0:n], [S], [o2], S)
        gemm(k, c, f'g4_{li}', actT, NTOK, KF, blocks, None, 't', epi4, TG=128, WMAX=256)
        k.phase_end()
        if li + 1 < depth:
            xn_ = k.dram(f'x2_{li}', [NTOK, D_MODEL], F32)
            hTn = k.dram(f'hT{li + 1}', [KC, 128, NTOK], BF16)
            norm_phase(k, c, f'n3_{li}', NTOK, D_MODEL, [(x1, x1.h.ap(), NTOK)], [(o2, o2.h.ap(), NTOK)],
                       (IN['norm_ffn_post'], IN['norm_ffn_post'].h.ap()[li]), (IN['norm_mix_pre'], IN['norm_mix_pre'].h.ap()[li + 1]),
                       [(xn_, xn_.h.ap(), NTOK)], hTn)
            c.next_hT = hTn
            xcur = xn_
        else:
            norm_phase(k, c, f'n3_{li}', NTOK, D_MODEL, [(x1, x1.h.ap(), NTOK)], [(o2, o2.h.ap(), NTOK)],
                       (IN['norm_ffn_post'], IN['norm_ffn_post'].h.ap()[li]), None, [(y, y.h.ap(), NTOK)], None)

    IN, OUT = c.IN, c.OUT
    def I(nm): return (IN[nm], IN[nm].h.ap()[li])
    def O(nm): return (OUT[nm], OUT[nm].h.ap()[li])
    k.scope_begin()
    gdn_phase(k, c, li, projT, mixT, SEQ, NB_S, IN, O('p_gdn'), O('p_gdn_conv'), O('s_gdn'), O('s_gdn_conv'), I('st_gdn'), I('st_gdn_conv'),
              OFF_GDN_QKV, OFF_GDN_Z, OFF_GDN_B, kc0=0)
    k.scope_end()
    k.scope_begin()
    rwkv_phase(k, c, li, projT, mixT, SEQ, NB_S, IN, O('p_rwkv'), O('p_rwkv_shift'), O('s_rwkv'), O('s_rwkv_shift'), I('st_rwkv'), I('st_rwkv_shift'),
               OFF_RWKV, kc0=8)
    k.scope_end()
    k.scope_begin()
    ssd_phase(k, c, li, projT, mixT, SEQ, NB_S, IN, O('p_ssm'), O('p_ssm_conv'), O('s_ssm'), O('s_ssm_conv'), I('st_ssm'), I('st_ssm_conv'),
              OFF_SSM_Z, OFF_SSM_XBC, OFF_SSM_DT, kc0=16)
    k.scope_end()
    k.scope_begin()
    swa_phase(k, c, li, projT, mixT, SEQ, NB_S, OFF_SWA, O('p_swa_k'), O('p_swa_v'), O('s_swa_k'), O('s_swa_v'), I('c_swa_k'), I('c_swa_v'), kc0=24)
    k.scope_end()

def make_consts(k, c):
    NG = 128 * 20 + 512
    G = k.sb('swa_G', [128, NG], BF16, persist=True)
    R = k.sb('swa_R', [128, 512], F32, persist=True)
    k.phase_begin()
    d = k.sb('swa_d', [128, NG], F32)
    t1 = k.sb('swa_t1', [128, NG], F32)
    t2 = k.sb('swa_t2', [128, NG], F32)
    acc = k.sb('swa_acc', [128, NG], F32)
    di = k.sb('swa_di', [128, NG], I32)
    di2 = k.sb('swa_di2', [128, NG], I32)
    k.op('pool', lambda e: e.iota(di[:], pattern=[[1, NG]], base=-384, channel_multiplier=-1), [], [di])
    k.op('dve', lambda e: e.tensor_copy(out=d[:], in_=di[:]), [di], [d])
    first = True
    for (win, dil) in ((128, 1), (512, 4), (2048, 16)):
        k.op('dve', lambda e: e.tensor_scalar(out=t1[:], in0=d[:], scalar1=0.0, scalar2=None, op0=ALU.is_ge), [d], [t1])
        k.op('dve', lambda e: e.tensor_scalar(out=t2[:], in0=d[:], scalar1=float(win), scalar2=None, op0=ALU.is_le), [d], [t2])
        k.op('dve', lambda e: e.tensor_tensor(out=t1[:], in0=t1[:], in1=t2[:], op=ALU.mult), [t1, t2], [t1])
        if dil > 1:
            k.op('dve', lambda e: e.tensor_single_scalar(out=di2[:], in_=di[:], scalar=dil - 1, op=ALU.bitwise_and), [di], [di2])
            k.op('dve', lambda e: e.tensor_scalar(out=t2[:], in0=di2[:], scalar1=0.0, scalar2=None, op0=ALU.is_equal), [di2], [t2])
            k.op('dve', lambda e: e.tensor_tensor(out=t1[:], in0=t1[:], in1=t2[:], op=ALU.mult), [t1, t2], [t1])
        if first:
            k.op('dve', lambda e: e.tensor_copy(out=acc[:], in_=t1[:]), [t1], [acc]); first = False
        else:
            k.op('dve', lambda e: e.tensor_tensor(out=acc[:], in0=acc[:], in1=t1[:], op=ALU.add), [acc, t1], [acc])
    k.op('dve', lambda e: e.tensor_copy(out=G[:], in_=acc[:]), [acc], [G])
    ri = k.sb('swa_ri', [128, 512], I32)
    k.op('pool', lambda e: e.iota(ri[:], pattern=[[-1, 512]], base=0, channel_multiplier=1), [], [ri])
    k.op('dve', lambda e: e.tensor_copy(out=R[:], in_=ri[:]), [ri], [R])
    k.phase_end()
    c.swa_G = G; c.swa_R = R

def swa_phase(k, c, li, projT, mixT, T, NS, off_q, pk, pv, sk, sv, ck, cv, kc0=24):
    G = c.swa_G; R = c.swa_R
    nt = T // 128
    QS = 128 ** -0.5
    k.phase_begin()
    NTOKL = T + NS * 8
    qf = k.sb('swa_qf', [128, NTOKL], F32)
    kf = k.sb('swa_kf', [128, NTOKL], F32)
    vf = k.sb('swa_vf', [128, NTOKL], F32)
    qb = k.sb('swa_qb', [128, NTOKL], BF16)
    kb = k.sb('swa_kb', [128, NTOKL + 2048], BF16)
    va = k.sb('swa_va', [128, nt + 17, 130], BF16)
    tok = [k.sb(f'swa_tok{i}', [128, 128], F32) for i in range(4)]
    tmp = [k.sb(f'swa_tmp{i}', [128, 512], F32) for i in range(2)]
    pe_ = [k.sb(f'swa_pe{i}', [128, 512], BF16) for i in range(2)]
    pm = [k.sb(f'swa_pm{i}', [128, 512], BF16) for i in range(2)]
    osb = [k.sb(f'swa_o{i}', [128, 132], F32) for i in range(2)]
    ost = [k.sb(f'swa_ost{i}', [128, 512], BF16) for i in range(2)]
    cst = k.sb('swa_cst', [128, 16, 128], F32)
    PS_S = [k.ps(f'swa_pss{i}', [128, 512], F32) for i in range(2)]
    PS_O = [k.ps(f'swa_pso{i}', [128, 512], F32) for i in range(4)]
    PS_T = [k.ps(f'swa_pst{i}', [128, 512], F32) for i in range(2)]
    cnt = {'s': 0, 't': 0, 'tok': 0, 'o': 0, 'ost': 0}
    k.op('pool', lambda e: e.memset(va[:], 1.0), [], [va])
    def tr_to_tok(src_ap, n):
        P = PS_T[cnt['t'] % 2]; cnt['t'] += 1
        k.op('pe', lambda e: e.transpose(out=P[0:n, 0:128], in_=src_ap, identity=c.idf[:, :]), [qf, kf, vf, c.idf], [P])
        S = tok[cnt['tok'] % 4]; cnt['tok'] += 1
        k.op('act', lambda e: e.copy(out=S[0:n, :], in_=P[0:n, 0:128]), [P], [S])
        return S
    def attend(h, q_cols, nq, q0pos, key_tiles, out_cb):
        slope = 2.0 ** (-8.0 * (h + 1) / 8)
        nqt = (nq + 127) // 128
        POs = [PS_O[i] for i in range(nqt)]
        started = [False] * nqt
        for ki, (kc, nk, k0pos, vi) in enumerate(key_tiles):
            PSs = PS_S[cnt['s'] % 2]; TM = tmp[cnt['s'] % 2]; PE_ = pe_[cnt['s'] % 2]; PM = pm[cnt['s'] % 2]; cnt['s'] += 1
            k.op('pe', lambda e: e.matmul(PSs[0:nk, 0:nq], lhsT=kb[:, kc:kc + nk], rhs=qb[:, q_cols:q_cols + nq], start=True, stop=True), [kb, qb], [PSs])
            k.op('dve', lambda e: e.scalar_tensor_tensor(out=TM[0:nk, 0:nq], in0=R[0:nk, 0:nq], scalar=slope, in1=PSs[0:nk, 0:nq], op0=ALU.mult, op1=ALU.add), [R, PSs], [TM])
            k.op('act', lambda e: e.activation(out=PE_[0:nk, 0:nq], in_=TM[0:nk, 0:nq], func=AF.Exp, bias=float(-slope * (q0pos - k0pos)), scale=1.0), [TM], [PE_])
            dl = (q0pos - k0pos) // 128
            assert (q0pos - k0pos) % 128 == 0 and -3 <= dl <= 16
            g0 = 128 * (dl + 3)
            k.op('pool', lambda e: e.tensor_tensor(out=PM[0:nk, 0:nq], in0=PE_[0:nk, 0:nq], in1=G[0:nk, g0:g0 + nq], op=ALU.mult), [PE_, G], [PM])
            for qt in range(nqt):
                n = min(128, nq - qt * 128)
                if k0pos > q0pos + qt * 128 + n - 1:
                    continue
                last = True
                for (kc2, nk2, k0pos2, vi2) in key_tiles[ki + 1:]:
                    if k0pos2 <= q0pos + qt * 128 + n - 1:
                        last = False; break
                k.op('pe', lambda e: e.matmul(POs[qt][0:n, 0:129], lhsT=PM[0:nk, qt * 128:qt * 128 + n], rhs=va[0:nk, vi, 0:129], start=(not started[qt]), stop=last), [PM, va], [POs[qt]])
                started[qt] = True
        for qt in range(nqt):
            n = min(128, nq - qt * 128)
            O = osb[cnt['o'] % 2]; cnt['o'] += 1
            k.op('dve', lambda e: e.reciprocal(out=O[0:n, 130:131], in_=POs[qt][0:n, 128:129]), [POs[qt]], [O])
            k.op('dve', lambda e: e.tensor_scalar(out=O[0:n, 0:128], in0=POs[qt][0:n, 0:128], scalar1=O[0:n, 130:131], scalar2=None, op0=ALU.mult), [POs[qt], O], [O])
            out_cb(qt, n, O)
    for h in range(8):
        for (dst, off) in ((qf, off_q + h * 128), (kf, off_q + 1024 + h * 128), (vf, off_q + 2048 + h * 128)):
            k.dma('sp', dst[:], projT.h.ap()[off:off + 128, :], [projT], [dst], dst)
        k.op('act', lambda e: e.activation(out=qb[:], in_=qf[:], func=AF.Copy, scale=QS), [qf], [qb])
        k.op('dve', lambda e: e.tensor_copy(out=kb[:, 0:NTOKL], in_=kf[:]), [kf], [kb])
        for t in range(nt):
            S = tr_to_tok(kf[:, t * 128:(t + 1) * 128], 128)
            k.dma('sp', pk[1][t * 128:(t + 1) * 128, h * 128:(h + 1) * 128], S[:, :], [S], [pk[0]], S)
            S = tr_to_tok(vf[:, t * 128:(t + 1) * 128], 128)
            k.dma('sp', pv[1][t * 128:(t + 1) * 128, h * 128:(h + 1) * 128], S[:, :], [S], [pv[0]], S)
            k.op('pool', lambda e: e.tensor_copy(out=va[:, t, 0:128], in_=S[:, :]), [S], [va])
        for qblk in range(0, T, 512):
            nq = min(512, T - qblk)
            kts = [(kt * 128, 128, kt * 128, kt) for kt in range((qblk + nq) // 128)]
            OST = ost[cnt['ost'] % 2]; cnt['ost'] += 1
            def ocb(qt, n, O, OST=OST):
                P = PS_T[cnt['t'] % 2]; cnt['t'] += 1
                k.op('pe', lambda e: e.transpose(out=P[:, 0:n], in_=O[0:n, 0:128], identity=c.idf[0:n, 0:n]), [O, c.idf], [P])
                k.op('act', lambda e: e.copy(out=OST[:, qt * 128:qt * 128 + n], in_=P[:, 0:n]), [P], [OST])
            attend(h, qblk, nq, qblk, kts, ocb)
            k.dma('sp', mixT.h.ap()[kc0 + h, :, qblk:qblk + nq], OST[:, 0:nq], [OST], [mixT], OST)
        for b in range(NS):
            col = T + b * 8
            S = tr_to_tok(kf[:, col:col + 8], 8)
            k.dma('sp', sk[1][b, :, h * 128:(h + 1) * 128], S[0:8, :], [S], [sk[0]], S)
            S = tr_to_tok(vf[:, col:col + 8], 8)
            k.dma('sp', sv[1][b, :, h * 128:(h + 1) * 128], S[0:8, :], [S], [sv[0]], S)
            k.op('pool', lambda e: e.tensor_copy(out=va[0:8, nt + 16, 0:128], in_=S[0:8, :]), [S], [va])
            k.dma('sp', cst[:], ck[1][b, :, h * 128:(h + 1) * 128].rearrange("(t p) e -> p t e", p=128), [ck[0]], [cst], cst)
            for t in range(16):
                P = PS_T[cnt['t'] % 2]; cnt['t'] += 1
                k.op('pe', lambda e: e.transpose(out=P[:, 0:128], in_=cst[:, t, :], identity=c.idf[:, :]), [cst, c.idf], [P])
                k.op('act', lambda e: e.copy(out=kb[:, NTOKL + t * 128:NTOKL + (t + 1) * 128], in_=P[:, 0:128]), [P], [kb])
            k.dma('sp', cst[:], cv[1][b, :, h * 128:(h + 1) * 128].rearrange("(t p) e -> p t e", p=128), [cv[0]], [cst], cst)
            k.op('dve', lambda e: e.tensor_copy(out=va[:, nt:nt + 16, 0:128], in_=cst[:]), [cst], [va])
            kts = [(NTOKL + t * 128, 128, t * 128, nt + t) for t in range(16)] + [(col, 8, 2048, nt + 16)]
            OST = ost[cnt['ost'] % 2]; cnt['ost'] += 1
            def ocb2(qt, n, O, OST=OST):
                P = PS_T[cnt['t'] % 2]; cnt['t'] += 1
                k.op('pe', lambda e: e.transpose(out=P[:, 0:n], in_=O[0:n, 0:128], identity=c.idf[0:n, 0:n]), [O, c.idf], [P])
                k.op('act', lambda e: e.copy(out=OST[:, 0:n], in_=P[:, 0:n]), [P], [OST])
            attend(h, col, 8, 2048, kts, ocb2)
            k.dma('sp', mixT.h.ap()[kc0 + h, :, col:col + 8], OST[:, 0:8], [OST], [mixT], OST)
    k.phase_end()

def make_consts2(k, c):
    U = k.sb('c_utri', [128, 128], F32, persist=True)
    MN = k.sb('c_mneg', [128, 128], F32, persist=True)
    ONES = k.sb('c_ones', [128, 128], F32, persist=True)
    SEL = {}
    k.op('pool', lambda e: e.memset(ONES[:], 1.0), [], [ONES])
    k.op('pool', lambda e: e.memset(U[:], 1.0), [], [U])
    k.op('pool', lambda e: e.affine_select(out=U[:], in_=U[:], pattern=[[1, 128]], compare_op=ALU.is_ge, fill=0.0, base=0, channel_multiplier=-1), [U], [U])
    k.op('pool', lambda e: e.memset(MN[:], 0.0), [], [MN])
    k.op('pool', lambda e: e.affine_select(out=MN[:], in_=MN[:], pattern=[[1, 128]], compare_op=ALU.is_ge, fill=-30000.0, base=0, channel_multiplier=-1), [MN], [MN])
    for n in (128, 8):
        S = k.sb(f'c_sel{n}', [128, 128], F32, persist=True)
        k.op('pool', lambda e: e.memset(S[:], 1.0), [], [S])
        k.op('pool', lambda e: e.affine_select(out=S[:], in_=S[:], pattern=[[0, 128]], compare_op=ALU.is_equal, fill=0.0, base=-(n - 1), channel_multiplier=1), [S], [S])
        SEL[n] = S
    eps6 = k.sb('c_eps6', [128, 1], F32, persist=True)
    k.op('pool', lambda e: e.memset(eps6[:], 1e-6), [], [eps6])
    c.eps6 = eps6
    epsgn = k.sb('c_epsgn', [128, 1], F32, persist=True)
    k.op('pool', lambda e: e.memset(epsgn[:], 64e-5), [], [epsgn])
    c.epsgn = epsgn
    c.U = U; c.MN = MN; c.ONES = ONES; c.SEL = SEL

def bc_row(k, name, tl, ap_row, ncols):
    t = k.sb(name, [128, ncols], F32)
    k.dma('sp', t[:], ap_row.partition_broadcast(128), [tl], [t], t)
    return t

def ssd_phase(k, c, li, projT, mixT, T, NS, IN, pz_out, pconv_out, sz_out, sconv_out, st_z, st_conv, off_z, off_xbc, off_dt, kc0=16):
    Z = k.sb('ssd_Z', [128, 1024], F32, persist=True)
    cwS = k.sb('ssd_cw', [128, 12, 5], F32, persist=True)
    scv = k.sb('ssd_scv', [128, 12, 3 * NS], F32, persist=True)
    cvo = k.sb('ssd_cvo', [128, 12, 3 * (NS + 1)], F32, persist=True)
    k.phase_begin()
    cw_t = IN['ssm_conv_w']; cb_t = IN['ssm_conv_b']
    rows_to_fm(k, c, cwS, [(cw_t, cw_t.h.ap()[li, i]) for i in range(4)] + [(cb_t, cb_t.h.ap()[li])], 12, 'ssdcw')
    k.phase_end()
    k.phase_begin()
    rows_to_fm(k, c, scv, [(st_conv[0], st_conv[1][b, r]) for b in range(NS) for r in range(3)], 12, 'ssdsc')
    k.phase_end()
    k.phase_begin()
    dtb = bc_row(k, 'ssd_dtb', IN['ssm_dt_bias'], IN['ssm_dt_bias'].h.ap()[li], 16)
    Aneg = bc_row(k, 'ssd_A', IN['ssm_A_log'], IN['ssm_A_log'].h.ap()[li], 16)
    k.op('act', lambda e: e.activation(out=Aneg[:], in_=Aneg[:], func=AF.Exp), [Aneg], [Aneg])
    k.op('dve', lambda e: e.tensor_scalar(out=Aneg[:], in0=Aneg[:], scalar1=-1.0, scalar2=None, op0=ALU.mult), [Aneg], [Aneg])
    Dsk = bc_row(k, 'ssd_D', IN['ssm_D'], IN['ssm_D'].h.ap()[li], 16)
    nw = bc_row(k, 'ssd_nw', IN['ssm_norm_w'], IN['ssm_norm_w'].h.ap()[li], 1024)
    xbc = k.sb('ssd_xbc', [128, 12, 131], F32)
    xc = k.sb('ssd_xc', [128, 12, 128], F32)
    xt2 = k.sb('ssd_xt2', [128, 12, 128], F32)
    zT = k.sb('ssd_zT', [128, 8, 128], F32)
    dtT = k.sb('ssd_dtT', [16, 128], F32)
    Xtok = k.sb('ssd_Xtok', [128, 1024], F32)
    ztok = k.sb('ssd_ztok', [128, 1024], F32)
    Btok = k.sb('ssd_Btok', [128, 2, 128], F32)
    sm = k.sb('ssd_sm', [128, 16, 8], F32)
    DB = k.sb('ssd_DB', [128, 16], F32)
    Xdt = k.sb('ssd_Xdt', [128, 1024], F32)
    Xw = k.sb('ssd_Xw', [128, 1024], F32)
    Rt = k.sb('ssd_Rt', [128, 16, 128], F32)
    LT = k.sb('ssd_LT', [128, 16, 128], F32)
    CBs = k.sb('ssd_CBs', [128, 2, 128], F32)
    Y = k.sb('ssd_Y', [128, 1024], F32)
    Y2 = k.sb('ssd_Y2', [128, 1024], F32)
    nst = k.sb('ssd_nst', [128, 8], F32)
    ost = k.sb('ssd_ost', [128, 8, 128], BF16)
    zio = k.sb('ssd_zio', [64, 16, 128], F32)
    PB = [k.ps(f'ssd_pb{i}', [128, 512], F32) for i in range(8)]
    def b2(i):
        return PB[i]
    def sm_(n, j):
        return sm[0:n, :, j]
    def load_state(b):
        k.dma('sp', zio[:], st_z[1][b].rearrange("h p n -> p h n"), [st_z[0]], [zio], zio)
        for h in range(16):
            P = PB[h // 8]
            k.op('pe', lambda e: e.transpose(out=P[:, (h % 8) * 64:(h % 8 + 1) * 64], in_=zio[:, h, :], identity=c.idf[0:64, 0:64]), [zio, c.idf], [P])
        for g in range(2):
            k.op('dve', lambda e: e.tensor_copy(out=Z[:, g * 512:(g + 1) * 512], in_=PB[g][:, :]), [PB[g]], [Z])
    def store_state(dst):
        for h in range(16):
            P = PB[h // 4]
            k.op('pe', lambda e: e.transpose(out=P[0:64, (h % 4) * 128:(h % 4 + 1) * 128], in_=Z[:, h * 64:(h + 1) * 64], identity=c.idf[:, :]), [Z, c.idf], [P])
        for q in range(4):
            k.op('dve', lambda e: e.tensor_copy(out=zio[:, q * 4:(q + 1) * 4, :], in_=PB[q][0:64, :].rearrange("p (h n) -> p h n", n=128)), [PB[q]], [zio])
        k.dma('sp', dst[1].rearrange("h p n -> p h n"), zio[:], [zio], [dst[0]], zio)
    def chunk(t0, n, carry):
        xv = projT.h.ap()[off_xbc:off_xbc + 1536, :].rearrange("(f p) t -> p f t", p=128)
        if carry is None:
            k.dma('sp', xbc[:, :, 0:n + 3], xv[:, :, t0 - 3:t0 + n], [projT], [xbc], xbc)
        else:
            k.dma('sp', xbc[:, :, 3:n + 3], xv[:, :, t0:t0 + n], [projT], [xbc], xbc)
            if carry == 'zero':
                k.op('pool', lambda e: e.memset(xbc[:, :, 0:3], 0.0), [], [xbc])
            else:
                b = carry[1]
                k.op('pool', lambda e: e.tensor_copy(out=xbc[:, :, 0:3], in_=scv[:, :, 3 * b:3 * b + 3]), [scv], [xbc])
        k.dma('sp', zT[:, :, 0:n], projT.h.ap()[off_z:off_z + 1024, t0:t0 + n].rearrange("(f p) t -> p f t", p=128), [projT], [zT], zT)
        k.dma('sp', dtT[:, 0:n], projT.h.ap()[off_dt:off_dt + 16, t0:t0 + n], [projT], [dtT], dtT)
        def wv(i):
            return cwS[:, :, i:i + 1].broadcast_to([128, 12, n])
        k.op('dve', lambda e: e.tensor_tensor(out=xc[:, :, 0:n], in0=xbc[:, :, 3:3 + n], in1=wv(3), op=ALU.mult), [xbc, cwS], [xc])
        for i in (2, 1, 0):
            k.op('pool', lambda e: e.tensor_tensor(out=xt2[:, :, 0:n], in0=xbc[:, :, i:i + n], in1=wv(i), op=ALU.mult), [xbc, cwS], [xt2])
            k.op('dve', lambda e: e.tensor_tensor(out=xc[:, :, 0:n], in0=xc[:, :, 0:n], in1=xt2[:, :, 0:n], op=ALU.add), [xc, xt2], [xc])
        k.op('dve', lambda e: e.tensor_tensor(out=xc[:, :, 0:n], in0=xc[:, :, 0:n], in1=wv(4), op=ALU.add), [xc, cwS], [xc])
        k.op('act', lambda e: e.activation(out=xc[:, :, 0:n], in_=xc[:, :, 0:n], func=AF.Silu), [xc], [xc])
        for f in range(8):
            P = PB[f // 4]
            k.op('pe', lambda e: e.transpose(out=P[0:n, (f % 4) * 128:(f % 4 + 1) * 128], in_=xc[:, f, 0:n], identity=c.idf[:, :]), [xc, c.idf], [P])
        for g in range(2):
            k.op('dve', lambda e: e.tensor_copy(out=Xtok[0:n, g * 512:(g + 1) * 512], in_=PB[g][0:n, :]), [PB[g]], [Xtok])
        for f in range(8):
            P = PB[2 + f // 4]
            k.op('pe', lambda e: e.transpose(out=P[0:n, (f % 4) * 128:(f % 4 + 1) * 128], in_=zT[:, f, 0:n], identity=c.idf[:, :]), [zT, c.idf], [P])
        for g in range(2):
            k.op('act', lambda e: e.activation(out=ztok[0:n, g * 512:(g + 1) * 512], in_=PB[2 + g][0:n, :], func=AF.Silu), [PB[2 + g]], [ztok])
        P = PB[4]
        for g in range(2):
            k.op('pe', lambda e: e.transpose(out=P[0:n, g * 128:(g + 1) * 128], in_=xc[:, 8 + g, 0:n], identity=c.idf[:, :]), [xc, c.idf], [P])
        k.op('pe', lambda e: e.transpose(out=P[0:n, 256:272], in_=dtT[0:16, 0:n], identity=c.idf[0:16, 0:16]), [dtT, c.idf], [P])
        k.op('dve', lambda e: e.tensor_copy(out=Btok[0:n, :, :], in_=P[0:n, 0:256].rearrange("p (g s) -> p g s", s=128)), [P], [Btok])
        k.op('dve', lambda e: e.tensor_tensor(out=sm_(n, 0), in0=P[0:n, 256:272], in1=dtb[0:n, :], op=ALU.add), [P, dtb], [sm])
        k.op('act', lambda e: e.activation(out=sm_(n, 0), in_=sm_(n, 0), func=AF.Exp), [sm], [sm])
        k.op('act', lambda e: e.activation(out=sm_(n, 0), in_=sm_(n, 0), func=AF.Ln, bias=1.0, scale=1.0), [sm], [sm])
        k.op('dve', lambda e: e.tensor_tensor(out=sm_(n, 1), in0=sm_(n, 0), in1=Aneg[0:n, :], op=ALU.mult), [sm, Aneg], [sm])
        P5 = PB[5]
        k.op('pe', lambda e: e.matmul(P5[0:n, 0:16], lhsT=c.U[0:n, 0:n], rhs=sm_(n, 1), start=True, stop=True), [c.U, sm], [P5])
        k.op('dve', lambda e: e.tensor_copy(out=sm_(n, 2), in_=P5[0:n, 0:16]), [P5], [sm])
        k.op('act', lambda e: e.activation(out=sm_(n, 3), in_=P5[0:n, 0:16], func=AF.Exp), [P5], [sm])
        P6 = PB[6]
        k.op('pe', lambda e: e.matmul(P6[:, 0:16], lhsT=c.SEL[n][0:n, :], rhs=sm_(n, 2), start=True, stop=True), [c.SEL[n], sm], [P6])
        k.op('act', lambda e: e.activation(out=DB[:, :], in_=P6[:, 0:16], func=AF.Exp), [P6], [DB])
        k.op('dve', lambda e: e.tensor_tensor(out=sm_(n, 4), in0=P6[0:n, 0:16], in1=sm_(n, 2), op=ALU.subtract), [P6, sm], [sm])
        k.op('act', lambda e: e.activation(out=sm_(n, 4), in_=sm_(n, 4), func=AF.Exp), [sm], [sm])
        X3 = Xtok[0:n, :].rearrange("p (h q) -> p h q", q=64)
        k.op('dve', lambda e: e.tensor_tensor(out=Xdt[0:n, :].rearrange("p (h q) -> p h q", q=64), in0=X3, in1=sm[0:n, :, 0:1].broadcast_to([n, 16, 64]), op=ALU.mult), [Xtok, sm], [Xdt])
        k.op('pool', lambda e: e.tensor_tensor(out=Xw[0:n, :].rearrange("p (h q) -> p h q", q=64), in0=Xdt[0:n, :].rearrange("p (h q) -> p h q", q=64), in1=sm[0:n, :, 4:5].broadcast_to([n, 16, 64]), op=ALU.mult), [Xdt, sm], [Xw])
        P7 = PB[7]
        for g in range(2):
            k.op('pe', lambda e: e.matmul(P7[0:n, g * 128:g * 128 + n], lhsT=xc[:, 8 + g, 0:n], rhs=xc[:, 10 + g, 0:n], start=True, stop=True), [xc], [P7])
        k.op('dve', lambda e: e.tensor_copy(out=CBs[0:n, :, 0:n], in_=P7[0:n, 0:256].rearrange("p (g s) -> p g s", s=128)[:, :, 0:n]), [P7], [CBs])
        k.op('pool', lambda e: e.tensor_tensor(out=Rt[0:n, :, 0:n], in0=c.U[0:n, 0:n].unsqueeze(1).broadcast_to([n, 16, n]), in1=sm[0:n, :, 1:2].broadcast_to([n, 16, n]), op=ALU.mult), [c.U, sm], [Rt])
        for q in range(4):
            k.op('pe', lambda e: e.matmul(PB[4 + q][0:n, 0:4 * n].rearrange("p (h i) -> p h i", i=n), lhsT=c.ONES[0:n, 0:n], rhs=Rt[0:n, 4 * q:4 * q + 4, 0:n], start=True, stop=True), [c.ONES, Rt], [PB[4 + q]])
        for q in range(4):
            k.op('dve', lambda e: e.tensor_tensor(out=LT[0:n, 4 * q:4 * q + 4, 0:n], in0=PB[4 + q][0:n, 0:4 * n].rearrange("p (h i) -> p h i", i=n), in1=sm[0:n, 4 * q:4 * q + 4, 2:3].broadcast_to([n, 4, n]), op=ALU.subtract), [PB[4 + q], sm], [LT])
        k.op('pool', lambda e: e.tensor_tensor(out=LT[0:n, :, 0:n], in0=LT[0:n, :, 0:n], in1=c.MN[0:n, 0:n].unsqueeze(1).broadcast_to([n, 16, n]), op=ALU.add), [LT, c.MN], [LT])
        k.op('act', lambda e: e.activation(out=LT[0:n, :, 0:n], in_=LT[0:n, :, 0:n], func=AF.Exp), [LT], [LT])
        k.op('dve', lambda e: e.tensor_tensor(out=LT[0:n, :, 0:n].rearrange("p (g r) i -> p g r i", r=8), in0=LT[0:n, :, 0:n].rearrange("p (g r) i -> p g r i", r=8), in1=CBs[0:n, :, 0:n].unsqueeze(2).broadcast_to([n, 2, 8, n]), op=ALU.mult), [LT, CBs], [LT])
        for h in range(16):
            P = PB[h // 8]
            k.op('pe', lambda e: e.matmul(P[0:n, (h % 8) * 64:(h % 8 + 1) * 64], lhsT=LT[0:n, h, 0:n], rhs=Xdt[0:n, h * 64:(h + 1) * 64], start=True, stop=True), [LT, Xdt], [P])
        for g in range(2):
            k.op('pe', lambda e: e.matmul(PB[2 + g][0:n, :], lhsT=xc[:, 10 + g, 0:n], rhs=Z[:, g * 512:(g + 1) * 512], start=True, stop=True), [xc, Z], [PB[2 + g]])
        for g in range(2):
            sl = slice(g * 512, (g + 1) * 512)
            k.op('dve', lambda e: e.tensor_tensor(out=Y[0:n, sl].rearrange("p (h q) -> p h q", q=64), in0=PB[2 + g][0:n, :].rearrange("p (h q) -> p h q", q=64), in1=sm[0:n, 8 * g:8 * g + 8, 3:4].broadcast_to([n, 8, 64]), op=ALU.mult), [PB[2 + g], sm], [Y])
            k.op('dve', lambda e: e.tensor_tensor(out=Y[0:n, sl], in0=Y[0:n, sl], in1=PB[g][0:n, :], op=ALU.add), [Y, PB[g]], [Y])
        k.op('pool', lambda e: e.tensor_tensor(out=Y2[0:n, :].rearrange("p (h q) -> p h q", q=64), in0=X3, in1=Dsk[0:n, :].unsqueeze(2).broadcast_to([n, 16, 64]), op=ALU.mult), [Xtok, Dsk], [Y2])
        k.op('dve', lambda e: e.tensor_tensor(out=Y[0:n, :], in0=Y[0:n, :], in1=Y2[0:n, :], op=ALU.add), [Y, Y2], [Y])
        k.op('dve', lambda e: e.tensor_tensor(out=Y[0:n, :], in0=Y[0:n, :], in1=ztok[0:n, :], op=ALU.mult), [Y, ztok], [Y])
        for g in range(2):
            sl = slice(g * 512, (g + 1) * 512)
            k.op('act', lambda e: e.activation(out=Y2[0:n, sl], in_=Y[0:n, sl], func=AF.Square, accum_out=nst[0:n, g:g + 1]), [Y], [Y2, nst])
        k.op('dve', lambda e: e.tensor_scalar(out=nst[0:n, 2:4], in0=nst[0:n, 0:2], scalar1=1.0 / 512, scalar2=1e-6, op0=ALU.mult, op1=ALU.add), [nst], [nst])
        k.op('act', lambda e: e.activation(out=nst[0:n, 4:6], in_=nst[0:n, 2:4], func=AF.Sqrt), [nst], [nst])
        k.op('dve', lambda e: e.reciprocal(out=nst[0:n, 6:8], in_=nst[0:n, 4:6]), [nst], [nst])
        for g in range(2):
            sl = slice(g * 512, (g + 1) * 512)
            k.op('dve', lambda e: e.scalar_tensor_tensor(out=Y2[0:n, sl], in0=Y[0:n, sl], scalar=nst[0:n, 6 + g:7 + g], in1=nw[0:n, sl], op0=ALU.mult, op1=ALU.mult), [Y, nst, nw], [Y2])
        for f in range(8):
            P = PB[f // 4]
            k.op('pe', lambda e: e.transpose(out=P[:, (f % 4) * 128:(f % 4) * 128 + n], in_=Y2[0:n, f * 128:(f + 1) * 128], identity=c.idf[0:n, 0:n]), [Y2, c.idf], [P])
        for g in range(2):
            k.op('act', lambda e: e.copy(out=ost[:, g * 4:(g + 1) * 4, 0:n], in_=PB[g][:, :].rearrange("p (f t) -> p f t", t=128)[:, :, 0:n]), [PB[g]], [ost])
        k.dma('sp', mixT.h.ap().rearrange("k p t -> p k t")[:, kc0:kc0 + 8, t0:t0 + n], ost[:, :, 0:n], [ost], [mixT], ost)
        for g in range(2):
            k.op('pe', lambda e: e.matmul(PB[2 + g][:, :], lhsT=Btok[0:n, g, :], rhs=Xw[0:n, g * 512:(g + 1) * 512], start=True, stop=True), [Btok, Xw], [PB[2 + g]])
        k.op('pool', lambda e: e.tensor_tensor(out=Z[:, :].rearrange("p (h q) -> p h q", q=64), in0=Z[:, :].rearrange("p (h q) -> p h q", q=64), in1=DB[:, :].unsqueeze(2).broadcast_to([128, 16, 64]), op=ALU.mult), [Z, DB], [Z])
        for g in range(2):
            k.op('dve', lambda e: e.tensor_tensor(out=Z[:, g * 512:(g + 1) * 512], in0=Z[:, g * 512:(g + 1) * 512], in1=PB[2 + g][:, :], op=ALU.add), [Z, PB[2 + g]], [Z])
    k.op('pool', lambda e: e.memset(Z[:], 0.0), [], [Z])
    for t0 in range(0, T, 128):
        chunk(t0, 128, 'zero' if t0 == 0 else None)
        if t0 + 128 == T:
            k.op('pool', lambda e: e.tensor_copy(out=cvo[:, :, 0:3], in_=xbc[:, :, 128:131]), [xbc], [cvo])
    store_state(pz_out)
    for b in range(NS):
        load_state(b)
        chunk(T + 8 * b, 8, ('state', b))
        k.op('pool', lambda e: e.tensor_copy(out=cvo[:, :, 3 * (b + 1):3 * (b + 2)], in_=xbc[:, :, 8:11]), [xbc], [cvo])
        store_state((sz_out[0], sz_out[1][b]))
    k.phase_end()
    k.phase_begin()
    fm_to_rows(k, c, cvo, 3 * (NS + 1), 12, [(pconv_out[0], pconv_out[1][r]) for r in range(3)] + [(sconv_out[0], sconv_out[1][b, r]) for b in range(NS) for r in range(3)], 'ssdcvo')
    k.phase_end()

def gdn_phase(k, c, li, projT, mixT, T, NS, IN, ps_out, pconv_out, ss_out, sconv_out, st_s, st_conv, off_qkv, off_z, off_ba, kc0=0):
    S = k.sb('gdn_S', [128, 8, 128], F32, persist=True)
    cwS = k.sb('gdn_cw', [128, 24, 4], F32, persist=True)
    scv = k.sb('gdn_scv', [128, 24, 3 * NS], F32, persist=True)
    cvo = k.sb('gdn_cvo', [128, 24, 3 * (NS + 1)], F32, persist=True)
    OD = k.sb('gdn_od', [128, 128], F32, persist=True)
    k.op('dve', lambda e: e.tensor_scalar(out=OD[:], in0=c.idf[:], scalar1=-1.0, scalar2=1.0, op0=ALU.mult, op1=ALU.add), [c.idf], [OD])
    nwc3 = k.sb('gdn_nw', [128, 1, 1], F32, persist=True)
    k.phase_begin()
    rows_to_fm(k, c, nwc3, [(IN['gdn_norm_w'], IN['gdn_norm_w'].h.ap()[li])], 1, 'gdnnw')
    k.phase_end()
    k.phase_begin()
    cw_t = IN['gdn_conv_w']
    rows_to_fm(k, c, cwS, [(cw_t, cw_t.h.ap()[li, i]) for i in range(4)], 24, 'gdncw')
    k.phase_end()
    k.phase_begin()
    rows_to_fm(k, c, scv, [(st_conv[0], st_conv[1][b, r]) for b in range(NS) for r in range(3)], 24, 'gdnsc')
    k.phase_end()
    k.phase_begin()
    dtb = bc_row(k, 'gdn_dtb', IN['gdn_dt_bias'], IN['gdn_dt_bias'].h.ap()[li], 8)
    Aneg = bc_row(k, 'gdn_A', IN['gdn_A_log'], IN['gdn_A_log'].h.ap()[li], 8)
    k.op('act', lambda e: e.activation(out=Aneg[:], in_=Aneg[:], func=AF.Exp), [Aneg], [Aneg])
    k.op('dve', lambda e: e.tensor_scalar(out=Aneg[:], in0=Aneg[:], scalar1=-1.0, scalar2=None, op0=ALU.mult), [Aneg], [Aneg])
    xin = k.sb('gdn_xin', [128, 24, 131], F32)
    xc = k.sb('gdn_xc', [128, 24, 128], F32)
    xt2 = k.sb('gdn_xt2', [128, 24, 128], F32)
    zT = k.sb('gdn_zT', [128, 8, 128], F32)
    baT = k.sb('gdn_baT', [16, 128], F32)
    sm = k.sb('gdn_sm', [128, 8, 8], F32)
    EGl = k.sb('gdn_EGl', [128, 8], F32)
    Rt = k.sb('gdn_Rt', [128, 8, 128], F32)
    EB = k.sb('gdn_EB', [128, 8, 128], F32)
    gam = k.sb('gdn_gam', [128, 8, 128], F32)
    Ktok = k.sb('gdn_Ktok', [128, 8, 128], F32)
    Vtok = k.sb('gdn_Vtok', [128, 8, 128], F32)
    NT = k.sb('gdn_NT', [128, 8, 128], F32)
    Nm = k.sb('gdn_Nm', [128, 8, 128], F32)
    NT2 = k.sb('gdn_NT2', [128, 8, 128], F32)
    Nm2 = k.sb('gdn_Nm2', [128, 8, 128], F32)
    PT = k.sb('gdn_PT', [128, 8, 128], F32)
    Aqk = k.sb('gdn_Aqk', [128, 8, 128], F32)
    W1 = k.sb('gdn_W1', [128, 8, 128], F32)
    W2 = k.sb('gdn_W2', [128, 8, 128], F32)
    W3 = k.sb('gdn_W3', [128, 8, 128], F32)
    ost = k.sb('gdn_ost', [128, 8, 128], BF16)
    PB = [k.ps(f'gdn_pb{i}', [128, 512], F32) for i in range(8)]
    def pv(pair, h, rows, cols):
        return PB[2 * pair + h // 4][0:rows, (h % 4) * 128:(h % 4) * 128 + cols]
    def pbank(pair, half, rows, cols):
        return PB[2 * pair + half][0:rows, :].rearrange("p (h i) -> p h i", i=128)[:, :, 0:cols]
    def ptl(pair, half):
        return PB[2 * pair + half]
    def evac(dst, pair, rows, cols, fn):
        for half in range(2):
            fn(half, dst[0:rows, 4 * half:4 * half + 4, 0:cols], pbank(pair, half, rows, cols), ptl(pair, half))
    def chunk(t0, n, carry, steps):
        xv = projT.h.ap()[off_qkv:off_qkv + 3072, :].rearrange("(f p) t -> p f t", p=128)
        if carry is None:
            k.dma('sp', xin[:, :, 0:n + 3], xv[:, :, t0 - 3:t0 + n], [projT], [xin], xin)
        else:
            k.dma('sp', xin[:, :, 3:n + 3], xv[:, :, t0:t0 + n], [projT], [xin], xin)
            if carry == 'zero':
                k.op('pool', lambda e: e.memset(xin[:, :, 0:3], 0.0), [], [xin])
            else:
                b = carry[1]
                k.op('pool', lambda e: e.tensor_copy(out=xin[:, :, 0:3], in_=scv[:, :, 3 * b:3 * b + 3]), [scv], [xin])
        k.dma('sp', zT[:, :, 0:n], projT.h.ap()[off_z:off_z + 1024, t0:t0 + n].rearrange("(f p) t -> p f t", p=128), [projT], [zT], zT)
        k.dma('sp', baT[:, 0:n], projT.h.ap()[off_ba:off_ba + 16, t0:t0 + n], [projT], [baT], baT)
        def wv(i):
            return cwS[:, :, i:i + 1].broadcast_to([128, 24, n])
        k.op('dve', lambda e: e.tensor_tensor(out=xc[:, :, 0:n], in0=xin[:, :, 3:3 + n], in1=wv(3), op=ALU.mult), [xin, cwS], [xc])
        for i in (2, 1, 0):
            k.op('pool', lambda e: e.tensor_tensor(out=xt2[:, :, 0:n], in0=xin[:, :, i:i + n], in1=wv(i), op=ALU.mult), [xin, cwS], [xt2])
            k.op('dve', lambda e: e.tensor_tensor(out=xc[:, :, 0:n], in0=xc[:, :, 0:n], in1=xt2[:, :, 0:n], op=ALU.add), [xc, xt2], [xc])
        k.op('act', lambda e: e.activation(out=xc[:, :, 0:n], in_=xc[:, :, 0:n], func=AF.Silu), [xc], [xc])
        k.op('act', lambda e: e.activation(out=zT[:, :, 0:n], in_=zT[:, :, 0:n], func=AF.Silu), [zT], [zT])
        if getattr(c, 'gstop', 99) <= 1: return
        k.op('act', lambda e: e.activation(out=xt2[:, 0:16, 0:n], in_=xc[:, 0:16, 0:n], func=AF.Square), [xc], [xt2])
        for q in range(4):
            k.op('pe', lambda e: e.matmul(PB[q][:, 0:4 * n].rearrange("p (h i) -> p h i", i=n), lhsT=c.ONES[:, :], rhs=xt2[:, 4 * q:4 * q + 4, 0:n], start=True, stop=True), [c.ONES, xt2], [PB[q]])
        for q in range(4):
            k.op('act', lambda e: e.activation(out=xt2[:, 4 * q:4 * q + 4, 0:n], in_=PB[q][:, 0:4 * n].rearrange("p (h i) -> p h i", i=n), func=AF.Sqrt, bias=c.eps6[:, 0:1], scale=1.0), [PB[q], c.eps6], [xt2])
        k.op('dve', lambda e: e.reciprocal(out=xt2[:, 0:16, 0:n], in_=xt2[:, 0:16, 0:n]), [xt2], [xt2])
        k.op('dve', lambda e: e.scalar_tensor_tensor(out=xc[:, 0:8, 0:n], in0=xc[:, 0:8, 0:n], scalar=128 ** -0.5, in1=xt2[:, 0:8, 0:n], op0=ALU.mult, op1=ALU.mult), [xc, xt2], [xc])
        k.op('dve', lambda e: e.tensor_tensor(out=xc[:, 8:16, 0:n], in0=xc[:, 8:16, 0:n], in1=xt2[:, 8:16, 0:n], op=ALU.mult), [xc, xt2], [xc])
        if getattr(c, 'gstop', 99) <= 2: return
        for h in range(8):
            k.op('pe', lambda e: e.transpose(out=pv(2, h, n, 128), in_=xc[:, 8 + h, 0:n], identity=c.idf[:, :]), [xc, c.idf], [ptl(2, h // 4)])
            k.op('pe', lambda e: e.transpose(out=pv(3, h, n, 128), in_=xc[:, 16 + h, 0:n], identity=c.idf[:, :]), [xc, c.idf], [ptl(3, h // 4)])
        evac(Ktok, 2, n, 128, lambda half, o, p, pt: k.op('dve', lambda e: e.tensor_copy(out=o, in_=p), [pt], [Ktok]))
        evac(Vtok, 3, n, 128, lambda half, o, p, pt: k.op('act', lambda e: e.copy(out=o, in_=p), [pt], [Vtok]))
        if getattr(c, 'gstop', 99) <= 3: return
        P0 = PB[0]
        k.op('pe', lambda e: e.transpose(out=P0[0:n, 0:16], in_=baT[0:16, 0:n], identity=c.idf[0:16, 0:16]), [baT, c.idf], [P0])
        k.op('act', lambda e: e.activation(out=sm[0:n, :, 0], in_=P0[0:n, 0:8], func=AF.Sigmoid), [P0], [sm])
        k.op('dve', lambda e: e.tensor_tensor(out=sm[0:n, :, 1], in0=P0[0:n, 8:16], in1=dtb[0:n, :], op=ALU.add), [P0, dtb], [sm])
        k.op('act', lambda e: e.activation(out=sm[0:n, :, 1], in_=sm[0:n, :, 1], func=AF.Exp), [sm], [sm])
        k.op('act', lambda e: e.activation(out=sm[0:n, :, 1], in_=sm[0:n, :, 1], func=AF.Ln, bias=1.0, scale=1.0), [sm], [sm])
        k.op('dve', lambda e: e.tensor_tensor(out=sm[0:n, :, 1], in0=sm[0:n, :, 1], in1=Aneg[0:n, :], op=ALU.mult), [sm, Aneg], [sm])
        k.op('dve', lambda e: e.tensor_scalar(out=sm[0:n, :, 4], in0=sm[0:n, :, 0], scalar1=-1.0, scalar2=None, op0=ALU.mult), [sm], [sm])
        P1 = PB[1]
        k.op('pe', lambda e: e.matmul(P1[0:n, 0:8], lhsT=c.U[0:n, 0:n], rhs=sm[0:n, :, 1], start=True, stop=True), [c.U, sm], [P1])
        k.op('dve', lambda e: e.tensor_copy(out=sm[0:n, :, 2], in_=P1[0:n, 0:8]), [P1], [sm])
        k.op('pe', lambda e: e.matmul(P0[:, 16:24], lhsT=c.SEL[n][0:n, :], rhs=sm[0:n, :, 2], start=True, stop=True), [c.SEL[n], sm], [P0])
        k.op('act', lambda e: e.activation(out=EGl[:, :], in_=P0[:, 16:24], func=AF.Exp), [P0], [EGl])
        k.op('dve', lambda e: e.tensor_tensor(out=sm[0:n, :, 3], in0=P0[0:n, 16:24], in1=sm[0:n, :, 2], op=ALU.subtract), [P0, sm], [sm])
        k.op('act', lambda e: e.activation(out=sm[0:n, :, 3], in_=sm[0:n, :, 3], func=AF.Exp), [sm], [sm])
        if getattr(c, 'gstop', 99) <= 4: return
        k.op('pool', lambda e: e.tensor_tensor(out=Rt[0:n, :, 0:n], in0=c.U[0:n, 0:n].unsqueeze(1).broadcast_to([n, 8, n]), in1=sm[0:n, :, 1:2].broadcast_to([n, 8, n]), op=ALU.mult), [c.U, sm], [Rt])
        for half in range(2):
            k.op('pe', lambda e: e.matmul(pbank(1, half, 128, n), lhsT=c.ONES[0:n, :], rhs=Rt[0:n, 4 * half:4 * half + 4, 0:n], start=True, stop=True), [c.ONES, Rt], [ptl(1, half)])
        if getattr(c, 'gstop', 99) <= 4.2: return
        evac(EB, 1, 128, n, lambda half, o, p, pt: k.op('act', lambda e: e.activation(out=o, in_=p, func=AF.Exp), [pt], [EB]))
        if getattr(c, 'gstop', 99) <= 4.4: return
        for half in range(2):
            k.op('dve', lambda e: e.tensor_tensor(out=gam[0:n, 4 * half:4 * half + 4, 0:n], in0=pbank(1, half, n, n), in1=sm[0:n, 4 * half:4 * half + 4, 2:3].broadcast_to([n, 4, n]), op=ALU.subtract), [ptl(1, half), sm], [gam])
        if getattr(c, 'gstop', 99) <= 4.6: return
        k.op('pool', lambda e: e.tensor_tensor(out=gam[0:n, :, 0:n], in0=gam[0:n, :, 0:n], in1=c.MN[0:n, 0:n].unsqueeze(1).broadcast_to([n, 8, n]), op=ALU.add), [gam, c.MN], [gam])
        k.op('act', lambda e: e.activation(out=gam[0:n, :, 0:n], in_=gam[0:n, :, 0:n], func=AF.Exp), [gam], [gam])
        if getattr(c, 'gstop', 99) <= 5: return
        for h in range(8):
            k.op('pe', lambda e: e.matmul(pv(2, h, n, n), lhsT=xc[:, 8 + h, 0:n], rhs=xc[:, 8 + h, 0:n], start=True, stop=True), [xc], [ptl(2, h // 4)])
            k.op('pe', lambda e: e.matmul(pv(3, h, n, n), lhsT=xc[:, 8 + h, 0:n], rhs=xc[:, h, 0:n], start=True, stop=True), [xc], [ptl(3, h // 4)])
        for half in range(2):
            hs = slice(4 * half, 4 * half + 4)
            k.op('dve', lambda e: e.tensor_tensor(out=NT[0:n, hs, 0:n], in0=pbank(2, half, n, n), in1=gam[0:n, hs, 0:n], op=ALU.mult), [ptl(2, half), gam], [NT])
            k.op('dve', lambda e: e.tensor_tensor(out=Aqk[0:n, hs, 0:n], in0=pbank(3, half, n, n), in1=gam[0:n, hs, 0:n], op=ALU.mult), [ptl(3, half), gam], [Aqk])
        k.op('pool', lambda e: e.tensor_tensor(out=NT[0:n, :, 0:n], in0=NT[0:n, :, 0:n], in1=sm[0:n, :, 4:5].broadcast_to([n, 8, n]), op=ALU.mult), [NT, sm], [NT])
        k.op('pool', lambda e: e.tensor_tensor(out=NT[0:n, :, 0:n], in0=NT[0:n, :, 0:n], in1=OD[0:n, 0:n].unsqueeze(1).broadcast_to([n, 8, n]), op=ALU.mult), [NT, OD], [NT])
        if getattr(c, 'gstop', 99) <= 6: return
        for h in range(8):
            k.op('pe', lambda e: e.transpose(out=pv(0, h, n, n), in_=NT[0:n, h, 0:n], identity=c.idf[0:n, 0:n]), [NT, c.idf], [ptl(0, h // 4)])
        evac(Nm, 0, n, n, lambda half, o, p, pt: k.op('act', lambda e: e.copy(out=o, in_=p), [pt], [Nm]))
        k.op('dve', lambda e: e.tensor_tensor(out=PT[0:n, :, 0:n], in0=NT[0:n, :, 0:n], in1=c.idf[0:n, 0:n].unsqueeze(1).broadcast_to([n, 8, n]), op=ALU.add), [NT, c.idf], [PT])
        X, XT, X2, XT2 = Nm, NT, Nm2, NT2
        for it in range(steps):
            lastit = (it == steps - 1)
            for h in range(8):
                k.op('pe', lambda e: e.matmul(pv(1, h, n, n), lhsT=XT[0:n, h, 0:n], rhs=X[0:n, h, 0:n], start=True, stop=True), [XT, X], [ptl(1, h // 4)])
                if not lastit:
                    k.op('pe', lambda e: e.matmul(pv(2, h, n, n), lhsT=X[0:n, h, 0:n], rhs=XT[0:n, h, 0:n], start=True, stop=True), [XT, X], [ptl(2, h // 4)])
            evac(X2, 1, n, n, lambda half, o, p, pt: k.op('act', lambda e: e.copy(out=o, in_=p), [pt], [X2]))
            if not lastit:
                evac(XT2, 2, n, n, lambda half, o, p, pt: k.op('dve', lambda e: e.tensor_copy(out=o, in_=p), [pt], [XT2]))
            for h in range(8):
                k.op('pe', lambda e: e.matmul(pv(3, h, n, n), lhsT=X2[0:n, h, 0:n], rhs=PT[0:n, h, 0:n], start=True, stop=True), [X2, PT], [ptl(3, h // 4)])
            for half in range(2):
                hs = slice(4 * half, 4 * half + 4)
                k.op('dve', lambda e: e.tensor_tensor(out=PT[0:n, hs, 0:n], in0=PT[0:n, hs, 0:n], in1=pbank(3, half, n, n), op=ALU.add), [PT, ptl(3, half)], [PT])
            X, X2 = X2, X
            XT, XT2 = XT2, XT
        if getattr(c, 'gstop', 99) <= 7: return
        k.op('pool', lambda e: e.tensor_tensor(out=W1[:, :, 0:n], in0=xc[:, 8:16, 0:n], in1=EB[:, :, 0:n], op=ALU.mult), [xc, EB], [W1])
        k.op('pool', lambda e: e.tensor_tensor(out=W2[:, :, 0:n], in0=xc[:, 0:8, 0:n], in1=EB[:, :, 0:n], op=ALU.mult), [xc, EB], [W2])
        for h in range(8):
            k.op('pe', lambda e: e.matmul(pv(0, h, n, 128), lhsT=W1[:, h, 0:n], rhs=S[:, h, :], start=True, stop=True), [W1, S], [ptl(0, h // 4)])
        for half in range(2):
            hs = slice(4 * half, 4 * half + 4)
            k.op('dve', lambda e: e.tensor_tensor(out=W3[0:n, hs, :], in0=Vtok[0:n, hs, :], in1=pbank(0, half, n, 128), op=ALU.subtract), [Vtok, ptl(0, half)], [W3])
        for h in range(8):
            k.op('pe', lambda e: e.matmul(pv(1, h, n, 128), lhsT=PT[0:n, h, 0:n], rhs=W3[0:n, h, :], start=True, stop=True), [PT, W3], [ptl(1, h // 4)])
        for half in range(2):
            hs = slice(4 * half, 4 * half + 4)
            k.op('dve', lambda e: e.tensor_tensor(out=W1[0:n, hs, :], in0=pbank(1, half, n, 128), in1=sm[0:n, hs, 0:1].broadcast_to([n, 4, 128]), op=ALU.mult), [ptl(1, half), sm], [W1])
        for h in range(8):
            k.op('pe', lambda e: e.matmul(pv(2, h, 128, n), lhsT=S[:, h, :], rhs=W2[:, h, 0:n], start=True, stop=False), [S, W2], [ptl(2, h // 4)])
            k.op('pe', lambda e: e.matmul(pv(2, h, 128, n), lhsT=W1[0:n, h, :], rhs=Aqk[0:n, h, 0:n], start=False, stop=True), [W1, Aqk], [ptl(2, h // 4)])
        k.op('pool', lambda e: e.tensor_tensor(out=W3[0:n, :, :], in0=Ktok[0:n, :, :], in1=sm[0:n, :, 3:4].broadcast_to([n, 8, 128]), op=ALU.mult), [Ktok, sm], [W3])
        for h in range(8):
            k.op('pe', lambda e: e.matmul(pv(3, h, 128, 128), lhsT=W3[0:n, h, :], rhs=W1[0:n, h, :], start=True, stop=True), [W3, W1], [ptl(3, h // 4)])
        k.op('pool', lambda e: e.tensor_tensor(out=S[:, :, :], in0=S[:, :, :], in1=EGl[:, :].unsqueeze(2).broadcast_to([128, 8, 128]), op=ALU.mult), [S, EGl], [S])
        for half in range(2):
            hs = slice(4 * half, 4 * half + 4)
            k.op('dve', lambda e: e.tensor_tensor(out=S[:, hs, :], in0=S[:, hs, :], in1=pbank(3, half, 128, 128), op=ALU.add), [S, ptl(3, half)], [S])
        if getattr(c, 'gstop', 99) <= 8: return
        evac(W2, 2, 128, n, lambda half, o, p, pt: k.op('act', lambda e: e.activation(out=o, in_=p, func=AF.Square), [pt], [W2]))
        if getattr(c, 'gstop', 99) <= 8.2: return
        for half in range(2):
            k.op('pe', lambda e: e.matmul(pbank(0, half, 128, n), lhsT=c.ONES[:, :], rhs=W2[:, 4 * half:4 * half + 4, 0:n], start=True, stop=True), [c.ONES, W2], [ptl(0, half)])
        if getattr(c, 'gstop', 99) <= 8.4: return
        evac(W2, 0, 128, n, lambda half, o, p, pt: k.op('act', lambda e: e.activation(out=o, in_=p, func=AF.Sqrt, bias=c.eps6[:, 0:1], scale=1.0 / 128), [pt, c.eps6], [W2]))
        if getattr(c, 'gstop', 99) <= 8.6: return
        k.op('dve', lambda e: e.reciprocal(out=W2[:, :, 0:n], in_=W2[:, :, 0:n]), [W2], [W2])
        for half in range(2):
            hs = slice(4 * half, 4 * half + 4)
            k.op('dve', lambda e: e.tensor_tensor(out=W2[:, hs, 0:n], in0=W2[:, hs, 0:n], in1=pbank(2, half, 128, n), op=ALU.mult), [W2, ptl(2, half)], [W2])
        if getattr(c, 'gstop', 99) <= 8.8: return
        k.op('dve', lambda e: e.scalar_tensor_tensor(out=ost[:, :, 0:n], in0=W2[:, :, 0:n], scalar=nwc3[:, 0, 0:1], in1=zT[:, :, 0:n], op0=ALU.mult, op1=ALU.mult), [W2, zT, nwc3], [ost])
        k.dma('sp', mixT.h.ap().rearrange("k p t -> p k t")[:, kc0:kc0 + 8, t0:t0 + n], ost[:, :, 0:n], [ost], [mixT], ost)
    k.op('pool', lambda e: e.memset(S[:], 0.0), [], [S])
    for t0 in range(0, T, 128):
        chunk(t0, 128, 'zero' if t0 == 0 else None, 6)
        if t0 + 128 == T:
            k.op('pool', lambda e: e.tensor_copy(out=cvo[:, :, 0:3], in_=xin[:, :, 128:131]), [xin], [cvo])
    k.dma('sp', ps_out[1].rearrange("h d e -> d h e"), S[:], [S], [ps_out[0]], S)
    for b in range(0 if getattr(c, 'skip_sample', False) else NS):
        k.dma('sp', S[:], st_s[1][b].rearrange("h d e -> d h e"), [st_s[0]], [S], S)
        chunk(T + 8 * b, 8, ('state', b), 2)
        k.op('pool', lambda e: e.tensor_copy(out=cvo[:, :, 3 * (b + 1):3 * (b + 2)], in_=xin[:, :, 8:11]), [xin], [cvo])
        k.dma('sp', ss_out[1][b].rearrange("h d e -> d h e"), S[:], [S], [ss_out[0]], S)
    k.phase_end()
    k.phase_begin()
    fm_to_rows(k, c, cvo, 3 * (NS + 1), 24, [(pconv_out[0], pconv_out[1][r]) for r in range(3)] + [(sconv_out[0], sconv_out[1][b, r]) for b in range(NS) for r in range(3)], 'gdncvo')
    k.phase_end()

def row_to_col(k, c, dst_ap, dst_tl, tl, ap_row, m, name):
    Rt = k.sb(f'{name}_r', [1, 128], F32)
    k.dma('sp', Rt[0:1, 0:m], ap_row.unsqueeze(0), [tl], [Rt], Rt)
    P = k.ps(f'{name}_p', [128, 8], F32)
    k.op('pe', lambda e: e.transpose(out=P[0:m, 0:1], in_=Rt[0:1, 0:m], identity=c.idf[0:1, 0:1]), [Rt, c.idf], [P])
    k.op('dve', lambda e: e.tensor_copy(out=dst_ap, in_=P[0:m, 0:1]), [P], [dst_tl])

def rwkv_phase(k, c, li, projT, mixT, T, NS, IN, pS_out, pshift_out, sS_out, sshift_out, st_S, st_shift, off, kc0=8):
    NSH = 1 + NS
    PM = k.sb('rw_PM', [128, 2], F32, persist=True)
    par = k.sb('rw_par', [128, 8, 8], F32, persist=True)
    mu = k.sb('rw_mu', [128, 27, 1], F32, persist=True)
    shp = k.sb('rw_shp', [128, 27, NS], F32, persist=True)
    sho = k.sb('rw_sho', [128, 27, NSH], F32, persist=True)
    BD = k.sb('rw_BD', [128, 128], F32, persist=True)
    US = k.sb('rw_US', [128, 128], F32, persist=True)
    UI = k.sb('rw_UI', [128, 128], F32, persist=True)
    k.op('pool', lambda e: e.memset(BD[:], 0.0), [], [BD])
    k.op('pool', lambda e: e.memset(PM[:], 0.0), [], [PM])
    k.op('pool', lambda e: e.memset(PM[0:64, 0:1], 1.0), [], [PM])
    k.op('pool', lambda e: e.memset(PM[64:128, 1:2], 1.0), [], [PM])
    k.op('pool', lambda e: e.memset(BD[0:64, 0:64], 1.0), [], [BD])
    k.op('pool', lambda e: e.memset(BD[64:128, 64:128], 1.0), [], [BD])
    k.op('dve', lambda e: e.tensor_tensor(out=US[:], in0=c.U[:], in1=c.idf[:], op=ALU.subtract), [c.U, c.idf], [US])
    k.op('dve', lambda e: e.tensor_copy(out=UI[:], in_=c.U[:]), [c.U], [UI])
    k.op('pool', lambda e: e.memset(sho[:], 0.0), [], [sho])
    k.op('pool', lambda e: e.memset(shp[:], 0.0), [], [shp])
    k.op('pool', lambda e: e.memset(mu[:], 0.0), [], [mu])
    k.phase_begin()
    names = ['rwkv_a0', 'rwkv_k_k', 'rwkv_k_a', 'rwkv_r_k', 'rwkv_ln_w', 'rwkv_ln_b']
    rows_to_fm(k, c, par[:, :, 0:6], [(IN[nm], IN[nm].h.ap()[li]) for nm in names], 8, 'rwpar') if False else None
    par6 = k.sb('rw_par6', [128, 8, 6], F32)
    rows_to_fm(k, c, par6, [(IN[nm], IN[nm].h.ap()[li]) for nm in names], 8, 'rwpar')
    k.op('dve', lambda e: e.tensor_copy(out=par[:, :, 0:6], in_=par6[:]), [par6], [par])
    k.phase_end()
    k.phase_begin()
    mu26 = k.sb('rw_mu26', [128, 26, 1], F32)
    rows_to_fm(k, c, mu26, [(IN['rwkv_mu'], IN['rwkv_mu'].h.ap()[li, 0:3328])], 26, 'rwmu')
    k.op('dve', lambda e: e.tensor_copy(out=mu[:, 0:26, :], in_=mu26[:]), [mu26], [mu])
    row_to_col(k, c, mu[0:32, 26, 0:1], mu, IN['rwkv_mu'], IN['rwkv_mu'].h.ap()[li, 3328:3360], 32, 'rwmut')
    k.phase_end()
    k.phase_begin()
    sh26 = k.sb('rw_sh26', [128, 26, NS], F32)
    rows_to_fm(k, c, sh26, [(st_shift[0], st_shift[1][b, 0:3328]) for b in range(NS)], 26, 'rwsh')
    k.op('dve', lambda e: e.tensor_copy(out=shp[:, 0:26, :], in_=sh26[:]), [sh26], [shp])
    for b in range(NS):
        row_to_col(k, c, shp[0:32, 26, b:b + 1], shp, st_shift[0], st_shift[1][b, 3328:3360], 32, f'rwsht{b}')
    k.phase_end()
    k.phase_begin()
    w0B = bc_row(k, 'rw_w0B', IN['rwkv_w0'], IN['rwkv_w0'].h.ap()[li], 1024)
    w2 = k.sb('rw_w2', [64, 1024], F32)
    a2 = k.sb('rw_a2', [128, 1024], F32)
    g2a = k.sb('rw_g2a', [128, 1024], F32)
    g2b = k.sb('rw_g2b', [32, 1024], F32)
    k.dma('sp', w2[:], IN['rwkv_w2'].h.ap()[li], [IN['rwkv_w2']], [w2], w2)
    k.dma('sp', a2[64:128, :], IN['rwkv_a2'].h.ap()[li], [IN['rwkv_a2']], [a2], a2)
    k.dma('sp', g2a[:], IN['rwkv_g2'].h.ap()[li, 0:128], [IN['rwkv_g2']], [g2a], g2a)
    k.dma('sp', g2b[:], IN['rwkv_g2'].h.ap()[li, 128:160], [IN['rwkv_g2']], [g2b], g2b)
    zin = k.sb('rw_zin', [128, 27, 129], F32)
    zm = k.sb('rw_zm', [128, 27, 128], F32)
    t24 = k.sb('rw_t24', [128, 27, 128], F32)
    EL = k.sb('rw_EL', [128, 8, 128], F32)
    ELi = k.sb('rw_ELi', [128, 8, 128], F32)
    ELx = k.sb('rw_ELx', [128, 8, 128], F32)
    av = k.sb('rw_av', [128, 8, 128], F32)
    kk = k.sb('rw_kk', [128, 8, 128], F32)
    rt_ = k.sb('rw_rt', [128, 8, 128], F32)
    kt_ = k.sb('rw_kt', [128, 8, 128], F32)
    bt_ = k.sb('rw_bt', [128, 8, 128], F32)
    at_ = av
    gate = k.sb('rw_gate', [128, 8, 128], F32)
    bon = k.sb('rw_bon', [128, 8, 128], F32)
    yT = k.sb('rw_yT', [128, 8, 128], F32)
    ldt = k.sb('rw_ldt', [128, 1024], F32)
    Vtok = kk; Ktok = ELi; Btok = ELx
    NT = k.sb('rw_NT', [128, 4, 128], F32); Nm = k.sb('rw_Nm', [128, 4, 128], F32)
    NT2 = k.sb('rw_NT2', [128, 4, 128], F32); Nm2 = k.sb('rw_Nm2', [128, 4, 128], F32)
    PT = k.sb('rw_PT', [128, 4, 128], F32)
    Aak = k.sb('rw_Aak', [128, 4, 128], F32); Arb = k.sb('rw_Arb', [128, 4, 128], F32); Ark = k.sb('rw_Ark', [128, 4, 128], F32)
    btM = k.sb('rw_btM', [128, 4, 128], F32); ktM = k.sb('rw_ktM', [128, 4, 128], F32)
    X1s = k.sb('rw_X1s', [128, 4, 64], F32); Uu = k.sb('rw_Uu', [128, 2, 128], F32); Wt = k.sb('rw_Wt', [128, 2, 128], F32)
    Upad = k.sb('rw_Upad', [128, 4, 128], F32); Vpad = k.sb('rw_Vpad', [128, 4, 128], F32)
    Zbd = k.sb('rw_Zbd', [128, 8, 128], F32); zio = k.sb('rw_zio', [128, 8, 128], F32)
    ost = k.sb('rw_ost', [128, 8, 128], BF16)
    PB = [k.ps(f'rw_pb{i}', [128, 512], F32) for i in range(8)]
    k.op('pool', lambda e: e.memset(zin[:], 0.0), [], [zin])
    def pv(pair, hh, rows, cols):
        return PB[2 * pair + hh // 4][0:rows, (hh % 4) * 128:(hh % 4) * 128 + cols]
    def pbank(pair, half, rows, cols):
        return PB[2 * pair + half][0:rows, :].rearrange("p (h i) -> p h i", i=128)[:, :, 0:cols]
    def ptl(pair, half):
        return PB[2 * pair + half]
    def fm(tile, h, n):
        return tile[(h % 2) * 64:(h % 2) * 64 + 64, h // 2, 0:n]
    def chunk(t0, n, carry, steps):
        if getattr(c, 'rstop', 99) <= 0: return
        zv = projT.h.ap()[off:off + 3328, :].rearrange("(f p) t -> p f t", p=128)
        zv2 = projT.h.ap()[off + 3328:off + 3360, :]
        if carry is None:
            k.dma('sp', zin[:, 0:26, 0:n + 1], zv[:, :, t0 - 1:t0 + n], [projT], [zin], zin)
            k.dma('sp', zin[0:32, 26, 0:n + 1], zv2[:, t0 - 1:t0 + n], [projT], [zin], zin)
        else:
            k.dma('sp', zin[:, 0:26, 1:n + 1], zv[:, :, t0:t0 + n], [projT], [zin], zin)
            k.dma('sp', zin[0:32, 26, 1:n + 1], zv2[:, t0:t0 + n], [projT], [zin], zin)
            if carry == 'zero':
                k.op('pool', lambda e: e.memset(zin[:, :, 0:1], 0.0), [], [zin])
            else:
                b = carry[1]
                k.op('pool', lambda e: e.tensor_copy(out=zin[:, :, 0:1], in_=shp[:, :, b:b + 1]), [shp], [zin])
        if getattr(c, 'rstop', 99) <= 0.5: return
        k.op('dve', lambda e: e.tensor_tensor(out=t24[:, :, 0:n], in0=zin[:, :, 0:n], in1=zin[:, :, 1:n + 1], op=ALU.subtract), [zin], [t24])
        k.op('pool', lambda e: e.tensor_tensor(out=t24[:, :, 0:n], in0=t24[:, :, 0:n], in1=mu[:, :, 0:1].broadcast_to([128, 27, n]), op=ALU.mult), [t24, mu], [t24])
        k.op('dve', lambda e: e.tensor_tensor(out=zm[:, :, 0:n], in0=t24[:, :, 0:n], in1=zin[:, :, 1:n + 1], op=ALU.add), [t24, zin], [zm])
        if getattr(c, 'rstop', 99) <= 1: return
        k.op('act', lambda e: e.activation(out=t24[0:64, 24, 0:n], in_=zm[0:64, 24, 0:n], func=AF.Tanh), [zm], [t24])
        k.op('act', lambda e: e.activation(out=t24[:, 25, 0:n], in_=zm[:, 25, 0:n], func=AF.Sigmoid), [zm], [t24])
        k.op('act', lambda e: e.activation(out=t24[0:32, 26, 0:n], in_=zm[0:32, 26, 0:n], func=AF.Sigmoid), [zm], [t24])
        for g in range(2):
            k.op('pe', lambda e: e.matmul(PB[g][0:n, :], lhsT=t24[0:64, 24, 0:n], rhs=w2[:, g * 512:(g + 1) * 512], start=True, stop=True), [t24, w2], [PB[g]])
            k.op('dve', lambda e: e.tensor_tensor(out=ldt[0:n, g * 512:(g + 1) * 512], in0=PB[g][0:n, :], in1=w0B[0:n, g * 512:(g + 1) * 512], op=ALU.add), [PB[g], w0B], [ldt])
        k.op('act', lambda e: e.activation(out=ldt[0:n, :], in_=ldt[0:n, :], func=AF.Sigmoid), [ldt], [ldt])
        k.op('dve', lambda e: e.tensor_scalar(out=ldt[0:n, :], in0=ldt[0:n, :], scalar1=-math.exp(-0.5), scalar2=None, op0=ALU.mult), [ldt], [ldt])
        if getattr(c, 'rstop', 99) <= 2: return
        for f in range(8):
            k.op('pe', lambda e: e.matmul(pv(1, f, 128, n), lhsT=ldt[0:n, f * 128:(f + 1) * 128], rhs=UI[0:n, 0:n], start=True, stop=True), [ldt, UI], [ptl(1, f // 4)])
            k.op('pe', lambda e: e.matmul(pv(2, f, 128, n), lhsT=ldt[0:n, f * 128:(f + 1) * 128], rhs=US[0:n, 0:n], start=True, stop=True), [ldt, US], [ptl(2, f // 4)])
        for half in range(2):
            hs = slice(4 * half, 4 * half + 4)
            k.op('act', lambda e: e.activation(out=EL[:, hs, 0:n], in_=pbank(1, half, 128, n), func=AF.Exp), [ptl(1, half)], [EL])
            k.op('act', lambda e: e.activation(out=ELi[:, hs, 0:n], in_=pbank(1, half, 128, n), func=AF.Exp, scale=-1.0), [ptl(1, half)], [ELi])
            k.op('act', lambda e: e.activation(out=ELx[:, hs, 0:n], in_=pbank(2, half, 128, n), func=AF.Exp), [ptl(2, half)], [ELx])
        if getattr(c, 'rstop', 99) <= 3: return
        for f in range(8):
            k.op('pe', lambda e: e.matmul(pv(0, f, 128, n), lhsT=a2[64:128, f * 128:(f + 1) * 128], rhs=zm[64:128, 24, 0:n], start=True, stop=True), [a2, zm], [ptl(0, f // 4)])
            k.op('pe', lambda e: e.matmul(pv(3, f, 128, n), lhsT=g2a[:, f * 128:(f + 1) * 128], rhs=t24[:, 25, 0:n], start=True, stop=False), [g2a, t24], [ptl(3, f // 4)])
            k.op('pe', lambda e: e.matmul(pv(3, f, 128, n), lhsT=g2b[:, f * 128:(f + 1) * 128], rhs=t24[0:32, 26, 0:n], start=False, stop=True), [g2b, t24], [ptl(3, f // 4)])
        for half in range(2):
            hs = slice(4 * half, 4 * half + 4)
            k.op('dve', lambda e: e.tensor_tensor(out=av[:, hs, 0:n], in0=pbank(0, half, 128, n), in1=par[:, hs, 0:1].broadcast_to([128, 4, n]), op=ALU.add), [ptl(0, half), par], [av])
            k.op('act', lambda e: e.copy(out=gate[:, hs, 0:n], in_=pbank(3, half, 128, n)), [ptl(3, half)], [gate])
        k.op('act', lambda e: e.activation(out=av[:, :, 0:n], in_=av[:, :, 0:n], func=AF.Sigmoid), [av], [av])
        R_ = zm[:, 0:8, 0:n]; K_ = zm[:, 8:16, 0:n]; V_ = zm[:, 16:24, 0:n]
        def pb(j):
            return par[:, :, j:j + 1].broadcast_to([128, 8, n])
        if getattr(c, 'rstop', 99) <= 4: return
        k.op('dve', lambda e: e.tensor_tensor(out=kk[:, :, 0:n], in0=K_, in1=pb(1), op=ALU.mult), [zm, par], [kk])
        k.op('act', lambda e: e.activation(out=t24[:, 0:8, 0:n], in_=kk[:, :, 0:n], func=AF.Square), [kk], [t24])
        for half in range(2):
            k.op('pe', lambda e: e.matmul(pbank(0, half, 128, n), lhsT=BD[:, :], rhs=t24[:, 4 * half:4 * half + 4, 0:n], start=True, stop=True), [BD, t24], [ptl(0, half)])
        for half in range(2):
            hs = slice(4 * half, 4 * half + 4)
            k.op('act', lambda e: e.activation(out=t24[:, hs, 0:n], in_=pbank(0, half, 128, n), func=AF.Sqrt, bias=c.eps6[:, 0:1], scale=1.0), [ptl(0, half), c.eps6], [t24])
        k.op('dve', lambda e: e.reciprocal(out=t24[:, 0:8, 0:n], in_=t24[:, 0:8, 0:n]), [t24], [t24])
        k.op('dve', lambda e: e.tensor_tensor(out=kk[:, :, 0:n], in0=kk[:, :, 0:n], in1=t24[:, 0:8, 0:n], op=ALU.mult), [kk, t24], [kk])
        k.op('dve', lambda e: e.scalar_tensor_tensor(out=t24[:, 8:16, 0:n], in0=av[:, :, 0:n], scalar=-1.0, in1=pb(2), op0=ALU.add, op1=ALU.mult), [av, par], [t24])
        k.op('dve', lambda e: e.scalar_tensor_tensor(out=t24[:, 8:16, 0:n], in0=t24[:, 8:16, 0:n], scalar=1.0, in1=K_, op0=ALU.add, op1=ALU.mult), [t24, zm], [t24])
        KM = t24[:, 8:16, 0:n]
        k.op('dve', lambda e: e.tensor_tensor(out=t24[:, 16:24, 0:n], in0=R_, in1=KM, op=ALU.mult), [zm, t24], [t24])
        k.op('pool', lambda e: e.tensor_tensor(out=t24[:, 16:24, 0:n], in0=t24[:, 16:24, 0:n], in1=pb(3), op=ALU.mult), [t24, par], [t24])
        for half in range(2):
            k.op('pe', lambda e: e.matmul(pbank(3, half, 128, n), lhsT=BD[:, :], rhs=t24[:, 16 + 4 * half:16 + 4 * half + 4, 0:n], start=True, stop=True), [BD, t24], [ptl(3, half)])
        for half in range(2):
            hs = slice(4 * half, 4 * half + 4)
            k.op('dve', lambda e: e.tensor_tensor(out=bon[:, hs, 0:n], in0=pbank(3, half, 128, n), in1=zm[:, 16 + 4 * half:16 + 4 * half + 4, 0:n], op=ALU.mult), [ptl(3, half), zm], [bon])
        if getattr(c, 'rstop', 99) <= 5: return
        k.op('dve', lambda e: e.tensor_tensor(out=rt_[:, :, 0:n], in0=R_, in1=EL[:, :, 0:n], op=ALU.mult), [zm, EL], [rt_])
        k.op('pool', lambda e: e.tensor_tensor(out=kt_[:, :, 0:n], in0=KM, in1=ELi[:, :, 0:n], op=ALU.mult), [t24, ELi], [kt_])
        k.op('dve', lambda e: e.tensor_tensor(out=bt_[:, :, 0:n], in0=kk[:, :, 0:n], in1=av[:, :, 0:n], op=ALU.mult), [kk, av], [bt_])
        k.op('pool', lambda e: e.tensor_tensor(out=bt_[:, :, 0:n], in0=bt_[:, :, 0:n], in1=ELi[:, :, 0:n], op=ALU.mult), [bt_, ELi], [bt_])
        k.op('dve', lambda e: e.scalar_tensor_tensor(out=at_[:, :, 0:n], in0=kk[:, :, 0:n], scalar=-1.0, in1=ELx[:, :, 0:n], op0=ALU.mult, op1=ALU.mult), [kk, ELx], [at_])
        if getattr(c, 'rstop', 99) <= 6: return
        for (src, dst, pr, eng) in ((zm, Vtok, 0, 'dve'), (kt_, Ktok, 1, 'act'), (bt_, Btok, 2, 'dve')):
            for f in range(8):
                sap = src[:, 16 + f, 0:n] if src is zm else src[:, f, 0:n]
                k.op('pe', lambda e: e.transpose(out=pv(pr, f, n, 128), in_=sap, identity=c.idf[:, :]), [src, c.idf], [ptl(pr, f // 4)])
            for half in range(2):
                hs = slice(4 * half, 4 * half + 4)
                if eng == 'dve':
                    k.op('dve', lambda e: e.tensor_copy(out=dst[0:n, hs, :], in_=pbank(pr, half, n, 128)), [ptl(pr, half)], [dst])
                else:
                    k.op('act', lambda e: e.copy(out=dst[0:n, hs, :], in_=pbank(pr, half, n, 128)), [ptl(pr, half)], [dst])
        if getattr(c, 'rstop', 99) <= 7: return
        for hg in range(4):
            fts = slice(2 * hg, 2 * hg + 2)
            P0 = PB[0]; P1 = PB[1]; P2 = PB[2]; P3 = PB[3]; P4 = PB[4]; P5 = PB[5]; P6 = PB[6]; P7 = PB[7]
            def v4(P, rows, cols):
                return P[0:rows, :].rearrange("p (h i) -> p h i", i=128)[:, :, 0:cols]
            pm4 = PM[:, :].unsqueeze(1).unsqueeze(3).broadcast_to([128, 2, 2, n])
            k.op('dve', lambda e: e.tensor_tensor(out=btM[:, :, 0:n].rearrange("p (f w) t -> p f w t", w=2), in0=bt_[:, fts, 0:n].unsqueeze(2).broadcast_to([128, 2, 2, n]), in1=pm4, op=ALU.mult), [bt_, PM], [btM])
            k.op('pool', lambda e: e.tensor_tensor(out=ktM[:, :, 0:n].rearrange("p (f w) t -> p f w t", w=2), in0=kt_[:, fts, 0:n].unsqueeze(2).broadcast_to([128, 2, 2, n]), in1=pm4, op=ALU.mult), [kt_, PM], [ktM])
            for (ltM, rt, dst, msk, P) in ((btM, at_, NT, US, P4), (ktM, at_, Aak, US, P5), (btM, rt_, Arb, UI, P6), (ktM, rt_, Ark, UI, P7)):
                for hh in range(4):
                    f = 2 * hg + hh // 2
                    k.op('pe', lambda e: e.matmul(P[0:n, hh * 128:hh * 128 + n], lhsT=ltM[:, hh, 0:n], rhs=rt[:, f, 0:n], start=True, stop=True), [ltM, rt], [P])
                k.op('dve', lambda e: e.tensor_tensor(out=dst[0:n, :, 0:n], in0=v4(P, n, n), in1=msk[0:n, 0:n].unsqueeze(1).broadcast_to([n, 4, n]), op=ALU.mult), [P, msk], [dst])
            for hh in range(4):
                k.op('pe', lambda e: e.transpose(out=P0[0:n, hh * 128:hh * 128 + n], in_=NT[0:n, hh, 0:n], identity=c.idf[0:n, 0:n]), [NT, c.idf], [P0])
            k.op('act', lambda e: e.copy(out=Nm[0:n, :, 0:n], in_=v4(P0, n, n)), [P0], [Nm])
            k.op('dve', lambda e: e.tensor_tensor(out=PT[0:n, :, 0:n], in0=NT[0:n, :, 0:n], in1=c.idf[0:n, 0:n].unsqueeze(1).broadcast_to([n, 4, n]), op=ALU.add), [NT, c.idf], [PT])
            X, XT, X2, XT2 = Nm, NT, Nm2, NT2
            for it in range(steps):
                lastit = (it == steps - 1)
                for hh in range(4):
                    k.op('pe', lambda e: e.matmul(P1[0:n, hh * 128:hh * 128 + n], lhsT=XT[0:n, hh, 0:n], rhs=X[0:n, hh, 0:n], start=True, stop=True), [XT, X], [P1])
                    if not lastit:
                        k.op('pe', lambda e: e.matmul(P2[0:n, hh * 128:hh * 128 + n], lhsT=X[0:n, hh, 0:n], rhs=XT[0:n, hh, 0:n], start=True, stop=True), [XT, X], [P2])
                k.op('act', lambda e: e.copy(out=X2[0:n, :, 0:n], in_=v4(P1, n, n)), [P1], [X2])
                if not lastit:
                    k.op('dve', lambda e: e.tensor_copy(out=XT2[0:n, :, 0:n], in_=v4(P2, n, n)), [P2], [XT2])
                for hh in range(4):
                    k.op('pe', lambda e: e.matmul(P3[0:n, hh * 128:hh * 128 + n], lhsT=X2[0:n, hh, 0:n], rhs=PT[0:n, hh, 0:n], start=True, stop=True), [X2, PT], [P3])
                k.op('dve', lambda e: e.tensor_tensor(out=PT[0:n, :, 0:n], in0=PT[0:n, :, 0:n], in1=v4(P3, n, n), op=ALU.add), [PT, P3], [PT])
                X, X2 = X2, X
                XT, XT2 = XT2, XT
            if getattr(c, 'rstop', 99) <= 9: return
            for fl in range(2):
                f = 2 * hg + fl
                k.op('pe', lambda e: e.matmul(P4[0:n, fl * 128:fl * 128 + 128], lhsT=at_[:, f, 0:n], rhs=Zbd[:, f, :], start=True, stop=False), [at_, Zbd], [P4])
                for two in range(2):
                    hh = 2 * fl + two
                    k.op('pe', lambda e: e.matmul(P4[0:n, fl * 128 + two * 64:fl * 128 + two * 64 + 64], lhsT=Aak[0:n, hh, 0:n], rhs=Vtok[0:n, f, two * 64:two * 64 + 64], start=False, stop=(two == 1)), [Aak, Vtok], [P4])
            k.op('dve', lambda e: e.tensor_copy(out=X1s[0:n, :, :], in_=P4[0:n, 0:256].rearrange("p (h e) -> p h e", e=64)), [P4], [X1s])
            for hh in range(4):
                k.op('pe', lambda e: e.matmul(P5[0:n, hh * 64:hh * 64 + 64], lhsT=PT[0:n, hh, 0:n], rhs=X1s[0:n, hh, :], start=True, stop=True), [PT, X1s], [P5])
            k.op('act', lambda e: e.copy(out=Uu[0:n, :, :], in_=P5[0:n, 0:256].rearrange("p (f e) -> p f e", e=128)), [P5], [Uu])
            for two in range(2):
                k.op('dve', lambda e: e.tensor_copy(out=Upad[0:n, :, :].rearrange("p (f w) e -> p f w e", w=2)[:, :, two, two * 64:two * 64 + 64], in_=P5[0:n, 0:256].rearrange("p (f w e) -> p f w e", w=2, e=64)[:, :, two, :]), [P5], [Upad])
                k.op('pool', lambda e: e.tensor_copy(out=Vpad[0:n, :, :].rearrange("p (f w) e -> p f w e", w=2)[:, :, two, two * 64:two * 64 + 64], in_=Vtok[0:n, fts, two * 64:two * 64 + 64]), [Vtok], [Vpad])
            for fl in range(2):
                f = 2 * hg + fl
                po = P6[:, fl * 128:fl * 128 + n]
                k.op('pe', lambda e: e.matmul(po, lhsT=Zbd[:, f, :], rhs=rt_[:, f, 0:n], start=True, stop=False), [Zbd, rt_], [P6])
                for two in range(2):
                    hh = 2 * fl + two
                    k.op('pe', lambda e: e.matmul(po, lhsT=Upad[0:n, hh, :], rhs=Arb[0:n, hh, 0:n], start=False, stop=False), [Upad, Arb], [P6])
                    k.op('pe', lambda e: e.matmul(po, lhsT=Vpad[0:n, hh, :], rhs=Ark[0:n, hh, 0:n], start=False, stop=(two == 1)), [Vpad, Ark], [P6])
            k.op('dve', lambda e: e.tensor_copy(out=yT[:, fts, 0:n], in_=P6[:, 0:256].rearrange("p (f t) -> p f t", t=128)[:, :, 0:n]), [P6], [yT])
            for fl in range(2):
                f = 2 * hg + fl
                pz = P7[:, fl * 128:fl * 128 + 128]
                k.op('pe', lambda e: e.matmul(pz, lhsT=Btok[0:n, f, :], rhs=Uu[0:n, fl, :], start=True, stop=False), [Btok, Uu], [P7])
                k.op('pe', lambda e: e.matmul(pz, lhsT=Ktok[0:n, f, :], rhs=Vtok[0:n, f, :], start=False, stop=True), [Ktok, Vtok], [P7])
            k.op('dve', lambda e: e.tensor_tensor(out=Wt[:, :, :], in0=P7[:, 0:256].rearrange("p (f e) -> p f e", e=128), in1=BD[:, :].unsqueeze(1).broadcast_to([128, 2, 128]), op=ALU.mult), [P7, BD], [Wt])
            k.op('dve', lambda e: e.tensor_tensor(out=Zbd[:, fts, :], in0=Zbd[:, fts, :], in1=Wt[:, :, :], op=ALU.add), [Zbd, Wt], [Zbd])
            k.op('pool', lambda e: e.tensor_tensor(out=Zbd[:, fts, :], in0=Zbd[:, fts, :], in1=EL[:, fts, n - 1:n].broadcast_to([128, 2, 128]), op=ALU.mult), [Zbd, EL], [Zbd])
        if getattr(c, 'rstop', 99) <= 10: return
        for half in range(2):
            k.op('pe', lambda e: e.matmul(pbank(0, half, 128, n), lhsT=BD[:, :], rhs=yT[:, 4 * half:4 * half + 4, 0:n], start=True, stop=True), [BD, yT], [ptl(0, half)])
        for half in range(2):
            hs = slice(4 * half, 4 * half + 4)
            k.op('dve', lambda e: e.scalar_tensor_tensor(out=yT[:, hs, 0:n], in0=pbank(0, half, 128, n), scalar=-1.0 / 64, in1=yT[:, hs, 0:n], op0=ALU.mult, op1=ALU.add), [ptl(0, half), yT], [yT])
        k.op('act', lambda e: e.activation(out=t24[:, 0:8, 0:n], in_=yT[:, :, 0:n], func=AF.Square), [yT], [t24])
        for half in range(2):
            k.op('pe', lambda e: e.matmul(pbank(1, half, 128, n), lhsT=BD[:, :], rhs=t24[:, 4 * half:4 * half + 4, 0:n], start=True, stop=True), [BD, t24], [ptl(1, half)])
        for half in range(2):
            hs = slice(4 * half, 4 * half + 4)
            k.op('act', lambda e: e.activation(out=t24[:, hs, 0:n], in_=pbank(1, half, 128, n), func=AF.Sqrt, bias=c.epsgn[:, 0:1], scale=1.0 / 64), [ptl(1, half), c.epsgn], [t24])
        k.op('dve', lambda e: e.reciprocal(out=t24[:, 0:8, 0:n], in_=t24[:, 0:8, 0:n]), [t24], [t24])
        k.op('dve', lambda e: e.tensor_tensor(out=yT[:, :, 0:n], in0=yT[:, :, 0:n], in1=t24[:, 0:8, 0:n], op=ALU.mult), [yT, t24], [yT])
        k.op('pool', lambda e: e.tensor_tensor(out=yT[:, :, 0:n], in0=yT[:, :, 0:n], in1=pb(4), op=ALU.mult), [yT, par], [yT])
        k.op('dve', lambda e: e.tensor_tensor(out=yT[:, :, 0:n], in0=yT[:, :, 0:n], in1=pb(5), op=ALU.add), [yT, par], [yT])
        k.op('dve', lambda e: e.tensor_tensor(out=yT[:, :, 0:n], in0=yT[:, :, 0:n], in1=bon[:, :, 0:n], op=ALU.add), [yT, bon], [yT])
        k.op('dve', lambda e: e.tensor_tensor(out=ost[:, :, 0:n], in0=yT[:, :, 0:n], in1=gate[:, :, 0:n], op=ALU.mult), [yT, gate], [ost])
        k.dma('sp', mixT.h.ap().rearrange("k p t -> p k t")[:, kc0:kc0 + 8, t0:t0 + n], ost[:, :, 0:n], [ost], [mixT], ost)
    def load_Z(b):
        for two in range(2):
            k.dma('sp', zio[two * 64:two * 64 + 64, :, two * 64:two * 64 + 64], st_S[1][b].rearrange("(f w) v kk -> w v f kk", w=2)[two], [st_S[0]], [zio], zio)
        for f in range(8):
            k.op('pe', lambda e: e.transpose(out=PB[f // 4][:, (f % 4) * 128:(f % 4) * 128 + 128], in_=zio[:, f, :], identity=c.idf[:, :]), [zio, c.idf], [PB[f // 4]])
        for g in range(2):
            k.op('dve', lambda e: e.tensor_copy(out=Zbd[:, 4 * g:4 * g + 4, :], in_=PB[g][:, :].rearrange("p (f e) -> p f e", e=128)), [PB[g]], [Zbd])
    def store_Z(dst):
        for f in range(8):
            k.op('pe', lambda e: e.transpose(out=PB[f // 4][:, (f % 4) * 128:(f % 4) * 128 + 128], in_=Zbd[:, f, :], identity=c.idf[:, :]), [Zbd, c.idf], [PB[f // 4]])
        for g in range(2):
            k.op('dve', lambda e: e.tensor_copy(out=zio[:, 4 * g:4 * g + 4, :], in_=PB[g][:, :].rearrange("p (f e) -> p f e", e=128)), [PB[g]], [zio])
        for two in range(2):
            k.dma('sp', dst[1].rearrange("(f w) v kk -> w v f kk", w=2)[two], zio[two * 64:two * 64 + 64, :, two * 64:two * 64 + 64], [zio], [dst[0]], zio)
    k.op('pool', lambda e: e.memset(zio[:], 0.0), [], [zio])
    k.op('pool', lambda e: e.memset(Upad[:], 0.0), [], [Upad])
    k.op('pool', lambda e: e.memset(Vpad[:], 0.0), [], [Vpad])
    k.op('pool', lambda e: e.memset(Zbd[:], 0.0), [], [Zbd])
    for t0 in range(0, T, 128):
        chunk(t0, 128, 'zero' if t0 == 0 else None, 6)
        if t0 + 128 == T:
            k.op('pool', lambda e: e.tensor_copy(out=sho[:, :, 0:1], in_=zin[:, :, 128:129]), [zin], [sho])
    if not getattr(c, 'rskip_store', False):
        store_Z(pS_out)
    for b in range(0 if getattr(c, 'skip_sample', False) else NS):
        load_Z(b)
        chunk(T + 8 * b, 8, ('state', b), 2)
        k.op('pool', lambda e: e.tensor_copy(out=sho[:, :, b + 1:b + 2], in_=zin[:, :, 8:9]), [zin], [sho])
        store_Z((sS_out[0], sS_out[1][b]))
    k.phase_end()
    k.phase_begin()
    Rt = k.sb('rw_shrt', [NSH, 27 * 128], F32)
    PP = [k.ps(f'rw_shp{i}', [128, 512], F32) for i in range(4)]
    for f0 in range(0, 27, 4):
        nf = min(4, 27 - f0)
        P = PP[(f0 // 4) % 4]
        for f in range(nf):
            k.op('pe', lambda e: e.transpose(out=P[0:NSH, f * 128:(f + 1) * 128], in_=sho[:, f0 + f, :], identity=c.idf[:, :]), [sho, c.idf], [P])
        k.op('dve', lambda e: e.tensor_copy(out=Rt[0:NSH, f0 * 128:(f0 + nf) * 128], in_=P[0:NSH, 0:nf * 128]), [P], [Rt])
    k.dma('sp', pshift_out[1].unsqueeze(0), Rt[0:1, 0:3360], [Rt], [pshift_out[0]], Rt)
    for b in range(NS):
        k.dma('sp', sshift_out[1][b].unsqueeze(0), Rt[b + 1:b + 2, 0:3360], [Rt], [sshift_out[0]], Rt)
    k.phase_end()
def run_mixers(k, c, li, projT, mixT):
    IN, OUT = c.IN, c.OUT
    def I(nm): return (IN[nm], IN[nm].h.ap()[li])
    def O(nm): return (OUT[nm], OUT[nm].h.ap()[li])
    k.scope_begin()
    gdn_phase(k, c, li, projT, mixT, SEQ, NB_S, IN, O('p_gdn'), O('p_gdn_conv'), O('s_gdn'), O('s_gdn_conv'), I('st_gdn'), I('st_gdn_conv'),
              OFF_GDN_QKV, OFF_GDN_Z, OFF_GDN_B, kc0=0)
    k.scope_end()
    k.scope_begin()
    rwkv_phase(k, c, li, projT, mixT, SEQ, NB_S, IN, O('p_rwkv'), O('p_rwkv_shift'), O('s_rwkv'), O('s_rwkv_shift'), I('st_rwkv'), I('st_rwkv_shift'),
               OFF_RWKV, kc0=8)
    k.scope_end()
    k.scope_begin()
    ssd_phase(k, c, li, projT, mixT, SEQ, NB_S, IN, O('p_ssm'), O('p_ssm_conv'), O('s_ssm'), O('s_ssm_conv'), I('st_ssm'), I('st_ssm_conv'),
              OFF_SSM_Z, OFF_SSM_XBC, OFF_SSM_DT, kc0=16)
    k.scope_end()
    k.scope_begin()
    swa_phase(k, c, li, projT, mixT, SEQ, NB_S, OFF_SWA, O('p_swa_k'), O('p_swa_v'), O('s_swa_k'), O('s_swa_v'), I('c_swa_k'), I('c_swa_v'), kc0=24)
    k.scope_end()
_OUT_ORDER = ['y_prompt', 'y_sample', 'p_gdn', 'p_gdn_conv', 'p_rwkv', 'p_rwkv_shift', 'p_ssm', 'p_ssm_conv', 'p_swa_k', 'p_swa_v',
              'p_ffn_conv', 's_gdn', 's_gdn_conv', 's_rwkv', 's_rwkv_shift', 's_ssm', 's_ssm_conv', 's_swa_k', 's_swa_v', 's_ffn_conv']
MIXERS = True

def kernel(**inp):
    f32 = np.float32
    A = {k_: np.asarray(v) for k_, v in inp.items()}
    shared = {}
    for nm in ['norm_mix_pre', 'norm_mix_post', 'norm_ffn_pre', 'norm_ffn_post', 'w_in', 'w_out', 'gdn_conv_w', 'gdn_A_log', 'gdn_dt_bias',
               'gdn_norm_w', 'rwkv_mu', 'rwkv_w0', 'rwkv_w2', 'rwkv_a0', 'rwkv_a2', 'rwkv_g2', 'rwkv_k_k', 'rwkv_k_a', 'rwkv_ln_w', 'rwkv_ln_b',
               'ssm_conv_w', 'ssm_conv_b', 'ssm_dt_bias', 'ssm_A_log', 'ssm_D', 'ssm_norm_w', 'ffn_w_up', 'ffn_conv_w', 'ffn_conv_b', 'ffn_w_down']:
        shared[nm] = np.ascontiguousarray(A[nm], dtype=f32)
    shared['rwkv_r_k'] = np.ascontiguousarray(A['rwkv_r_k'].reshape(DEPTH, 1024), dtype=f32)
    maps = []
    for cidx in range(4):
        m = dict(shared)
        b0 = 2 * cidx
        m['xin'] = np.ascontiguousarray(np.concatenate([A['x_prompt'][cidx], A['x_sample'][b0:b0 + 2].reshape(16, D_MODEL)], axis=0), dtype=f32)
        m['st_gdn'] = np.ascontiguousarray(A['state_gdn'][:, b0:b0 + 2]); m['st_gdn_conv'] = np.ascontiguousarray(A['state_gdn_conv'][:, b0:b0 + 2])
        m['st_rwkv'] = np.ascontiguousarray(A['state_rwkv'][:, b0:b0 + 2]); m['st_rwkv_shift'] = np.ascontiguousarray(A['state_rwkv_shift'][:, b0:b0 + 2])
        m['st_ssm'] = np.ascontiguousarray(A['state_ssm'][:, b0:b0 + 2]); m['st_ssm_conv'] = np.ascontiguousarray(A['state_ssm_conv'][:, b0:b0 + 2])
        m['c_swa_k'] = np.ascontiguousarray(A['cache_swa_k'][:, b0:b0 + 2].reshape(DEPTH, 2, 2048, 1024))
        m['c_swa_v'] = np.ascontiguousarray(A['cache_swa_v'][:, b0:b0 + 2].reshape(DEPTH, 2, 2048, 1024))
        m['st_ffn_conv'] = np.ascontiguousarray(A['state_ffn_conv'][:, b0:b0 + 2])
        maps.append(m)
    maps = maps + maps
    nc = bass.Bass("TRN2", target_bir_lowering=False)
    with ExitStack() as es:
        build_program(nc, es, mixers=MIXERS)
    res = run_bass_kernel_spmd(nc, maps, core_ids=list(range(8)))
    R = res.results
    L = DEPTH
    out = {}
    out['y_prompt'] = np.stack([R[c_]['y'][:SEQ] for c_ in range(4)])
    out['y_sample'] = np.concatenate([R[c_]['y'][SEQ:].reshape(2, 8, D_MODEL) for c_ in range(4)], axis=0)
    def pst(nm, shape):
        return np.stack([R[c_][nm].reshape((L,) + shape) for c_ in range(4)], axis=1)
    def sst(nm, shape):
        return np.concatenate([R[c_][nm].reshape((L, 2) + shape) for c_ in range(4)], axis=1)
    out['p_gdn'] = pst('p_gdn', (8, 128, 128)); out['p_gdn_conv'] = pst('p_gdn_conv', (3, 3072)); out['p_rwkv'] = pst('p_rwkv', (16, 64, 64))
    out['p_rwkv_shift'] = pst('p_rwkv_shift', (3360,)); out['p_ssm'] = pst('p_ssm', (16, 64, 128)); out['p_ssm_conv'] = pst('p_ssm_conv', (3, 1536))
    out['p_swa_k'] = pst('p_swa_k', (2048, 8, 128)); out['p_swa_v'] = pst('p_swa_v', (2048, 8, 128)); out['p_ffn_conv'] = pst('p_ffn_conv', (2, D_FF))
    out['s_gdn'] = sst('s_gdn', (8, 128, 128)); out['s_gdn_conv'] = sst('s_gdn_conv', (3, 3072)); out['s_rwkv'] = sst('s_rwkv', (16, 64, 64))
    out['s_rwkv_shift'] = sst('s_rwkv_shift', (3360,)); out['s_ssm'] = sst('s_ssm', (16, 64, 128)); out['s_ssm_conv'] = sst('s_ssm_conv', (3, 1536))
    out['s_swa_k'] = sst('s_swa_k', (8, 8, 128)); out['s_swa_v'] = sst('s_swa_v', (8, 8, 128)); out['s_ffn_conv'] = sst('s_ffn_conv', (2, D_FF))
    return tuple(np.ascontiguousarray(out[n_], dtype=f32) for n_ in _OUT_ORDER)
```

```python
import math
from concourse.bass_utils import run_bass_kernel_spmd
import numpy as np
from contextlib import ExitStack
import concourse.bass as bass
import concourse.mybir as mybir
F32 = mybir.dt.float32; BF16 = mybir.dt.bfloat16; I32 = mybir.dt.int32
AF = mybir.ActivationFunctionType; ALU = mybir.AluOpType
AX = mybir.AxisListType

SEM_LIMIT = 20000
SAME_ENG_RAW = True

class Tl:
    def __init__(self, k, name, h, space):
        self.k = k; self.name = name; self.h = h; self.space = space
        self.w = {}; self.r = {}
        self.ds = None
    def __getitem__(self, idx):
        return self.h[idx]
    def ap(self):
        return self.h.ap() if self.space == 'dram' else self.h[:]

class DS:
    def __init__(self, sem, name):
        self.sem = sem; self.cnt = 0; self.name = name

class KB:
    def __init__(self, nc, es):
        self.nc = nc; self.es = es
        self.E = {'pe': nc.tensor, 'act': nc.scalar, 'dve': nc.vector, 'pool': nc.gpsimd, 'sp': nc.sync}
        self.esem = {}; self.ecnt = {}; self.eep = {}
        for e in self.E:
            self.eep[e] = 0; self.ecnt[e] = 0
            self.esem[e] = self._sem(f"e_{e}_0")
        self.waited = {}
        self.ds_free = []; self.ds_all = []
        self.nsem = 0
        self.phase_stack = None
        self.phase_tiles = []
        self.persist_tiles = []
        self.phase_ds = []
        self.ninst = 0
    def _sem(self, name):
        s = self.es.enter_context(self.nc.semaphore(name))
        self.nsem = getattr(self, 'nsem', 0) + 1
        return s
    def dram(self, name, shape, dtype, kind="Internal"):
        h = self.nc.dram_tensor(name, list(shape), dtype, kind=kind)
        t = Tl(self, name, h, 'dram')
        self.persist_tiles.append(t)
        return t
    def sb(self, name, shape, dtype, persist=False):
        self.uid = getattr(self, 'uid', 0) + 1; name = f"{name}_u{self.uid}"
        base = self.scope_stack if getattr(self, 'scope_stack', None) is not None else self.es
        st = base if (persist or self.phase_stack is None) else self.phase_stack
        h = st.enter_context(self.nc.sbuf_tensor(name, list(shape), dtype))
        t = Tl(self, name, h, 'sb')
        (self.persist_tiles if st is not self.phase_stack else self.phase_tiles).append(t)
        return t
    def scope_begin(self):
        assert self.phase_stack is None and getattr(self, 'scope_stack', None) is None
        self.scope_stack = ExitStack(); self.scope_stack.__enter__()
    def scope_end(self):
        assert self.phase_stack is None
        self.barrier()
        self.scope_stack.__exit__(None, None, None)
        self.scope_stack = None
    def ps(self, name, shape, dtype=F32, persist=False):
        self.uid = getattr(self, 'uid', 0) + 1; name = f"{name}_u{self.uid}"
        st = self.es if (persist or self.phase_stack is None) else self.phase_stack
        h = st.enter_context(self.nc.psum_tensor(name, list(shape), dtype))
        t = Tl(self, name, h, 'ps')
        (self.persist_tiles if st is self.es else self.phase_tiles).append(t)
        return t
    def get_ds(self, t):
        if t.ds is None:
            if self.ds_free:
                t.ds = self.ds_free.pop()
            else:
                t.ds = DS(self._sem(f"d{len(self.ds_all)}"), f"d{len(self.ds_all)}")
                self.ds_all.append(t.ds)
            if t in self.phase_tiles:
                self.phase_ds.append(t.ds)
        return t.ds
    def _wait(self, eng, deps, skip_self_for=None):
        for key, (sem, val) in deps.items():
            wk = (eng, key)
            if self.waited.get(wk, 0) >= val:
                continue
            self.E[eng].wait_ge(sem, val)
            self.waited[wk] = val
    def _deps(self, eng, reads, writes):
        deps = {}
        own = f"e_{eng}_"
        def add(d, raw):
            for key, (sem, val) in d.items():
                if key.startswith(own):
                    if eng == 'pe' or not (raw and SAME_ENG_RAW):
                        continue
                if key not in deps or deps[key][1] < val:
                    deps[key] = (sem, val)
        for t in reads:
            add(t.w, True)
            if t.space == 'ps':
                add(t.r, False)
        for t in writes:
            add(t.w, False); add(t.r, False)
        return deps
    def _tok(self, eng):
        if self.ecnt[eng] >= SEM_LIMIT:
            self.eep[eng] += 1; self.ecnt[eng] = 0
            self.esem[eng] = self._sem(f"e_{eng}_{self.eep[eng]}")
        self.ecnt[eng] += 1
        return f"e_{eng}_{self.eep[eng]}", self.esem[eng], self.ecnt[eng]
    def op(self, eng, fn, reads=(), writes=()):
        deps = self._deps(eng, reads, writes)
        self._wait(eng, deps)
        key, sem, val = self._tok(eng)
        inst = fn(self.E[eng])
        inst.then_inc(sem, 1)
        self.ninst += 1
        for t in reads:
            t.r[key] = (sem, val)
        for t in writes:
            t.w = {key: (sem, val)}; t.r = {}
        return inst
    def dma(self, q, out, in_, reads, writes, sbt, **kw):
        deps = self._deps(q, reads, writes)
        self._wait(q, deps)
        ds = self.get_ds(sbt)
        inst = self.E[q].dma_start(out=out, in_=in_, **kw)
        ds.cnt += 1
        inst.then_inc(ds.sem, 16)
        self.ninst += 1
        tok = (ds.sem, 16 * ds.cnt)
        for t in reads:
            t.r[ds.name] = tok
        for t in writes:
            if t.space == 'dram':
                t.w[ds.name] = tok
            else:
                t.w = {ds.name: tok}; t.r = {}
        return inst
    def barrier(self, engines=None):
        engines = engines or list(self.E)
        toks = {}
        for e in self.E:
            if self.ecnt[e] > 0:
                toks[f"e_{e}_{self.eep[e]}"] = (self.esem[e], self.ecnt[e])
        for ds in self.ds_all:
            if ds.cnt > 0:
                toks[ds.name] = (ds.sem, 16 * ds.cnt)
        for e in engines:
            own = f"e_{e}_"
            self._wait(e, {k: v for k, v in toks.items() if not k.startswith(own)})
        for t in self.persist_tiles + self.phase_tiles:
            t.w = {}; t.r = {}
    def phase_begin(self):
        assert self.phase_stack is None
        self.phase_stack = ExitStack()
        self.phase_stack.__enter__()
        self.phase_tiles = []; self.phase_ds = []
    def phase_end(self):
        self.barrier()
        for ds in self.phase_ds:
            self.ds_free.append(ds)
        self.phase_stack.__exit__(None, None, None)
        self.phase_stack = None
        self.phase_tiles = []; self.phase_ds = []

def bcast_rows(ap_row, nparts):
    return ap_row.partition_broadcast(nparts) if hasattr(ap_row, 'partition_broadcast') else ap_row

class Ctx:
    pass

def make_ident(k):
    idf = k.sb('ident_f', [128, 128], F32, persist=True)
    idb = k.sb('ident_b', [128, 128], BF16, persist=True)
    k.op('pool', lambda e: e.memset(idf[:], 1.0), [], [idf])
    k.op('pool', lambda e: e.affine_select(out=idf[:], in_=idf[:], pattern=[[-1, 128]], compare_op=ALU.is_equal,
                                             fill=0.0, base=0, channel_multiplier=1), [idf], [idf])
    k.op('dve', lambda e: e.tensor_copy(out=idb[:], in_=idf[:]), [idf], [idb])
    return idf, idb

def norm_phase(k, c, name, NTOK, D, x_src, o_src, wpost_row, wpre_row, x_dst, hT_dst, eps=1e-6):
    KC = D // 128
    k.phase_begin()
    wB = {}
    for nm, wr in (('post', wpost_row), ('pre', wpre_row)):
        if wr is None: continue
        t = k.sb(f'{name}_w{nm}', [128, D], F32)
        k.dma('sp', t[:], wr[1].partition_broadcast(128), [wr[0]], [t], t)
        wB[nm] = t
    NS = 2
    xs = [k.sb(f'{name}_x{i}', [128, D], F32) for i in range(NS)]
    os_ = [k.sb(f'{name}_o{i}', [128, D], F32) for i in range(NS)] if o_src else None
    junk = k.sb(f'{name}_junk', [128, D], BF16)
    xn = [k.sb(f'{name}_xn{i}', [128, D], BF16) for i in range(NS)]
    st = [k.sb(f'{name}_st{i}', [128, 8], F32) for i in range(NS)]
    GT = 512
    hst = [k.sb(f'{name}_h{i}', [128, KC, GT], BF16) for i in range(2)] if hT_dst else None
    pst = [k.ps(f'{name}_p{i}', [128, 2048], BF16) for i in range(2)] if hT_dst else None
    def row_ap(pieces, r0, n):
        off = 0
        for (tl, ap, nr) in pieces:
            if r0 >= off and r0 + n <= off + nr:
                return tl, ap[r0 - off:r0 - off + n, :]
            off += nr
        raise ValueError((r0, n))
    ntile = (NTOK + 127) // 128
    it = 0
    pcount = 0
    for g0 in range(0, NTOK, GT):
        gn = min(GT, NTOK - g0)
        gi = (g0 // GT) % 2
        for t0 in range(g0, g0 + gn, 128):
            n = min(128, NTOK - t0)
            s = it % NS; it += 1
            X = xs[s]
            tl, ap = row_ap(x_src, t0, n)
            k.dma('sp', X[0:n, :], ap, [tl], [X], X)
            S = st[s]
            if o_src:
                O = os_[s]
                tl, ap = row_ap(o_src, t0, n)
                k.dma('sp', O[0:n, :], ap, [tl], [O], O)
                k.op('act', lambda e: e.activation(out=junk[0:n, :], in_=O[0:n, :], func=AF.Square, accum_out=S[0:n, 0:1]), [O], [junk, S])
                k.op('dve', lambda e: e.tensor_scalar(out=S[0:n, 1:2], in0=S[0:n, 0:1], scalar1=1.0 / D, scalar2=eps, op0=ALU.mult, op1=ALU.add), [S], [S])
                k.op('act', lambda e: e.activation(out=S[0:n, 2:3], in_=S[0:n, 1:2], func=AF.Sqrt), [S], [S])
                k.op('dve', lambda e: e.reciprocal(out=S[0:n, 3:4], in_=S[0:n, 2:3]), [S], [S])
                k.op('dve', lambda e: e.scalar_tensor_tensor(out=O[0:n, :], in0=O[0:n, :], scalar=S[0:n, 3:4], in1=wB['post'][0:n, :], op0=ALU.mult, op1=ALU.mult), [O, S, wB['post']], [O])
                k.op('pool', lambda e: e.tensor_tensor(out=X[0:n, :], in0=X[0:n, :], in1=O[0:n, :], op=ALU.add), [X, O], [X])
            if x_dst:
                tl, ap = row_ap(x_dst, t0, n)
                k.dma('sp', ap, X[0:n, :], [X], [tl], X)
            if hT_dst:
                XN = xn[s]
                k.op('act', lambda e: e.activation(out=junk[0:n, :], in_=X[0:n, :], func=AF.Square, accum_out=S[0:n, 4:5]), [X], [junk, S])
                k.op('dve', lambda e: e.tensor_scalar(out=S[0:n, 5:6], in0=S[0:n, 4:5], scalar1=1.0 / D, scalar2=eps, op0=ALU.mult, op1=ALU.add), [S], [S])
                k.op('act', lambda e: e.activation(out=S[0:n, 6:7], in_=S[0:n, 5:6], func=AF.Sqrt), [S], [S])
                k.op('dve', lambda e: e.reciprocal(out=S[0:n, 7:8], in_=S[0:n, 6:7]), [S], [S])
                k.op('dve', lambda e: e.scalar_tensor_tensor(out=XN[0:n, :], in0=X[0:n, :], scalar=S[0:n, 7:8], in1=wB['pre'][0:n, :], op0=ALU.mult, op1=ALU.mult), [X, S, wB['pre']], [XN])
                H = hst[gi]
                tt = (t0 - g0)
                for half in range(KC // 16 if KC >= 16 else 1):
                    nk = min(16, KC)
                    P = pst[pcount % 2]; pcount += 1
                    for j in range(nk):
                        kc = half * 16 + j
                        k.op('pe', lambda e: e.transpose(out=P[:, j * 128:j * 128 + n], in_=XN[0:n, kc * 128:(kc + 1) * 128], identity=c.idb[0:n, 0:n]), [XN, c.idb], [P])
                    eng = 'act' if (pcount % 2) else 'dve'
                    src = P[:, 0:nk * 128].rearrange("p (k t) -> p k t", t=128)[:, :, 0:n]
                    dst = H[:, half * 16:half * 16 + nk, tt:tt + n]
                    if eng == 'act':
                        k.op('act', lambda e: e.copy(out=dst, in_=src), [P], [H])
                    else:
                        k.op('dve', lambda e: e.tensor_copy(out=dst, in_=src), [P], [H])
        if hT_dst:
            H = hst[gi]
            k.dma('sp', hT_dst.h.ap().rearrange("k p t -> p k t")[:, :, g0:g0 + gn], H[:, :, 0:gn], [H], [hT_dst], H)
    k.phase_end()

def gemm(k, c, name, XT, NTOK, KC, wsegs_blocks, wtile, orient, epilogue, TG=512, xbufs=2, wbufs=2, WMAX=512, xload=None):
    Wt = [k.sb(f'{name}_w{i}', [128, KC, WMAX], BF16) for i in range(wbufs)]
    Xt = [k.sb(f'{name}_x{i}', [128, KC, TG], BF16) for i in range(xbufs)]
    NPS = 8
    PS = [k.ps(f'{name}_ps{i}', [128, 512], F32) for i in range(NPS)]
    psi = 0
    groups = [(g0, min(TG, NTOK - g0)) for g0 in range(0, NTOK, TG)]
    single_x = (len(groups) <= xbufs)
    if getattr(c, 'limit_blocks', None):
        wsegs_blocks = wsegs_blocks[:c.limit_blocks]
    nb = len(wsegs_blocks)
    widths = {}
    def emit_W(bi):
        W = Wt[bi % wbufs]
        off = 0
        for (wtl, wap) in wsegs_blocks[bi]:
            wd = wap.shape[1]
            wv = wap.rearrange("(k p) c -> p k c", p=128)
            for k0 in range(0, KC, 16):
                k1 = min(KC, k0 + 16)
                k.dma('pool', W[:, k0:k1, off:off + wd], wv[:, k0:k1, :], [wtl], [W], W)
            off += wd
        widths[bi] = off
    xstate = {'n': 0, 'loaded': {}}
    xof = {}
    def emit_X(bi, gi):
        g0, gn = groups[gi]
        if single_x:
            if gi in xstate['loaded']:
                xof[(bi, gi)] = xstate['loaded'][gi]; return
        X = Xt[xstate['n'] % xbufs]; xstate['n'] += 1
        xv = XT.h.ap().rearrange("k p t -> p k t") if xload is None else None
        for k0 in range(0, KC, 16):
            k1 = min(KC, k0 + 16)
            src = xv[:, k0:k1, g0:g0 + gn] if xload is None else xload(g0, gn, k0, k1)
            k.dma('sp', X[:, k0:k1, 0:gn], src, [XT], [X], X)
        if single_x: xstate['loaded'][gi] = X
        xof[(bi, gi)] = X
    items = [(bi, gi) for bi in range(nb) for gi in range(len(groups))]
    emit_W(0)
    emit_X(*items[0])
    for idx, (bi, gi) in enumerate(items):
        if gi == 0 and bi + 1 < nb:
            emit_W(bi + 1)
        W = Wt[bi % wbufs]; width = widths[bi]
        X = xof[(bi, gi)]
        g0, gn = groups[gi]
        outs = []
        if orient == 'f':
            for c0 in range(0, width, 128):
                m = min(128, width - c0)
                P = PS[psi % NPS]; psi += 1
                for kc in range(KC):
                    k.op('pe', lambda e: e.matmul(P[0:m, 0:gn], lhsT=W[:, kc, c0:c0 + m], rhs=X[:, kc, 0:gn], start=(kc == 0), stop=(kc == KC - 1)), [W, X], [P])
                outs.append((P, m, gn, c0))
        else:
            for t0 in range(0, gn, 128):
                m = min(128, gn - t0)
                P = PS[psi % NPS]; psi += 1
                for kc in range(KC):
                    k.op('pe', lambda e: e.matmul(P[0:m, 0:width], lhsT=X[:, kc, t0:t0 + m], rhs=W[:, kc, 0:width], start=(kc == 0), stop=(kc == KC - 1)), [W, X], [P])
                outs.append((P, m, width, g0 + t0))
        if idx + 1 < len(items):
            emit_X(*items[idx + 1])
        epilogue(bi, g0, gn, outs)
D_MODEL = 4096; SEQ = 2048; DEPTH = 2; NB_S = 2; DEC_SEQ = 8
NTOK = SEQ + NB_S * DEC_SEQ
GROUP_W = 1024
N_IN = 13120; D_FF = 11008
OFF_GDN_QKV = 0; OFF_GDN_Z = 3072; OFF_GDN_B = 4096; OFF_GDN_A = 4104; OFF_RWKV = 4112
OFF_SSM_Z = 7472; OFF_SSM_XBC = 8496; OFF_SSM_DT = 10032; OFF_SWA = 10048
RWKV_PROJ = 3360; SSM_XBC = 1536

def rows_to_fm(k, c, dst, rows, NF, name):
    R = len(rows)
    Rt = k.sb(f'{name}_rt', [R, NF * 128], F32)
    for r, (tl, ap) in enumerate(rows):
        k.dma('sp', Rt[r:r + 1, :], ap.unsqueeze(0), [tl], [Rt], Rt)
    per = 512 // R
    for f0 in range(0, NF, per):
        nf = min(per, NF - f0)
        P = k.ps(f'{name}_p{f0}', [128, 512], F32)
        for f in range(nf):
            k.op('pe', lambda e: e.transpose(out=P[:, f * R:(f + 1) * R], in_=Rt[0:R, (f0 + f) * 128:(f0 + f + 1) * 128], identity=c.idf[0:R, 0:R]), [Rt, c.idf], [P])
        k.op('dve', lambda e: e.tensor_copy(out=dst[:, f0:f0 + nf, :], in_=P[:, 0:nf * R].rearrange("p (f r) -> p f r", r=R)), [P], [dst])

def fm_to_rows(k, c, src, R, NF, outs, name):
    Rt = k.sb(f'{name}_rt', [R, NF * 128], F32)
    PP = [k.ps(f'{name}_p{i}', [128, 512], F32) for i in range(4)]
    for f0 in range(0, NF, 4):
        nf = min(4, NF - f0)
        P = PP[(f0 // 4) % 4]
        for f in range(nf):
            k.op('pe', lambda e: e.transpose(out=P[0:R, f * 128:(f + 1) * 128], in_=src[:, f0 + f, :], identity=c.idf[:, :]), [src, c.idf], [P])
        k.op('dve', lambda e: e.tensor_copy(out=Rt[0:R, f0 * 128:(f0 + nf) * 128], in_=P[0:R, 0:nf * 128]), [P], [Rt])
    for r, (tl, ap) in enumerate(outs):
        k.dma('sp', ap.unsqueeze(0), Rt[r:r + 1, :], [Rt], [tl], Rt)

class StopBuild(Exception):
    pass

def build_program(nc, es, mixers=True, limit_blocks=None, depth=DEPTH, stop=None):
    k = KB(nc, es); c = Ctx(); c.limit_blocks = limit_blocks
    try:
        _build_body(k, c, mixers, depth, stop)
    except StopBuild:
        pass
    k.barrier()
    k.c = c
    return k

def _build_body(k, c, mixers, depth, stop):
    f = F32
    IN = {}
    def inp(name, shape):
        IN[name] = k.dram(name, shape, f, kind="ExternalInput"); return IN[name]
    OUT = {}
    def outp(name, shape):
        OUT[name] = k.dram(name, shape, f, kind="ExternalOutput"); return OUT[name]
    L = DEPTH
    xin = inp('xin', [NTOK, D_MODEL])
    inp('st_gdn', [L, NB_S, 8, 128, 128]); inp('st_gdn_conv', [L, NB_S, 3, 3072]); inp('st_rwkv', [L, NB_S, 16, 64, 64])
    inp('st_rwkv_shift', [L, NB_S, 3360]); inp('st_ssm', [L, NB_S, 16, 64, 128]); inp('st_ssm_conv', [L, NB_S, 3, 1536])
    inp('c_swa_k', [L, NB_S, 2048, 1024]); inp('c_swa_v', [L, NB_S, 2048, 1024]); inp('st_ffn_conv', [L, NB_S, 2, D_FF])
    for nm in ['norm_mix_pre', 'norm_mix_post', 'norm_ffn_pre', 'norm_ffn_post']:
        inp(nm, [L, D_MODEL])
    inp('w_in', [L, D_MODEL, N_IN]); inp('w_out', [L, D_MODEL, D_MODEL])
    inp('gdn_conv_w', [L, 4, 3072]); inp('gdn_A_log', [L, 8]); inp('gdn_dt_bias', [L, 8]); inp('gdn_norm_w', [L, 128])
    inp('rwkv_mu', [L, 3360]); inp('rwkv_w0', [L, 1024]); inp('rwkv_w2', [L, 64, 1024]); inp('rwkv_a0', [L, 1024])
    inp('rwkv_a2', [L, 64, 1024]); inp('rwkv_g2', [L, 160, 1024]); inp('rwkv_k_k', [L, 1024]); inp('rwkv_k_a', [L, 1024])
    inp('rwkv_r_k', [L, 1024]); inp('rwkv_ln_w', [L, 1024]); inp('rwkv_ln_b', [L, 1024])
    inp('ssm_conv_w', [L, 4, 1536]); inp('ssm_conv_b', [L, 1536]); inp('ssm_dt_bias', [L, 16]); inp('ssm_A_log', [L, 16])
    inp('ssm_D', [L, 16]); inp('ssm_norm_w', [L, 1024])
    inp('ffn_w_up', [L, D_MODEL, 2 * D_FF]); inp('ffn_conv_w', [L, 3, D_FF]); inp('ffn_conv_b', [L, D_FF]); inp('ffn_w_down', [L, D_FF, D_MODEL])
    y = outp('y', [NTOK, D_MODEL])
    outp('p_gdn', [L, 8, 128, 128]); outp('p_gdn_conv', [L, 3, 3072]); outp('p_rwkv', [L, 16, 64, 64]); outp('p_rwkv_shift', [L, 3360])
    outp('p_ssm', [L, 16, 64, 128]); outp('p_ssm_conv', [L, 3, 1536]); outp('p_swa_k', [L, 2048, 1024]); outp('p_swa_v', [L, 2048, 1024])
    outp('p_ffn_conv', [L, 2, D_FF])
    outp('s_gdn', [L, NB_S, 8, 128, 128]); outp('s_gdn_conv', [L, NB_S, 3, 3072]); outp('s_rwkv', [L, NB_S, 16, 64, 64]); outp('s_rwkv_shift', [L, NB_S, 3360])
    outp('s_ssm', [L, NB_S, 16, 64, 128]); outp('s_ssm_conv', [L, NB_S, 3, 1536]); outp('s_swa_k', [L, NB_S, 8, 1024]); outp('s_swa_v', [L, NB_S, 8, 1024])
    outp('s_ffn_conv', [L, NB_S, 2, D_FF])
    c.IN = IN; c.OUT = OUT
    c.idf, c.idb = make_ident(k)
    if mixers:
        make_consts(k, c)
        make_consts2(k, c)
    KC = D_MODEL // 128; KF = D_FF // 128
    xcur = xin
    for li in range(depth):
        hT = k.dram(f'hT{li}', [KC, 128, NTOK], BF16) if li == 0 else None
        projT = k.dram(f'projT{li}', [N_IN, NTOK], F32)
        mixT = k.dram(f'mixT{li}', [KC, 128, NTOK], BF16)
        o1 = k.dram(f'o1_{li}', [NTOK, D_MODEL], F32)
        x1 = k.dram(f'x1_{li}', [NTOK, D_MODEL], F32)
        h2T = k.dram(f'h2T{li}', [KC, 128, NTOK], BF16)
        actT = k.dram(f'actT{li}', [(NTOK + 127) // 128, 128, KF, 128], BF16)
        o2 = k.dram(f'o2_{li}', [NTOK, D_MODEL], F32)
        if li == 0:
            norm_phase(k, c, f'n1_{li}', NTOK, D_MODEL, [(xcur, xcur.h.ap(), NTOK)], None, None,
                       (IN['norm_mix_pre'], IN['norm_mix_pre'].h.ap()[li]), None, hT)
        else:
            hT = c.next_hT
        k.phase_begin()
        stg = [k.sb(f'g1stg{i}', [128, 512], F32) for i in range(4)]
        cnt = [0]
        w_in = IN['w_in']
        blocks = []; bstart = []
        for c0 in range(0, N_IN, 512):
            wd = min(512, N_IN - c0)
            blocks.append([(w_in, w_in.h.ap()[li, :, c0:c0 + wd])]); bstart.append(c0)
        def epi1(bi, g0, gn, outs):
            for (P, m, n, c0) in outs:
                S = stg[cnt[0] % 4]; cnt[0] += 1
                if cnt[0] % 2:
                    k.op('act', lambda e: e.copy(out=S[0:m, 0:n], in_=P[0:m, 0:n]), [P], [S])
                else:
                    k.op('dve', lambda e: e.tensor_copy(out=S[0:m, 0:n], in_=P[0:m, 0:n]), [P], [S])
                col = bstart[bi] + c0
                k.dma('sp', projT.h.ap()[col:col + m, g0:g0 + n], S[0:m, 0:n], [S], [projT], S)
        gemm(k, c, f'g1_{li}', hT, NTOK, KC, blocks, None, 'f', epi1)
        k.phase_end()
        if stop == 'g1': raise StopBuild()
        if mixers:
            run_mixers(k, c, li, projT, mixT)
        else:
            k.phase_begin()
            Z = k.sb('zeros', [128, KC, 512], BF16)
            k.op('pool', lambda e: e.memset(Z[:], 0.0), [], [Z])
            for g0 in range(0, NTOK, 512):
                gn = min(512, NTOK - g0)
                k.dma('sp', mixT.h.ap().rearrange("k p t -> p k t")[:, :, g0:g0 + gn], Z[:, :, 0:gn], [Z], [mixT], Z)
            k.phase_end()
        k.phase_begin()
        stg = [k.sb(f'g2stg{i}', [128, 512], F32) for i in range(4)]
        w_out = IN['w_out']
        blocks = [[(w_out, w_out.h.ap()[li, :, c0:c0 + 512])] for c0 in range(0, D_MODEL, 512)]
        def epi2(bi, g0, gn, outs):
            for (P, m, n, t0) in outs:
                S = stg[cnt[0] % 4]; cnt[0] += 1
                if cnt[0] % 2:
                    k.op('act', lambda e: e.copy(out=S[0:m, 0:n], in_=P[0:m, 0:n]), [P], [S])
                else:
                    k.op('dve', lambda e: e.tensor_copy(out=S[0:m, 0:n], in_=P[0:m, 0:n]), [P], [S])
                k.dma('sp', o1.h.ap()[t0:t0 + m, bi * 512:bi * 512 + n], S[0:m, 0:n], [S], [o1], S)
        gemm(k, c, f'g2_{li}', mixT, NTOK, KC, blocks, None, 't', epi2)
        k.phase_end()
        if stop == 'g2': raise StopBuild()
        norm_phase(k, c, f'n2_{li}', NTOK, D_MODEL, [(xcur, xcur.h.ap(), NTOK)], [(o1, o1.h.ap(), NTOK)],
                   (IN['norm_mix_post'], IN['norm_mix_post'].h.ap()[li]), (IN['norm_ffn_pre'], IN['norm_ffn_pre'].h.ap()[li]),
                   [(x1, x1.h.ap(), NTOK)], h2T)
        k.scope_begin()
        cw = k.sb(f'cw{li}', [128, KF, 4], F32, persist=True)
        fst = k.sb(f'fst{li}', [128, KF, 4], F32, persist=True)
        fco = k.sb(f'fco{li}', [128, KF, 6], F32, persist=True)
        k.phase_begin()
        fw = IN['ffn_conv_w']; fb = IN['ffn_conv_b']; fs = IN['st_ffn_conv']
        rows_to_fm(k, c, cw, [(fw, fw.h.ap()[li, 0]), (fw, fw.h.ap()[li, 1]), (fw, fw.h.ap()[li, 2]), (fb, fb.h.ap()[li])], KF, f'cwl{li}')
        k.phase_end()
        k.phase_begin()
        rows_to_fm(k, c, fst, [(fs, fs.h.ap()[li, 0, 0]), (fs, fs.h.ap()[li, 0, 1]), (fs, fs.h.ap()[li, 1, 0]), (fs, fs.h.ap()[li, 1, 1])], KF, f'fstl{li}')
        k.phase_end()
        k.phase_begin()
        Gt = [k.sb(f'g3G{j}', [128, 516], F32) for j in range(2)]
        At = [k.sb(f'g3A{j}', [128, 512], F32) for j in range(2)]
        ATs = [k.sb(f'g3AT{i}', [128, 2, 512], BF16) for i in range(2)]
        wu = IN['ffn_w_up']
        blocks = [[(wu, wu.h.ap()[li, :, j * 256:(j + 1) * 256]), (wu, wu.h.ap()[li, :, D_FF + j * 256:D_FF + (j + 1) * 256])] for j in range(KF // 2)]
        acnt = [0]
        def epi3(bi, g0, gn, outs):
            AT = ATs[acnt[0] % 2]; acnt[0] += 1
            for j in range(2):
                ft = bi * 2 + j
                Pg = outs[j][0]; Pv = outs[2 + j][0]
                G = Gt[j]; A = At[j]
                w0 = cw[:, ft, 0:1]; w1 = cw[:, ft, 1:2]; w2 = cw[:, ft, 2:3]; bb = cw[:, ft, 3:4]
                if gn == 512:
                    if g0 == 0:
                        k.op('pool', lambda e: e.memset(G[:, 0:2], 0.0), [], [G])
                    else:
                        k.op('pool', lambda e: e.tensor_copy(out=G[:, 0:2], in_=G[:, 512:514]), [G], [G])
                    k.op('act', lambda e: e.copy(out=G[:, 2:2 + gn], in_=Pg[:, 0:gn]), [Pg], [G])
                    nn = gn
                    if g0 + gn == SEQ:
                        k.op('pool', lambda e: e.tensor_copy(out=fco[:, ft, 0:2], in_=G[:, 512:514]), [G], [fco])
                else:
                    k.op('pool', lambda e: e.tensor_copy(out=G[:, 0:2], in_=fst[:, ft, 0:2]), [fst], [G])
                    k.op('pool', lambda e: e.tensor_copy(out=G[:, 10:12], in_=fst[:, ft, 2:4]), [fst], [G])
                    k.op('act', lambda e: e.copy(out=G[:, 2:10], in_=Pg[:, 0:8]), [Pg], [G])
                    k.op('act', lambda e: e.copy(out=G[:, 12:20], in_=Pg[:, 8:16]), [Pg], [G])
                    k.op('pool', lambda e: e.tensor_copy(out=fco[:, ft, 2:4], in_=G[:, 8:10]), [G], [fco])
                    k.op('pool', lambda e: e.tensor_copy(out=fco[:, ft, 4:6], in_=G[:, 18:20]), [G], [fco])
                    nn = 18
                k.op('dve', lambda e: e.tensor_scalar(out=A[:, 0:nn], in0=G[:, 2:2 + nn], scalar1=w2, scalar2=bb, op0=ALU.mult, op1=ALU.add), [G, cw], [A])
                k.op('dve', lambda e: e.scalar_tensor_tensor(out=A[:, 0:nn], in0=G[:, 1:1 + nn], scalar=w1, in1=A[:, 0:nn], op0=ALU.mult, op1=ALU.add), [G, cw, A], [A])
                k.op('dve', lambda e: e.scalar_tensor_tensor(out=A[:, 0:nn], in0=G[:, 0:nn], scalar=w0, in1=A[:, 0:nn], op0=ALU.mult, op1=ALU.add), [G, cw, A], [A])
                k.op('act', lambda e: e.activation(out=A[:, 0:nn], in_=A[:, 0:nn], func=AF.Silu), [A], [A])
                if gn == 512:
                    k.op('dve', lambda e: e.tensor_tensor(out=AT[:, j, 0:gn], in0=A[:, 0:gn], in1=Pv[:, 0:gn], op=ALU.mult), [A, Pv], [AT])
                else:
                    k.op('dve', lambda e: e.tensor_tensor(out=AT[:, j, 0:8], in0=A[:, 0:8], in1=Pv[:, 0:8], op=ALU.mult), [A, Pv], [AT])
                    k.op('dve', lambda e: e.tensor_tensor(out=AT[:, j, 8:16], in0=A[:, 10:18], in1=Pv[:, 8:16], op=ALU.mult), [A, Pv], [AT])
            if gn == 512:
                tt0 = g0 // 128
                for j in range(2):
                    k.dma('sp', actT.h.ap()[tt0:tt0 + 4, :, bi * 2 + j, :].rearrange("t p s -> p t s"),
                          AT[:, j, 0:512].rearrange("p (t s) -> p t s", s=128), [AT], [actT], AT)
            else:
                k.dma('sp', actT.h.ap()[g0 // 128, :, bi * 2:bi * 2 + 2, 0:gn], AT[:, :, 0:gn], [AT], [actT], AT)
        gemm(k, c, f'g3_{li}', h2T, NTOK, KC, blocks, None, 'f', epi3)
        k.phase_end()
        k.phase_begin()
        po = OUT['p_ffn_conv']; so = OUT['s_ffn_conv']
        fm_to_rows(k, c, fco, 6, KF, [(po, po.h.ap()[li, 0]), (po, po.h.ap()[li, 1]), (so, so.h.ap()[li, 0, 0]), (so, so.h.ap()[li, 0, 1]),
                                      (so, so.h.ap()[li, 1, 0]), (so, so.h.ap()[li, 1, 1])], f'fco{li}')
        k.phase_end()
        k.scope_end()
        k.phase_begin()
        stg = [k.sb(f'g4stg{i}', [128, 256], F32) for i in range(4)]
        wd_ = IN['ffn_w_down']
        blocks = [[(wd_, wd_.h.ap()[li, :, c0:c0 + 256])] for c0 in range(0, D_MODEL, 256)]
        def epi4(bi, g0, gn, outs):
            for (P, m, n, t0) in outs:
                S = stg[cnt[0] % 4]; cnt[0] += 1
                if cnt[0] % 2:
                    k.op('act', lambda e: e.copy(out=S[0:m, 0:n], in_=P[0:m, 0:n]), [P], [S])
                else:
                    k.op('dve', lambda e: e.tensor_copy(out=S[0:m, 0:n], in_=P[0:m, 0:n]), [P], [S])
                k.dma('sp', o2.h.ap()[t0:t0 + m, bi * 256:bi * 256 + n], S[0:m, 0:n], [S], [o2], S)
        gemm(k, c, f'g4_{li}', actT, NTOK, KF, blocks, None, 't', epi4, TG=128, WMAX=256,
             xload=lambda g0, gn, k0, k1: actT.h.ap()[g0 // 128, :, k0:k1, 0:gn])
        k.phase_end()
        if li + 1 < depth:
            xn_ = k.dram(f'x2_{li}', [NTOK, D_MODEL], F32)
            hTn = k.dram(f'hT{li + 1}', [KC, 128, NTOK], BF16)
            norm_phase(k, c, f'n3_{li}', NTOK, D_MODEL, [(x1, x1.h.ap(), NTOK)], [(o2, o2.h.ap(), NTOK)],
                       (IN['norm_ffn_post'], IN['norm_ffn_post'].h.ap()[li]), (IN['norm_mix_pre'], IN['norm_mix_pre'].h.ap()[li + 1]),
                       [(xn_, xn_.h.ap(), NTOK)], hTn)
            c.next_hT = hTn
            xcur = xn_
        else:
            norm_phase(k, c, f'n3_{li}', NTOK, D_MODEL, [(x1, x1.h.ap(), NTOK)], [(o2, o2.h.ap(), NTOK)],
                       (IN['norm_ffn_post'], IN['norm_ffn_post'].h.ap()[li]), None, [(y, y.h.ap(), NTOK)], None)

    IN, OUT = c.IN, c.OUT
    def I(nm): return (IN[nm], IN[nm].h.ap()[li])
    def O(nm): return (OUT[nm], OUT[nm].h.ap()[li])
    k.scope_begin()
    gdn_phase(k, c, li, projT, mixT, SEQ, NB_S, IN, O('p_gdn'), O('p_gdn_conv'), O('s_gdn'), O('s_gdn_conv'), I('st_gdn'), I('st_gdn_conv'),
              OFF_GDN_QKV, OFF_GDN_Z, OFF_GDN_B, kc0=0)
    k.scope_end()
    k.scope_begin()
    rwkv_phase(k, c, li, projT, mixT, SEQ, NB_S, IN, O('p_rwkv'), O('p_rwkv_shift'), O('s_rwkv'), O('s_rwkv_shift'), I('st_rwkv'), I('st_rwkv_shift'),
               OFF_RWKV, kc0=8)
    k.scope_end()
    k.scope_begin()
    ssd_phase(k, c, li, projT, mixT, SEQ, NB_S, IN, O('p_ssm'), O('p_ssm_conv'), O('s_ssm'), O('s_ssm_conv'), I('st_ssm'), I('st_ssm_conv'),
              OFF_SSM_Z, OFF_SSM_XBC, OFF_SSM_DT, kc0=16)
    k.scope_end()
    k.scope_begin()
    swa_phase(k, c, li, projT, mixT, SEQ, NB_S, OFF_SWA, O('p_swa_k'), O('p_swa_v'), O('s_swa_k'), O('s_swa_v'), I('c_swa_k'), I('c_swa_v'), kc0=24)
    k.scope_end()

def make_consts(k, c):
    NG = 128 * 20 + 512
    G = k.sb('swa_G', [128, NG], BF16, persist=True)
    R = k.sb('swa_R', [128, 512], F32, persist=True)
    k.phase_begin()
    d = k.sb('swa_d', [128, NG], F32)
    t1 = k.sb('swa_t1', [128, NG], F32)
    t2 = k.sb('swa_t2', [128, NG], F32)
    acc = k.sb('swa_acc', [128, NG], F32)
    di = k.sb('swa_di', [128, NG], I32)
    di2 = k.sb('swa_di2', [128, NG], I32)
    k.op('pool', lambda e: e.iota(di[:], pattern=[[1, NG]], base=-384, channel_multiplier=-1), [], [di])
    k.op('dve', lambda e: e.tensor_copy(out=d[:], in_=di[:]), [di], [d])
    first = True
    for (win, dil) in ((128, 1), (512, 4), (2048, 16)):
        k.op('dve', lambda e: e.tensor_scalar(out=t1[:], in0=d[:], scalar1=0.0, scalar2=None, op0=ALU.is_ge), [d], [t1])
        k.op('dve', lambda e: e.tensor_scalar(out=t2[:], in0=d[:], scalar1=float(win), scalar2=None, op0=ALU.is_le), [d], [t2])
        k.op('dve', lambda e: e.tensor_tensor(out=t1[:], in0=t1[:], in1=t2[:], op=ALU.mult), [t1, t2], [t1])
        if dil > 1:
            k.op('dve', lambda e: e.tensor_single_scalar(out=di2[:], in_=di[:], scalar=dil - 1, op=ALU.bitwise_and), [di], [di2])
            k.op('dve', lambda e: e.tensor_scalar(out=t2[:], in0=di2[:], scalar1=0.0, scalar2=None, op0=ALU.is_equal), [di2], [t2])
            k.op('dve', lambda e: e.tensor_tensor(out=t1[:], in0=t1[:], in1=t2[:], op=ALU.mult), [t1, t2], [t1])
        if first:
            k.op('dve', lambda e: e.tensor_copy(out=acc[:], in_=t1[:]), [t1], [acc]); first = False
        else:
            k.op('dve', lambda e: e.tensor_tensor(out=acc[:], in0=acc[:], in1=t1[:], op=ALU.add), [acc, t1], [acc])
    k.op('dve', lambda e: e.tensor_copy(out=G[:], in_=acc[:]), [acc], [G])
    ri = k.sb('swa_ri', [128, 512], I32)
    k.op('pool', lambda e: e.iota(ri[:], pattern=[[-1, 512]], base=0, channel_multiplier=1), [], [ri])
    k.op('dve', lambda e: e.tensor_copy(out=R[:], in_=ri[:]), [ri], [R])
    k.phase_end()
    c.swa_G = G; c.swa_R = R

def swa_phase(k, c, li, projT, mixT, T, NS, off_q, pk, pv, sk, sv, ck, cv, kc0=24):
    G = c.swa_G; R = c.swa_R
    nt = T // 128
    QS = 128 ** -0.5
    k.phase_begin()
    NTOKL = T + NS * 8
    qf = k.sb('swa_qf', [128, NTOKL], F32)
    kf = k.sb('swa_kf', [128, NTOKL], F32)
    vf = k.sb('swa_vf', [128, NTOKL], F32)
    qb = k.sb('swa_qb', [128, NTOKL], BF16)
    kb = k.sb('swa_kb', [128, NTOKL + 2048], BF16)
    va = k.sb('swa_va', [128, nt + 17, 130], BF16)
    tok = [k.sb(f'swa_tok{i}', [128, 128], F32) for i in range(4)]
    tmp = [k.sb(f'swa_tmp{i}', [128, 512], F32) for i in range(2)]
    pe_ = [k.sb(f'swa_pe{i}', [128, 512], BF16) for i in range(2)]
    pm = [k.sb(f'swa_pm{i}', [128, 512], BF16) for i in range(2)]
    osb = [k.sb(f'swa_o{i}', [128, 132], F32) for i in range(2)]
    ost = [k.sb(f'swa_ost{i}', [128, 512], BF16) for i in range(2)]
    cst = k.sb('swa_cst', [128, 16, 128], F32)
    PS_S = [k.ps(f'swa_pss{i}', [128, 512], F32) for i in range(2)]
    PS_O = [k.ps(f'swa_pso{i}', [128, 512], F32) for i in range(4)]
    PS_T = [k.ps(f'swa_pst{i}', [128, 512], F32) for i in range(2)]
    cnt = {'s': 0, 't': 0, 'tok': 0, 'o': 0, 'ost': 0}
    k.op('pool', lambda e: e.memset(va[:], 1.0), [], [va])
    def tr_to_tok(src_ap, n):
        P = PS_T[cnt['t'] % 2]; cnt['t'] += 1
        k.op('pe', lambda e: e.transpose(out=P[0:n, 0:128], in_=src_ap, identity=c.idf[:, :]), [qf, kf, vf, c.idf], [P])
        S = tok[cnt['tok'] % 4]; cnt['tok'] += 1
        k.op('act', lambda e: e.copy(out=S[0:n, :], in_=P[0:n, 0:128]), [P], [S])
        return S
    def attend(h, q_cols, nq, q0pos, key_tiles, out_cb):
        slope = 2.0 ** (-8.0 * (h + 1) / 8)
        nqt = (nq + 127) // 128
        POs = [PS_O[i] for i in range(nqt)]
        started = [False] * nqt
        for ki, (kc, nk, k0pos, vi) in enumerate(key_tiles):
            PSs = PS_S[cnt['s'] % 2]; TM = tmp[cnt['s'] % 2]; PE_ = pe_[cnt['s'] % 2]; PM = pm[cnt['s'] % 2]; cnt['s'] += 1
            k.op('pe', lambda e: e.matmul(PSs[0:nk, 0:nq], lhsT=kb[:, kc:kc + nk], rhs=qb[:, q_cols:q_cols + nq], start=True, stop=True), [kb, qb], [PSs])
            k.op('dve', lambda e: e.scalar_tensor_tensor(out=TM[0:nk, 0:nq], in0=R[0:nk, 0:nq], scalar=slope, in1=PSs[0:nk, 0:nq], op0=ALU.mult, op1=ALU.add), [R, PSs], [TM])
            k.op('act', lambda e: e.activation(out=PE_[0:nk, 0:nq], in_=TM[0:nk, 0:nq], func=AF.Exp, bias=float(-slope * (q0pos - k0pos)), scale=1.0), [TM], [PE_])
            dl = (q0pos - k0pos) // 128
            assert (q0pos - k0pos) % 128 == 0 and -3 <= dl <= 16
            g0 = 128 * (dl + 3)
            k.op('pool', lambda e: e.tensor_tensor(out=PM[0:nk, 0:nq], in0=PE_[0:nk, 0:nq], in1=G[0:nk, g0:g0 + nq], op=ALU.mult), [PE_, G], [PM])
            for qt in range(nqt):
                n = min(128, nq - qt * 128)
                if k0pos > q0pos + qt * 128 + n - 1:
                    continue
                last = True
                for (kc2, nk2, k0pos2, vi2) in key_tiles[ki + 1:]:
                    if k0pos2 <= q0pos + qt * 128 + n - 1:
                        last = False; break
                k.op('pe', lambda e: e.matmul(POs[qt][0:n, 0:129], lhsT=PM[0:nk, qt * 128:qt * 128 + n], rhs=va[0:nk, vi, 0:129], start=(not started[qt]), stop=last), [PM, va], [POs[qt]])
                started[qt] = True
        for qt in range(nqt):
            n = min(128, nq - qt * 128)
            O = osb[cnt['o'] % 2]; cnt['o'] += 1
            k.op('dve', lambda e: e.reciprocal(out=O[0:n, 130:131], in_=POs[qt][0:n, 128:129]), [POs[qt]], [O])
            k.op('dve', lambda e: e.tensor_scalar(out=O[0:n, 0:128], in0=POs[qt][0:n, 0:128], scalar1=O[0:n, 130:131], scalar2=None, op0=ALU.mult), [POs[qt], O], [O])
            out_cb(qt, n, O)
    for h in range(8):
        for (dst, off) in ((qf, off_q + h * 128), (kf, off_q + 1024 + h * 128), (vf, off_q + 2048 + h * 128)):
            k.dma('sp', dst[:], projT.h.ap()[off:off + 128, :], [projT], [dst], dst)
        k.op('act', lambda e: e.activation(out=qb[:], in_=qf[:], func=AF.Copy, scale=QS), [qf], [qb])
        k.op('dve', lambda e: e.tensor_copy(out=kb[:, 0:NTOKL], in_=kf[:]), [kf], [kb])
        for t in range(nt):
            S = tr_to_tok(kf[:, t * 128:(t + 1) * 128], 128)
            k.dma('sp', pk[1][t * 128:(t + 1) * 128, h * 128:(h + 1) * 128], S[:, :], [S], [pk[0]], S)
            S = tr_to_tok(vf[:, t * 128:(t + 1) * 128], 128)
            k.dma('sp', pv[1][t * 128:(t + 1) * 128, h * 128:(h + 1) * 128], S[:, :], [S], [pv[0]], S)
            k.op('pool', lambda e: e.tensor_copy(out=va[:, t, 0:128], in_=S[:, :]), [S], [va])
        for qblk in range(0, T, 512):
            nq = min(512, T - qblk)
            kts = [(kt * 128, 128, kt * 128, kt) for kt in range((qblk + nq) // 128)]
            OST = ost[cnt['ost'] % 2]; cnt['ost'] += 1
            def ocb(qt, n, O, OST=OST):
                P = PS_T[cnt['t'] % 2]; cnt['t'] += 1
                k.op('pe', lambda e: e.transpose(out=P[:, 0:n], in_=O[0:n, 0:128], identity=c.idf[0:n, 0:n]), [O, c.idf], [P])
                k.op('act', lambda e: e.copy(out=OST[:, qt * 128:qt * 128 + n], in_=P[:, 0:n]), [P], [OST])
            attend(h, qblk, nq, qblk, kts, ocb)
            k.dma('sp', mixT.h.ap()[kc0 + h, :, qblk:qblk + nq], OST[:, 0:nq], [OST], [mixT], OST)
        for b in range(NS):
            col = T + b * 8
            S = tr_to_tok(kf[:, col:col + 8], 8)
            k.dma('sp', sk[1][b, :, h * 128:(h + 1) * 128], S[0:8, :], [S], [sk[0]], S)
            S = tr_to_tok(vf[:, col:col + 8], 8)
            k.dma('sp', sv[1][b, :, h * 128:(h + 1) * 128], S[0:8, :], [S], [sv[0]], S)
            k.op('pool', lambda e: e.tensor_copy(out=va[0:8, nt + 16, 0:128], in_=S[0:8, :]), [S], [va])
            k.dma('sp', cst[:], ck[1][b, :, h * 128:(h + 1) * 128].rearrange("(t p) e -> p t e", p=128), [ck[0]], [cst], cst)
            for t in range(16):
                P = PS_T[cnt['t'] % 2]; cnt['t'] += 1
                k.op('pe', lambda e: e.transpose(out=P[:, 0:128], in_=cst[:, t, :], identity=c.idf[:, :]), [cst, c.idf], [P])
                k.op('act', lambda e: e.copy(out=kb[:, NTOKL + t * 128:NTOKL + (t + 1) * 128], in_=P[:, 0:128]), [P], [kb])
            k.dma('sp', cst[:], cv[1][b, :, h * 128:(h + 1) * 128].rearrange("(t p) e -> p t e", p=128), [cv[0]], [cst], cst)
            k.op('dve', lambda e: e.tensor_copy(out=va[:, nt:nt + 16, 0:128], in_=cst[:]), [cst], [va])
            kts = [(NTOKL + t * 128, 128, t * 128, nt + t) for t in range(16)] + [(col, 8, 2048, nt + 16)]
            OST = ost[cnt['ost'] % 2]; cnt['ost'] += 1
            def ocb2(qt, n, O, OST=OST):
                P = PS_T[cnt['t'] % 2]; cnt['t'] += 1
                k.op('pe', lambda e: e.transpose(out=P[:, 0:n], in_=O[0:n, 0:128], identity=c.idf[0:n, 0:n]), [O, c.idf], [P])
                k.op('act', lambda e: e.copy(out=OST[:, 0:n], in_=P[:, 0:n]), [P], [OST])
            attend(h, col, 8, 2048, kts, ocb2)
            k.dma('sp', mixT.h.ap()[kc0 + h, :, col:col + 8], OST[:, 0:8], [OST], [mixT], OST)
    k.phase_end()

def make_consts2(k, c):
    U = k.sb('c_utri', [128, 128], F32, persist=True)
    MN = k.sb('c_mneg', [128, 128], F32, persist=True)
    ONES = k.sb('c_ones', [128, 128], F32, persist=True)
    SEL = {}
    k.op('pool', lambda e: e.memset(ONES[:], 1.0), [], [ONES])
    k.op('pool', lambda e: e.memset(U[:], 1.0), [], [U])
    k.op('pool', lambda e: e.affine_select(out=U[:], in_=U[:], pattern=[[1, 128]], compare_op=ALU.is_ge, fill=0.0, base=0, channel_multiplier=-1), [U], [U])
    k.op('pool', lambda e: e.memset(MN[:], 0.0), [], [MN])
    k.op('pool', lambda e: e.affine_select(out=MN[:], in_=MN[:], pattern=[[1, 128]], compare_op=ALU.is_ge, fill=-30000.0, base=0, channel_multiplier=-1), [MN], [MN])
    for n in (128, 8):
        S = k.sb(f'c_sel{n}', [128, 128], F32, persist=True)
        k.op('pool', lambda e: e.memset(S[:], 1.0), [], [S])
        k.op('pool', lambda e: e.affine_select(out=S[:], in_=S[:], pattern=[[0, 128]], compare_op=ALU.is_equal, fill=0.0, base=-(n - 1), channel_multiplier=1), [S], [S])
        SEL[n] = S
    eps6 = k.sb('c_eps6', [128, 1], F32, persist=True)
    k.op('pool', lambda e: e.memset(eps6[:], 1e-6), [], [eps6])
    c.eps6 = eps6
    epsgn = k.sb('c_epsgn', [128, 1], F32, persist=True)
    k.op('pool', lambda e: e.memset(epsgn[:], 64e-5), [], [epsgn])
    c.epsgn = epsgn
    c.U = U; c.MN = MN; c.ONES = ONES; c.SEL = SEL

def bc_row(k, name, tl, ap_row, ncols):
    t = k.sb(name, [128, ncols], F32)
    k.dma('sp', t[:], ap_row.partition_broadcast(128), [tl], [t], t)
    return t

def ssd_phase(k, c, li, projT, mixT, T, NS, IN, pz_out, pconv_out, sz_out, sconv_out, st_z, st_conv, off_z, off_xbc, off_dt, kc0=16):
    Z = k.sb('ssd_Z', [128, 1024], F32, persist=True)
    cwS = k.sb('ssd_cw', [128, 12, 5], F32, persist=True)
    scv = k.sb('ssd_scv', [128, 12, 3 * NS], F32, persist=True)
    cvo = k.sb('ssd_cvo', [128, 12, 3 * (NS + 1)], F32, persist=True)
    k.phase_begin()
    cw_t = IN['ssm_conv_w']; cb_t = IN['ssm_conv_b']
    rows_to_fm(k, c, cwS, [(cw_t, cw_t.h.ap()[li, i]) for i in range(4)] + [(cb_t, cb_t.h.ap()[li])], 12, 'ssdcw')
    k.phase_end()
    k.phase_begin()
    rows_to_fm(k, c, scv, [(st_conv[0], st_conv[1][b, r]) for b in range(NS) for r in range(3)], 12, 'ssdsc')
    k.phase_end()
    k.phase_begin()
    dtb = bc_row(k, 'ssd_dtb', IN['ssm_dt_bias'], IN['ssm_dt_bias'].h.ap()[li], 16)
    Aneg = bc_row(k, 'ssd_A', IN['ssm_A_log'], IN['ssm_A_log'].h.ap()[li], 16)
    k.op('act', lambda e: e.activation(out=Aneg[:], in_=Aneg[:], func=AF.Exp), [Aneg], [Aneg])
    k.op('dve', lambda e: e.tensor_scalar(out=Aneg[:], in0=Aneg[:], scalar1=-1.0, scalar2=None, op0=ALU.mult), [Aneg], [Aneg])
    Dsk = bc_row(k, 'ssd_D', IN['ssm_D'], IN['ssm_D'].h.ap()[li], 16)
    nw = bc_row(k, 'ssd_nw', IN['ssm_norm_w'], IN['ssm_norm_w'].h.ap()[li], 1024)
    xbc = k.sb('ssd_xbc', [128, 12, 131], F32)
    xc = k.sb('ssd_xc', [128, 12, 128], F32)
    xt2 = k.sb('ssd_xt2', [128, 12, 128], F32)
    zT = k.sb('ssd_zT', [128, 8, 128], F32)
    dtT = k.sb('ssd_dtT', [16, 128], F32)
    Xtok = k.sb('ssd_Xtok', [128, 1024], F32)
    ztok = k.sb('ssd_ztok', [128, 1024], F32)
    Btok = k.sb('ssd_Btok', [128, 2, 128], F32)
    sm = k.sb('ssd_sm', [128, 16, 8], F32)
    DB = k.sb('ssd_DB', [128, 16], F32)
    Xdt = k.sb('ssd_Xdt', [128, 1024], F32)
    Xw = k.sb('ssd_Xw', [128, 1024], F32)
    Rt = k.sb('ssd_Rt', [128, 16, 128], F32)
    LT = k.sb('ssd_LT', [128, 16, 128], F32)
    CBs = k.sb('ssd_CBs', [128, 2, 128], F32)
    Y = k.sb('ssd_Y', [128, 1024], F32)
    Y2 = k.sb('ssd_Y2', [128, 1024], F32)
    nst = k.sb('ssd_nst', [128, 8], F32)
    ost = k.sb('ssd_ost', [128, 8, 128], BF16)
    zio = k.sb('ssd_zio', [64, 16, 128], F32)
    PB = [k.ps(f'ssd_pb{i}', [128, 512], F32) for i in range(8)]
    def b2(i):
        return PB[i]
    def sm_(n, j):
        return sm[0:n, :, j]
    def load_state(b):
        k.dma('sp', zio[:], st_z[1][b].rearrange("h p n -> p h n"), [st_z[0]], [zio], zio)
        for h in range(16):
            P = PB[h // 8]
            k.op('pe', lambda e: e.transpose(out=P[:, (h % 8) * 64:(h % 8 + 1) * 64], in_=zio[:, h, :], identity=c.idf[0:64, 0:64]), [zio, c.idf], [P])
        for g in range(2):
            k.op('dve', lambda e: e.tensor_copy(out=Z[:, g * 512:(g + 1) * 512], in_=PB[g][:, :]), [PB[g]], [Z])
    def store_state(dst):
        for h in range(16):
            P = PB[h // 4]
            k.op('pe', lambda e: e.transpose(out=P[0:64, (h % 4) * 128:(h % 4 + 1) * 128], in_=Z[:, h * 64:(h + 1) * 64], identity=c.idf[:, :]), [Z, c.idf], [P])
        for q in range(4):
            k.op('dve', lambda e: e.tensor_copy(out=zio[:, q * 4:(q + 1) * 4, :], in_=PB[q][0:64, :].rearrange("p (h n) -> p h n", n=128)), [PB[q]], [zio])
        k.dma('sp', dst[1].rearrange("h p n -> p h n"), zio[:], [zio], [dst[0]], zio)
    def chunk(t0, n, carry):
        xv = projT.h.ap()[off_xbc:off_xbc + 1536, :].rearrange("(f p) t -> p f t", p=128)
        if carry is None:
            k.dma('sp', xbc[:, :, 0:n + 3], xv[:, :, t0 - 3:t0 + n], [projT], [xbc], xbc)
        else:
            k.dma('sp', xbc[:, :, 3:n + 3], xv[:, :, t0:t0 + n], [projT], [xbc], xbc)
            if carry == 'zero':
                k.op('pool', lambda e: e.memset(xbc[:, :, 0:3], 0.0), [], [xbc])
            else:
                b = carry[1]
                k.op('pool', lambda e: e.tensor_copy(out=xbc[:, :, 0:3], in_=scv[:, :, 3 * b:3 * b + 3]), [scv], [xbc])
        k.dma('sp', zT[:, :, 0:n], projT.h.ap()[off_z:off_z + 1024, t0:t0 + n].rearrange("(f p) t -> p f t", p=128), [projT], [zT], zT)
        k.dma('sp', dtT[:, 0:n], projT.h.ap()[off_dt:off_dt + 16, t0:t0 + n], [projT], [dtT], dtT)
        def wv(i):
            return cwS[:, :, i:i + 1].broadcast_to([128, 12, n])
        k.op('dve', lambda e: e.tensor_tensor(out=xc[:, :, 0:n], in0=xbc[:, :, 3:3 + n], in1=wv(3), op=ALU.mult), [xbc, cwS], [xc])
        for i in (2, 1, 0):
            k.op('pool', lambda e: e.tensor_tensor(out=xt2[:, :, 0:n], in0=xbc[:, :, i:i + n], in1=wv(i), op=ALU.mult), [xbc, cwS], [xt2])
            k.op('dve', lambda e: e.tensor_tensor(out=xc[:, :, 0:n], in0=xc[:, :, 0:n], in1=xt2[:, :, 0:n], op=ALU.add), [xc, xt2], [xc])
        k.op('dve', lambda e: e.tensor_tensor(out=xc[:, :, 0:n], in0=xc[:, :, 0:n], in1=wv(4), op=ALU.add), [xc, cwS], [xc])
        k.op('act', lambda e: e.activation(out=xc[:, :, 0:n], in_=xc[:, :, 0:n], func=AF.Silu), [xc], [xc])
        for f in range(8):
            P = PB[f // 4]
            k.op('pe', lambda e: e.transpose(out=P[0:n, (f % 4) * 128:(f % 4 + 1) * 128], in_=xc[:, f, 0:n], identity=c.idf[:, :]), [xc, c.idf], [P])
        for g in range(2):
            k.op('dve', lambda e: e.tensor_copy(out=Xtok[0:n, g * 512:(g + 1) * 512], in_=PB[g][0:n, :]), [PB[g]], [Xtok])
        for f in range(8):
            P = PB[2 + f // 4]
            k.op('pe', lambda e: e.transpose(out=P[0:n, (f % 4) * 128:(f % 4 + 1) * 128], in_=zT[:, f, 0:n], identity=c.idf[:, :]), [zT, c.idf], [P])
        for g in range(2):
            k.op('act', lambda e: e.activation(out=ztok[0:n, g * 512:(g + 1) * 512], in_=PB[2 + g][0:n, :], func=AF.Silu), [PB[2 + g]], [ztok])
        P = PB[4]
        for g in range(2):
            k.op('pe', lambda e: e.transpose(out=P[0:n, g * 128:(g + 1) * 128], in_=xc[:, 8 + g, 0:n], identity=c.idf[:, :]), [xc, c.idf], [P])
        k.op('pe', lambda e: e.transpose(out=P[0:n, 256:272], in_=dtT[0:16, 0:n], identity=c.idf[0:16, 0:16]), [dtT, c.idf], [P])
        k.op('dve', lambda e: e.tensor_copy(out=Btok[0:n, :, :], in_=P[0:n, 0:256].rearrange("p (g s) -> p g s", s=128)), [P], [Btok])
        k.op('dve', lambda e: e.tensor_tensor(out=sm_(n, 0), in0=P[0:n, 256:272], in1=dtb[0:n, :], op=ALU.add), [P, dtb], [sm])
        k.op('act', lambda e: e.activation(out=sm_(n, 0), in_=sm_(n, 0), func=AF.Exp), [sm], [sm])
        k.op('act', lambda e: e.activation(out=sm_(n, 0), in_=sm_(n, 0), func=AF.Ln, bias=1.0, scale=1.0), [sm], [sm])
        k.op('dve', lambda e: e.tensor_tensor(out=sm_(n, 1), in0=sm_(n, 0), in1=Aneg[0:n, :], op=ALU.mult), [sm, Aneg], [sm])
        P5 = PB[5]
        k.op('pe', lambda e: e.matmul(P5[0:n, 0:16], lhsT=c.U[0:n, 0:n], rhs=sm_(n, 1), start=True, stop=True), [c.U, sm], [P5])
        k.op('dve', lambda e: e.tensor_copy(out=sm_(n, 2), in_=P5[0:n, 0:16]), [P5], [sm])
        k.op('act', lambda e: e.activation(out=sm_(n, 3), in_=P5[0:n, 0:16], func=AF.Exp), [P5], [sm])
        P6 = PB[6]
        k.op('pe', lambda e: e.matmul(P6[:, 0:16], lhsT=c.SEL[n][0:n, :], rhs=sm_(n, 2), start=True, stop=True), [c.SEL[n], sm], [P6])
        k.op('act', lambda e: e.activation(out=DB[:, :], in_=P6[:, 0:16], func=AF.Exp), [P6], [DB])
        k.op('dve', lambda e: e.tensor_tensor(out=sm_(n, 4), in0=P6[0:n, 0:16], in1=sm_(n, 2), op=ALU.subtract), [P6, sm], [sm])
        k.op('act', lambda e: e.activation(out=sm_(n, 4), in_=sm_(n, 4), func=AF.Exp), [sm], [sm])
        X3 = Xtok[0:n, :].rearrange("p (h q) -> p h q", q=64)
        k.op('dve', lambda e: e.tensor_tensor(out=Xdt[0:n, :].rearrange("p (h q) -> p h q", q=64), in0=X3, in1=sm[0:n, :, 0:1].broadcast_to([n, 16, 64]), op=ALU.mult), [Xtok, sm], [Xdt])
        k.op('pool', lambda e: e.tensor_tensor(out=Xw[0:n, :].rearrange("p (h q) -> p h q", q=64), in0=Xdt[0:n, :].rearrange("p (h q) -> p h q", q=64), in1=sm[0:n, :, 4:5].broadcast_to([n, 16, 64]), op=ALU.mult), [Xdt, sm], [Xw])
        P7 = PB[7]
        for g in range(2):
            k.op('pe', lambda e: e.matmul(P7[0:n, g * 128:g * 128 + n], lhsT=xc[:, 8 + g, 0:n], rhs=xc[:, 10 + g, 0:n], start=True, stop=True), [xc], [P7])
        k.op('dve', lambda e: e.tensor_copy(out=CBs[0:n, :, 0:n], in_=P7[0:n, 0:256].rearrange("p (g s) -> p g s", s=128)[:, :, 0:n]), [P7], [CBs])
        k.op('pool', lambda e: e.tensor_tensor(out=Rt[0:n, :, 0:n], in0=c.U[0:n, 0:n].unsqueeze(1).broadcast_to([n, 16, n]), in1=sm[0:n, :, 1:2].broadcast_to([n, 16, n]), op=ALU.mult), [c.U, sm], [Rt])
        for q in range(4):
            k.op('pe', lambda e: e.matmul(PB[4 + q][0:n, 0:4 * n].rearrange("p (h i) -> p h i", i=n), lhsT=c.ONES[0:n, 0:n], rhs=Rt[0:n, 4 * q:4 * q + 4, 0:n], start=True, stop=True), [c.ONES, Rt], [PB[4 + q]])
        for q in range(4):
            k.op('dve', lambda e: e.tensor_tensor(out=LT[0:n, 4 * q:4 * q + 4, 0:n], in0=PB[4 + q][0:n, 0:4 * n].rearrange("p (h i) -> p h i", i=n), in1=sm[0:n, 4 * q:4 * q + 4, 2:3].broadcast_to([n, 4, n]), op=ALU.subtract), [PB[4 + q], sm], [LT])
        k.op('pool', lambda e: e.tensor_tensor(out=LT[0:n, :, 0:n], in0=LT[0:n, :, 0:n], in1=c.MN[0:n, 0:n].unsqueeze(1).broadcast_to([n, 16, n]), op=ALU.add), [LT, c.MN], [LT])
        k.op('act', lambda e: e.activation(out=LT[0:n, :, 0:n], in_=LT[0:n, :, 0:n], func=AF.Exp), [LT], [LT])
        k.op('dve', lambda e: e.tensor_tensor(out=LT[0:n, :, 0:n].rearrange("p (g r) i -> p g r i", r=8), in0=LT[0:n, :, 0:n].rearrange("p (g r) i -> p g r i", r=8), in1=CBs[0:n, :, 0:n].unsqueeze(2).broadcast_to([n, 2, 8, n]), op=ALU.mult), [LT, CBs], [LT])
        for h in range(16):
            P = PB[h // 8]
            k.op('pe', lambda e: e.matmul(P[0:n, (h % 8) * 64:(h % 8 + 1) * 64], lhsT=LT[0:n, h, 0:n], rhs=Xdt[0:n, h * 64:(h + 1) * 64], start=True, stop=True), [LT, Xdt], [P])
        for g in range(2):
            k.op('pe', lambda e: e.matmul(PB[2 + g][0:n, :], lhsT=xc[:, 10 + g, 0:n], rhs=Z[:, g * 512:(g + 1) * 512], start=True, stop=True), [xc, Z], [PB[2 + g]])
        for g in range(2):
            sl = slice(g * 512, (g + 1) * 512)
            k.op('dve', lambda e: e.tensor_tensor(out=Y[0:n, sl].rearrange("p (h q) -> p h q", q=64), in0=PB[2 + g][0:n, :].rearrange("p (h q) -> p h q", q=64), in1=sm[0:n, 8 * g:8 * g + 8, 3:4].broadcast_to([n, 8, 64]), op=ALU.mult), [PB[2 + g], sm], [Y])
            k.op('dve', lambda e: e.tensor_tensor(out=Y[0:n, sl], in0=Y[0:n, sl], in1=PB[g][0:n, :], op=ALU.add), [Y, PB[g]], [Y])
        k.op('pool', lambda e: e.tensor_tensor(out=Y2[0:n, :].rearrange("p (h q) -> p h q", q=64), in0=X3, in1=Dsk[0:n, :].unsqueeze(2).broadcast_to([n, 16, 64]), op=ALU.mult), [Xtok, Dsk], [Y2])
        k.op('dve', lambda e: e.tensor_tensor(out=Y[0:n, :], in0=Y[0:n, :], in1=Y2[0:n, :], op=ALU.add), [Y, Y2], [Y])
        k.op('dve', lambda e: e.tensor_tensor(out=Y[0:n, :], in0=Y[0:n, :], in1=ztok[0:n, :], op=ALU.mult), [Y, ztok], [Y])
        for g in range(2):
            sl = slice(g * 512, (g + 1) * 512)
            k.op('act', lambda e: e.activation(out=Y2[0:n, sl], in_=Y[0:n, sl], func=AF.Square, accum_out=nst[0:n, g:g + 1]), [Y], [Y2, nst])
        k.op('dve', lambda e: e.tensor_scalar(out=nst[0:n, 2:4], in0=nst[0:n, 0:2], scalar1=1.0 / 512, scalar2=1e-6, op0=ALU.mult, op1=ALU.add), [nst], [nst])
        k.op('act', lambda e: e.activation(out=nst[0:n, 4:6], in_=nst[0:n, 2:4], func=AF.Sqrt), [nst], [nst])
        k.op('dve', lambda e: e.reciprocal(out=nst[0:n, 6:8], in_=nst[0:n, 4:6]), [nst], [nst])
        for g in range(2):
            sl = slice(g * 512, (g + 1) * 512)
            k.op('dve', lambda e: e.scalar_tensor_tensor(out=Y2[0:n, sl], in0=Y[0:n, sl], scalar=nst[0:n, 6 + g:7 + g], in1=nw[0:n, sl], op0=ALU.mult, op1=ALU.mult), [Y, nst, nw], [Y2])
        for f in range(8):
            P = PB[f // 4]
            k.op('pe', lambda e: e.transpose(out=P[:, (f % 4) * 128:(f % 4) * 128 + n], in_=Y2[0:n, f * 128:(f + 1) * 128], identity=c.idf[0:n, 0:n]), [Y2, c.idf], [P])
        for g in range(2):
            k.op('act', lambda e: e.copy(out=ost[:, g * 4:(g + 1) * 4, 0:n], in_=PB[g][:, :].rearrange("p (f t) -> p f t", t=128)[:, :, 0:n]), [PB[g]], [ost])
        k.dma('sp', mixT.h.ap().rearrange("k p t -> p k t")[:, kc0:kc0 + 8, t0:t0 + n], ost[:, :, 0:n], [ost], [mixT], ost)
        for g in range(2):
            k.op('pe', lambda e: e.matmul(PB[2 + g][:, :], lhsT=Btok[0:n, g, :], rhs=Xw[0:n, g * 512:(g + 1) * 512], start=True, stop=True), [Btok, Xw], [PB[2 + g]])
        k.op('pool', lambda e: e.tensor_tensor(out=Z[:, :].rearrange("p (h q) -> p h q", q=64), in0=Z[:, :].rearrange("p (h q) -> p h q", q=64), in1=DB[:, :].unsqueeze(2).broadcast_to([128, 16, 64]), op=ALU.mult), [Z, DB], [Z])
        for g in range(2):
            k.op('dve', lambda e: e.tensor_tensor(out=Z[:, g * 512:(g + 1) * 512], in0=Z[:, g * 512:(g + 1) * 512], in1=PB[2 + g][:, :], op=ALU.add), [Z, PB[2 + g]], [Z])
    k.op('pool', lambda e: e.memset(Z[:], 0.0), [], [Z])
    for t0 in range(0, T, 128):
        chunk(t0, 128, 'zero' if t0 == 0 else None)
        if t0 + 128 == T:
            k.op('pool', lambda e: e.tensor_copy(out=cvo[:, :, 0:3], in_=xbc[:, :, 128:131]), [xbc], [cvo])
    store_state(pz_out)
    for b in range(NS):
        load_state(b)
        chunk(T + 8 * b, 8, ('state', b))
        k.op('pool', lambda e: e.tensor_copy(out=cvo[:, :, 3 * (b + 1):3 * (b + 2)], in_=xbc[:, :, 8:11]), [xbc], [cvo])
        store_state((sz_out[0], sz_out[1][b]))
    k.phase_end()
    k.phase_begin()
    fm_to_rows(k, c, cvo, 3 * (NS + 1), 12, [(pconv_out[0], pconv_out[1][r]) for r in range(3)] + [(sconv_out[0], sconv_out[1][b, r]) for b in range(NS) for r in range(3)], 'ssdcvo')
    k.phase_end()

def gdn_phase(k, c, li, projT, mixT, T, NS, IN, ps_out, pconv_out, ss_out, sconv_out, st_s, st_conv, off_qkv, off_z, off_ba, kc0=0):
    S = k.sb('gdn_S', [128, 8, 128], F32, persist=True)
    cwS = k.sb('gdn_cw', [128, 24, 4], F32, persist=True)
    scv = k.sb('gdn_scv', [128, 24, 3 * NS], F32, persist=True)
    cvo = k.sb('gdn_cvo', [128, 24, 3 * (NS + 1)], F32, persist=True)
    OD = k.sb('gdn_od', [128, 128], F32, persist=True)
    k.op('dve', lambda e: e.tensor_scalar(out=OD[:], in0=c.idf[:], scalar1=-1.0, scalar2=1.0, op0=ALU.mult, op1=ALU.add), [c.idf], [OD])
    nwc3 = k.sb('gdn_nw', [128, 1, 1], F32, persist=True)
    k.phase_begin()
    rows_to_fm(k, c, nwc3, [(IN['gdn_norm_w'], IN['gdn_norm_w'].h.ap()[li])], 1, 'gdnnw')
    k.phase_end()
    k.phase_begin()
    cw_t = IN['gdn_conv_w']
    rows_to_fm(k, c, cwS, [(cw_t, cw_t.h.ap()[li, i]) for i in range(4)], 24, 'gdncw')
    k.phase_end()
    k.phase_begin()
    rows_to_fm(k, c, scv, [(st_conv[0], st_conv[1][b, r]) for b in range(NS) for r in range(3)], 24, 'gdnsc')
    k.phase_end()
    k.phase_begin()
    dtb = bc_row(k, 'gdn_dtb', IN['gdn_dt_bias'], IN['gdn_dt_bias'].h.ap()[li], 8)
    Aneg = bc_row(k, 'gdn_A', IN['gdn_A_log'], IN['gdn_A_log'].h.ap()[li], 8)
    k.op('act', lambda e: e.activation(out=Aneg[:], in_=Aneg[:], func=AF.Exp), [Aneg], [Aneg])
    k.op('dve', lambda e: e.tensor_scalar(out=Aneg[:], in0=Aneg[:], scalar1=-1.0, scalar2=None, op0=ALU.mult), [Aneg], [Aneg])
    xin = k.sb('gdn_xin', [128, 24, 131], F32)
    xc = k.sb('gdn_xc', [128, 24, 128], F32)
    xt2 = k.sb('gdn_xt2', [128, 24, 128], F32)
    zT = k.sb('gdn_zT', [128, 8, 128], F32)
    baT = k.sb('gdn_baT', [16, 128], F32)
    sm = k.sb('gdn_sm', [128, 8, 8], F32)
    EGl = k.sb('gdn_EGl', [128, 8], F32)
    Rt = k.sb('gdn_Rt', [128, 8, 128], F32)
    EB = k.sb('gdn_EB', [128, 8, 128], F32)
    gam = k.sb('gdn_gam', [128, 8, 128], F32)
    Ktok = k.sb('gdn_Ktok', [128, 8, 128], F32)
    Vtok = k.sb('gdn_Vtok', [128, 8, 128], F32)
    NT = k.sb('gdn_NT', [128, 8, 128], F32)
    Nm = k.sb('gdn_Nm', [128, 8, 128], F32)
    NT2 = k.sb('gdn_NT2', [128, 8, 128], F32)
    Nm2 = k.sb('gdn_Nm2', [128, 8, 128], F32)
    PT = k.sb('gdn_PT', [128, 8, 128], F32)
    Aqk = k.sb('gdn_Aqk', [128, 8, 128], F32)
    W1 = k.sb('gdn_W1', [128, 8, 128], F32)
    W2 = k.sb('gdn_W2', [128, 8, 128], F32)
    W3 = k.sb('gdn_W3', [128, 8, 128], F32)
    ost = k.sb('gdn_ost', [128, 8, 128], BF16)
    PB = [k.ps(f'gdn_pb{i}', [128, 512], F32) for i in range(8)]
    def pv(pair, h, rows, cols):
        return PB[2 * pair + h // 4][0:rows, (h % 4) * 128:(h % 4) * 128 + cols]
    def pbank(pair, half, rows, cols):
        return PB[2 * pair + half][0:rows, :].rearrange("p (h i) -> p h i", i=128)[:, :, 0:cols]
    def ptl(pair, half):
        return PB[2 * pair + half]
    def evac(dst, pair, rows, cols, fn):
        for half in range(2):
            fn(half, dst[0:rows, 4 * half:4 * half + 4, 0:cols], pbank(pair, half, rows, cols), ptl(pair, half))
    def chunk(t0, n, carry, steps):
        xv = projT.h.ap()[off_qkv:off_qkv + 3072, :].rearrange("(f p) t -> p f t", p=128)
        if carry is None:
            k.dma('sp', xin[:, :, 0:n + 3], xv[:, :, t0 - 3:t0 + n], [projT], [xin], xin)
        else:
            k.dma('sp', xin[:, :, 3:n + 3], xv[:, :, t0:t0 + n], [projT], [xin], xin)
            if carry == 'zero':
                k.op('pool', lambda e: e.memset(xin[:, :, 0:3], 0.0), [], [xin])
            else:
                b = carry[1]
                k.op('pool', lambda e: e.tensor_copy(out=xin[:, :, 0:3], in_=scv[:, :, 3 * b:3 * b + 3]), [scv], [xin])
        k.dma('sp', zT[:, :, 0:n], projT.h.ap()[off_z:off_z + 1024, t0:t0 + n].rearrange("(f p) t -> p f t", p=128), [projT], [zT], zT)
        k.dma('sp', baT[:, 0:n], projT.h.ap()[off_ba:off_ba + 16, t0:t0 + n], [projT], [baT], baT)
        def wv(i):
            return cwS[:, :, i:i + 1].broadcast_to([128, 24, n])
        k.op('dve', lambda e: e.tensor_tensor(out=xc[:, :, 0:n], in0=xin[:, :, 3:3 + n], in1=wv(3), op=ALU.mult), [xin, cwS], [xc])
        for i in (2, 1, 0):
            k.op('pool', lambda e: e.tensor_tensor(out=xt2[:, :, 0:n], in0=xin[:, :, i:i + n], in1=wv(i), op=ALU.mult), [xin, cwS], [xt2])
            k.op('dve', lambda e: e.tensor_tensor(out=xc[:, :, 0:n], in0=xc[:, :, 0:n], in1=xt2[:, :, 0:n], op=ALU.add), [xc, xt2], [xc])
        k.op('act', lambda e: e.activation(out=xc[:, :, 0:n], in_=xc[:, :, 0:n], func=AF.Silu), [xc], [xc])
        k.op('act', lambda e: e.activation(out=zT[:, :, 0:n], in_=zT[:, :, 0:n], func=AF.Silu), [zT], [zT])
        if getattr(c, 'gstop', 99) <= 1: return
        k.op('act', lambda e: e.activation(out=xt2[:, 0:16, 0:n], in_=xc[:, 0:16, 0:n], func=AF.Square), [xc], [xt2])
        for q in range(4):
            k.op('pe', lambda e: e.matmul(PB[q][:, 0:4 * n].rearrange("p (h i) -> p h i", i=n), lhsT=c.ONES[:, :], rhs=xt2[:, 4 * q:4 * q + 4, 0:n], start=True, stop=True), [c.ONES, xt2], [PB[q]])
        for q in range(4):
            k.op('act', lambda e: e.activation(out=xt2[:, 4 * q:4 * q + 4, 0:n], in_=PB[q][:, 0:4 * n].rearrange("p (h i) -> p h i", i=n), func=AF.Sqrt, bias=c.eps6[:, 0:1], scale=1.0), [PB[q], c.eps6], [xt2])
        k.op('dve', lambda e: e.reciprocal(out=xt2[:, 0:16, 0:n], in_=xt2[:, 0:16, 0:n]), [xt2], [xt2])
        k.op('dve', lambda e: e.scalar_tensor_tensor(out=xc[:, 0:8, 0:n], in0=xc[:, 0:8, 0:n], scalar=128 ** -0.5, in1=xt2[:, 0:8, 0:n], op0=ALU.mult, op1=ALU.mult), [xc, xt2], [xc])
        k.op('dve', lambda e: e.tensor_tensor(out=xc[:, 8:16, 0:n], in0=xc[:, 8:16, 0:n], in1=xt2[:, 8:16, 0:n], op=ALU.mult), [xc, xt2], [xc])
        if getattr(c, 'gstop', 99) <= 2: return
        for h in range(8):
            k.op('pe', lambda e: e.transpose(out=pv(2, h, n, 128), in_=xc[:, 8 + h, 0:n], identity=c.idf[:, :]), [xc, c.idf], [ptl(2, h // 4)])
            k.op('pe', lambda e: e.transpose(out=pv(3, h, n, 128), in_=xc[:, 16 + h, 0:n], identity=c.idf[:, :]), [xc, c.idf], [ptl(3, h // 4)])
        evac(Ktok, 2, n, 128, lambda half, o, p, pt: k.op('dve', lambda e: e.tensor_copy(out=o, in_=p), [pt], [Ktok]))
        evac(Vtok, 3, n, 128, lambda half, o, p, pt: k.op('act', lambda e: e.copy(out=o, in_=p), [pt], [Vtok]))
        if getattr(c, 'gstop', 99) <= 3: return
        P0 = PB[0]
        k.op('pe', lambda e: e.transpose(out=P0[0:n, 0:16], in_=baT[0:16, 0:n], identity=c.idf[0:16, 0:16]), [baT, c.idf], [P0])
        k.op('act', lambda e: e.activation(out=sm[0:n, :, 0], in_=P0[0:n, 0:8], func=AF.Sigmoid), [P0], [sm])
        k.op('dve', lambda e: e.tensor_tensor(out=sm[0:n, :, 1], in0=P0[0:n, 8:16], in1=dtb[0:n, :], op=ALU.add), [P0, dtb], [sm])
        k.op('act', lambda e: e.activation(out=sm[0:n, :, 1], in_=sm[0:n, :, 1], func=AF.Exp), [sm], [sm])
        k.op('act', lambda e: e.activation(out=sm[0:n, :, 1], in_=sm[0:n, :, 1], func=AF.Ln, bias=1.0, scale=1.0), [sm], [sm])
        k.op('dve', lambda e: e.tensor_tensor(out=sm[0:n, :, 1], in0=sm[0:n, :, 1], in1=Aneg[0:n, :], op=ALU.mult), [sm, Aneg], [sm])
        k.op('dve', lambda e: e.tensor_scalar(out=sm[0:n, :, 4], in0=sm[0:n, :, 0], scalar1=-1.0, scalar2=None, op0=ALU.mult), [sm], [sm])
        P1 = PB[1]
        k.op('pe', lambda e: e.matmul(P1[0:n, 0:8], lhsT=c.U[0:n, 0:n], rhs=sm[0:n, :, 1], start=True, stop=True), [c.U, sm], [P1])
        k.op('dve', lambda e: e.tensor_copy(out=sm[0:n, :, 2], in_=P1[0:n, 0:8]), [P1], [sm])
        k.op('pe', lambda e: e.matmul(P0[:, 16:24], lhsT=c.SEL[n][0:n, :], rhs=sm[0:n, :, 2], start=True, stop=True), [c.SEL[n], sm], [P0])
        k.op('act', lambda e: e.activation(out=EGl[:, :], in_=P0[:, 16:24], func=AF.Exp), [P0], [EGl])
        k.op('dve', lambda e: e.tensor_tensor(out=sm[0:n, :, 3], in0=P0[0:n, 16:24], in1=sm[0:n, :, 2], op=ALU.subtract), [P0, sm], [sm])
        k.op('act', lambda e: e.activation(out=sm[0:n, :, 3], in_=sm[0:n, :, 3], func=AF.Exp), [sm], [sm])
        if getattr(c, 'gstop', 99) <= 4: return
        k.op('pool', lambda e: e.tensor_tensor(out=Rt[0:n, :, 0:n], in0=c.U[0:n, 0:n].unsqueeze(1).broadcast_to([n, 8, n]), in1=sm[0:n, :, 1:2].broadcast_to([n, 8, n]), op=ALU.mult), [c.U, sm], [Rt])
        for half in range(2):
            k.op('pe', lambda e: e.matmul(pbank(1, half, 128, n), lhsT=c.ONES[0:n, :], rhs=Rt[0:n, 4 * half:4 * half + 4, 0:n], start=True, stop=True), [c.ONES, Rt], [ptl(1, half)])
        if getattr(c, 'gstop', 99) <= 4.2: return
        evac(EB, 1, 128, n, lambda half, o, p, pt: k.op('act', lambda e: e.activation(out=o, in_=p, func=AF.Exp), [pt], [EB]))
        if getattr(c, 'gstop', 99) <= 4.4: return
        for half in range(2):
            k.op('dve', lambda e: e.tensor_tensor(out=gam[0:n, 4 * half:4 * half + 4, 0:n], in0=pbank(1, half, n, n), in1=sm[0:n, 4 * half:4 * half + 4, 2:3].broadcast_to([n, 4, n]), op=ALU.subtract), [ptl(1, half), sm], [gam])
        if getattr(c, 'gstop', 99) <= 4.6: return
        k.op('pool', lambda e: e.tensor_tensor(out=gam[0:n, :, 0:n], in0=gam[0:n, :, 0:n], in1=c.MN[0:n, 0:n].unsqueeze(1).broadcast_to([n, 8, n]), op=ALU.add), [gam, c.MN], [gam])
        k.op('act', lambda e: e.activation(out=gam[0:n, :, 0:n], in_=gam[0:n, :, 0:n], func=AF.Exp), [gam], [gam])
        if getattr(c, 'gstop', 99) <= 5: return
        for h in range(8):
            k.op('pe', lambda e: e.matmul(pv(2, h, n, n), lhsT=xc[:, 8 + h, 0:n], rhs=xc[:, 8 + h, 0:n], start=True, stop=True), [xc], [ptl(2, h // 4)])
            k.op('pe', lambda e: e.matmul(pv(3, h, n, n), lhsT=xc[:, 8 + h, 0:n], rhs=xc[:, h, 0:n], start=True, stop=True), [xc], [ptl(3, h // 4)])
        for half in range(2):
            hs = slice(4 * half, 4 * half + 4)
            k.op('dve', lambda e: e.tensor_tensor(out=NT[0:n, hs, 0:n], in0=pbank(2, half, n, n), in1=gam[0:n, hs, 0:n], op=ALU.mult), [ptl(2, half), gam], [NT])
            k.op('dve', lambda e: e.tensor_tensor(out=Aqk[0:n, hs, 0:n], in0=pbank(3, half, n, n), in1=gam[0:n, hs, 0:n], op=ALU.mult), [ptl(3, half), gam], [Aqk])
        k.op('pool', lambda e: e.tensor_tensor(out=NT[0:n, :, 0:n], in0=NT[0:n, :, 0:n], in1=sm[0:n, :, 4:5].broadcast_to([n, 8, n]), op=ALU.mult), [NT, sm], [NT])
        k.op('pool', lambda e: e.tensor_tensor(out=NT[0:n, :, 0:n], in0=NT[0:n, :, 0:n], in1=OD[0:n, 0:n].unsqueeze(1).broadcast_to([n, 8, n]), op=ALU.mult), [NT, OD], [NT])
        if getattr(c, 'gstop', 99) <= 6: return
        for h in range(8):
            k.op('pe', lambda e: e.transpose(out=pv(0, h, n, n), in_=NT[0:n, h, 0:n], identity=c.idf[0:n, 0:n]), [NT, c.idf], [ptl(0, h // 4)])
        evac(Nm, 0, n, n, lambda half, o, p, pt: k.op('act', lambda e: e.copy(out=o, in_=p), [pt], [Nm]))
        k.op('dve', lambda e: e.tensor_tensor(out=PT[0:n, :, 0:n], in0=NT[0:n, :, 0:n], in1=c.idf[0:n, 0:n].unsqueeze(1).broadcast_to([n, 8, n]), op=ALU.add), [NT, c.idf], [PT])
        X, XT, X2, XT2 = Nm, NT, Nm2, NT2
        for it in range(steps):
            lastit = (it == steps - 1)
            for h in range(8):
                k.op('pe', lambda e: e.matmul(pv(1, h, n, n), lhsT=XT[0:n, h, 0:n], rhs=X[0:n, h, 0:n], start=True, stop=True), [XT, X], [ptl(1, h // 4)])
                if not lastit:
                    k.op('pe', lambda e: e.matmul(pv(2, h, n, n), lhsT=X[0:n, h, 0:n], rhs=XT[0:n, h, 0:n], start=True, stop=True), [XT, X], [ptl(2, h // 4)])
            evac(X2, 1, n, n, lambda half, o, p, pt: k.op('act', lambda e: e.copy(out=o, in_=p), [pt], [X2]))
            if not lastit:
                evac(XT2, 2, n, n, lambda half, o, p, pt: k.op('dve', lambda e: e.tensor_copy(out=o, in_=p), [pt], [XT2]))
            for h in range(8):
                k.op('pe', lambda e: e.matmul(pv(3, h, n, n), lhsT=X2[0:n, h, 0:n], rhs=PT[0:n, h, 0:n], start=True, stop=True), [X2, PT], [ptl(3, h // 4)])
            for half in range(2):
                hs = slice(4 * half, 4 * half + 4)
                k.op('dve', lambda e: e.tensor_tensor(out=PT[0:n, hs, 0:n], in0=PT[0:n, hs, 0:n], in1=pbank(3, half, n, n), op=ALU.add), [PT, ptl(3, half)], [PT])
            X, X2 = X2, X
            XT, XT2 = XT2, XT
        if getattr(c, 'gstop', 99) <= 7: return
        k.op('pool', lambda e: e.tensor_tensor(out=W1[:, :, 0:n], in0=xc[:, 8:16, 0:n], in1=EB[:, :, 0:n], op=ALU.mult), [xc, EB], [W1])
        k.op('pool', lambda e: e.tensor_tensor(out=W2[:, :, 0:n], in0=xc[:, 0:8, 0:n], in1=EB[:, :, 0:n], op=ALU.mult), [xc, EB], [W2])
        for h in range(8):
            k.op('pe', lambda e: e.matmul(pv(0, h, n, 128), lhsT=W1[:, h, 0:n], rhs=S[:, h, :], start=True, stop=True), [W1, S], [ptl(0, h // 4)])
        for half in range(2):
            hs = slice(4 * half, 4 * half + 4)
            k.op('dve', lambda e: e.tensor_tensor(out=W3[0:n, hs, :], in0=Vtok[0:n, hs, :], in1=pbank(0, half, n, 128), op=ALU.subtract), [Vtok, ptl(0, half)], [W3])
        for h in range(8):
            k.op('pe', lambda e: e.matmul(pv(1, h, n, 128), lhsT=PT[0:n, h, 0:n], rhs=W3[0:n, h, :], start=True, stop=True), [PT, W3], [ptl(1, h // 4)])
        for half in range(2):
            hs = slice(4 * half, 4 * half + 4)
            k.op('dve', lambda e: e.tensor_tensor(out=W1[0:n, hs, :], in0=pbank(1, half, n, 128), in1=sm[0:n, hs, 0:1].broadcast_to([n, 4, 128]), op=ALU.mult), [ptl(1, half), sm], [W1])
        for h in range(8):
            k.op('pe', lambda e: e.matmul(pv(2, h, 128, n), lhsT=S[:, h, :], rhs=W2[:, h, 0:n], start=True, stop=False), [S, W2], [ptl(2, h // 4)])
            k.op('pe', lambda e: e.matmul(pv(2, h, 128, n), lhsT=W1[0:n, h, :], rhs=Aqk[0:n, h, 0:n], start=False, stop=True), [W1, Aqk], [ptl(2, h // 4)])
        k.op('pool', lambda e: e.tensor_tensor(out=W3[0:n, :, :], in0=Ktok[0:n, :, :], in1=sm[0:n, :, 3:4].broadcast_to([n, 8, 128]), op=ALU.mult), [Ktok, sm], [W3])
        for h in range(8):
            k.op('pe', lambda e: e.matmul(pv(3, h, 128, 128), lhsT=W3[0:n, h, :], rhs=W1[0:n, h, :], start=True, stop=True), [W3, W1], [ptl(3, h // 4)])
        k.op('pool', lambda e: e.tensor_tensor(out=S[:, :, :], in0=S[:, :, :], in1=EGl[:, :].unsqueeze(2).broadcast_to([128, 8, 128]), op=ALU.mult), [S, EGl], [S])
        for half in range(2):
            hs = slice(4 * half, 4 * half + 4)
            k.op('dve', lambda e: e.tensor_tensor(out=S[:, hs, :], in0=S[:, hs, :], in1=pbank(3, half, 128, 128), op=ALU.add), [S, ptl(3, half)], [S])
        if getattr(c, 'gstop', 99) <= 8: return
        evac(W2, 2, 128, n, lambda half, o, p, pt: k.op('act', lambda e: e.activation(out=o, in_=p, func=AF.Square), [pt], [W2]))
        if getattr(c, 'gstop', 99) <= 8.2: return
        for half in range(2):
            k.op('pe', lambda e: e.matmul(pbank(0, half, 128, n), lhsT=c.ONES[:, :], rhs=W2[:, 4 * half:4 * half + 4, 0:n], start=True, stop=True), [c.ONES, W2], [ptl(0, half)])
        if getattr(c, 'gstop', 99) <= 8.4: return
        evac(W2, 0, 128, n, lambda half, o, p, pt: k.op('act', lambda e: e.activation(out=o, in_=p, func=AF.Sqrt, bias=c.eps6[:, 0:1], scale=1.0 / 128), [pt, c.eps6], [W2]))
        if getattr(c, 'gstop', 99) <= 8.6: return
        k.op('dve', lambda e: e.reciprocal(out=W2[:, :, 0:n], in_=W2[:, :, 0:n]), [W2], [W2])
        for half in range(2):
            hs = slice(4 * half, 4 * half + 4)
            k.op('dve', lambda e: e.tensor_tensor(out=W2[:, hs, 0:n], in0=W2[:, hs, 0:n], in1=pbank(2, half, 128, n), op=ALU.mult), [W2, ptl(2, half)], [W2])
        if getattr(c, 'gstop', 99) <= 8.8: return
        k.op('dve', lambda e: e.scalar_tensor_tensor(out=ost[:, :, 0:n], in0=W2[:, :, 0:n], scalar=nwc3[:, 0, 0:1], in1=zT[:, :, 0:n], op0=ALU.mult, op1=ALU.mult), [W2, zT, nwc3], [ost])
        k.dma('sp', mixT.h.ap().rearrange("k p t -> p k t")[:, kc0:kc0 + 8, t0:t0 + n], ost[:, :, 0:n], [ost], [mixT], ost)
    k.op('pool', lambda e: e.memset(S[:], 0.0), [], [S])
    for t0 in range(0, T, 128):
        chunk(t0, 128, 'zero' if t0 == 0 else None, 6)
        if t0 + 128 == T:
            k.op('pool', lambda e: e.tensor_copy(out=cvo[:, :, 0:3], in_=xin[:, :, 128:131]), [xin], [cvo])
    k.dma('sp', ps_out[1].rearrange("h d e -> d h e"), S[:], [S], [ps_out[0]], S)
    for b in range(0 if getattr(c, 'skip_sample', False) else NS):
        k.dma('sp', S[:], st_s[1][b].rearrange("h d e -> d h e"), [st_s[0]], [S], S)
        chunk(T + 8 * b, 8, ('state', b), 2)
        k.op('pool', lambda e: e.tensor_copy(out=cvo[:, :, 3 * (b + 1):3 * (b + 2)], in_=xin[:, :, 8:11]), [xin], [cvo])
        k.dma('sp', ss_out[1][b].rearrange("h d e -> d h e"), S[:], [S], [ss_out[0]], S)
    k.phase_end()
    k.phase_begin()
    fm_to_rows(k, c, cvo, 3 * (NS + 1), 24, [(pconv_out[0], pconv_out[1][r]) for r in range(3)] + [(sconv_out[0], sconv_out[1][b, r]) for b in range(NS) for r in range(3)], 'gdncvo')
    k.phase_end()

def row_to_col(k, c, dst_ap, dst_tl, tl, ap_row, m, name):
    Rt = k.sb(f'{name}_r', [1, 128], F32)
    k.dma('sp', Rt[0:1, 0:m], ap_row.unsqueeze(0), [tl], [Rt], Rt)
    P = k.ps(f'{name}_p', [128, 8], F32)
    k.op('pe', lambda e: e.transpose(out=P[0:m, 0:1], in_=Rt[0:1, 0:m], identity=c.idf[0:1, 0:1]), [Rt, c.idf], [P])
    k.op('dve', lambda e: e.tensor_copy(out=dst_ap, in_=P[0:m, 0:1]), [P], [dst_tl])

def rwkv_phase(k, c, li, projT, mixT, T, NS, IN, pS_out, pshift_out, sS_out, sshift_out, st_S, st_shift, off, kc0=8):
    NSH = 1 + NS
    PM = k.sb('rw_PM', [128, 2], F32, persist=True)
    par = k.sb('rw_par', [128, 8, 8], F32, persist=True)
    mu = k.sb('rw_mu', [128, 27, 1], F32, persist=True)
    shp = k.sb('rw_shp', [128, 27, NS], F32, persist=True)
    sho = k.sb('rw_sho', [128, 27, NSH], F32, persist=True)
    BD = k.sb('rw_BD', [128, 128], F32, persist=True)
    US = k.sb('rw_US', [128, 128], F32, persist=True)
    UI = k.sb('rw_UI', [128, 128], F32, persist=True)
    k.op('pool', lambda e: e.memset(BD[:], 0.0), [], [BD])
    k.op('pool', lambda e: e.memset(PM[:], 0.0), [], [PM])
    k.op('pool', lambda e: e.memset(PM[0:64, 0:1], 1.0), [], [PM])
    k.op('pool', lambda e: e.memset(PM[64:128, 1:2], 1.0), [], [PM])
    k.op('pool', lambda e: e.memset(BD[0:64, 0:64], 1.0), [], [BD])
    k.op('pool', lambda e: e.memset(BD[64:128, 64:128], 1.0), [], [BD])
    k.op('dve', lambda e: e.tensor_tensor(out=US[:], in0=c.U[:], in1=c.idf[:], op=ALU.subtract), [c.U, c.idf], [US])
    k.op('dve', lambda e: e.tensor_copy(out=UI[:], in_=c.U[:]), [c.U], [UI])
    k.op('pool', lambda e: e.memset(sho[:], 0.0), [], [sho])
    k.op('pool', lambda e: e.memset(shp[:], 0.0), [], [shp])
    k.op('pool', lambda e: e.memset(mu[:], 0.0), [], [mu])
    k.phase_begin()
    names = ['rwkv_a0', 'rwkv_k_k', 'rwkv_k_a', 'rwkv_r_k', 'rwkv_ln_w', 'rwkv_ln_b']
    rows_to_fm(k, c, par[:, :, 0:6], [(IN[nm], IN[nm].h.ap()[li]) for nm in names], 8, 'rwpar') if False else None
    par6 = k.sb('rw_par6', [128, 8, 6], F32)
    rows_to_fm(k, c, par6, [(IN[nm], IN[nm].h.ap()[li]) for nm in names], 8, 'rwpar')
    k.op('dve', lambda e: e.tensor_copy(out=par[:, :, 0:6], in_=par6[:]), [par6], [par])
    k.phase_end()
    k.phase_begin()
    mu26 = k.sb('rw_mu26', [128, 26, 1], F32)
    rows_to_fm(k, c, mu26, [(IN['rwkv_mu'], IN['rwkv_mu'].h.ap()[li, 0:3328])], 26, 'rwmu')
    k.op('dve', lambda e: e.tensor_copy(out=mu[:, 0:26, :], in_=mu26[:]), [mu26], [mu])
    row_to_col(k, c, mu[0:32, 26, 0:1], mu, IN['rwkv_mu'], IN['rwkv_mu'].h.ap()[li, 3328:3360], 32, 'rwmut')
    k.phase_end()
    k.phase_begin()
    sh26 = k.sb('rw_sh26', [128, 26, NS], F32)
    rows_to_fm(k, c, sh26, [(st_shift[0], st_shift[1][b, 0:3328]) for b in range(NS)], 26, 'rwsh')
    k.op('dve', lambda e: e.tensor_copy(out=shp[:, 0:26, :], in_=sh26[:]), [sh26], [shp])
    for b in range(NS):
        row_to_col(k, c, shp[0:32, 26, b:b + 1], shp, st_shift[0], st_shift[1][b, 3328:3360], 32, f'rwsht{b}')
    k.phase_end()
    k.phase_begin()
    w0B = bc_row(k, 'rw_w0B', IN['rwkv_w0'], IN['rwkv_w0'].h.ap()[li], 1024)
    w2 = k.sb('rw_w2', [64, 1024], F32)
    a2 = k.sb('rw_a2', [128, 1024], F32)
    g2a = k.sb('rw_g2a', [128, 1024], F32)
    g2b = k.sb('rw_g2b', [32, 1024], F32)
    k.dma('sp', w2[:], IN['rwkv_w2'].h.ap()[li], [IN['rwkv_w2']], [w2], w2)
    k.dma('sp', a2[64:128, :], IN['rwkv_a2'].h.ap()[li], [IN['rwkv_a2']], [a2], a2)
    k.dma('sp', g2a[:], IN['rwkv_g2'].h.ap()[li, 0:128], [IN['rwkv_g2']], [g2a], g2a)
    k.dma('sp', g2b[:], IN['rwkv_g2'].h.ap()[li, 128:160], [IN['rwkv_g2']], [g2b], g2b)
    zin = k.sb('rw_zin', [128, 27, 129], F32)
    zm = k.sb('rw_zm', [128, 27, 128], F32)
    t24 = k.sb('rw_t24', [128, 27, 128], F32)
    EL = k.sb('rw_EL', [128, 8, 128], F32)
    ELi = k.sb('rw_ELi', [128, 8, 128], F32)
    ELx = k.sb('rw_ELx', [128, 8, 128], F32)
    av = k.sb('rw_av', [128, 8, 128], F32)
    kk = k.sb('rw_kk', [128, 8, 128], F32)
    rt_ = k.sb('rw_rt', [128, 8, 128], F32)
    kt_ = k.sb('rw_kt', [128, 8, 128], F32)
    bt_ = k.sb('rw_bt', [128, 8, 128], F32)
    at_ = av
    gate = k.sb('rw_gate', [128, 8, 128], F32)
    bon = k.sb('rw_bon', [128, 8, 128], F32)
    yT = k.sb('rw_yT', [128, 8, 128], F32)
    ldt = k.sb('rw_ldt', [128, 1024], F32)
    Vtok = kk; Ktok = ELi; Btok = ELx
    NT = k.sb('rw_NT', [128, 4, 128], F32); Nm = k.sb('rw_Nm', [128, 4, 128], F32)
    NT2 = k.sb('rw_NT2', [128, 4, 128], F32); Nm2 = k.sb('rw_Nm2', [128, 4, 128], F32)
    PT = k.sb('rw_PT', [128, 4, 128], F32)
    Aak = k.sb('rw_Aak', [128, 4, 128], F32); Arb = k.sb('rw_Arb', [128, 4, 128], F32); Ark = k.sb('rw_Ark', [128, 4, 128], F32)
    btM = k.sb('rw_btM', [128, 4, 128], F32); ktM = k.sb('rw_ktM', [128, 4, 128], F32)
    X1s = k.sb('rw_X1s', [128, 4, 64], F32); Uu = k.sb('rw_Uu', [128, 2, 128], F32); Wt = k.sb('rw_Wt', [128, 2, 128], F32)
    Upad = k.sb('rw_Upad', [128, 4, 128], F32); Vpad = k.sb('rw_Vpad', [128, 4, 128], F32)
    Zbd = k.sb('rw_Zbd', [128, 8, 128], F32); zio = k.sb('rw_zio', [128, 8, 128], F32)
    ost = k.sb('rw_ost', [128, 8, 128], BF16)
    PB = [k.ps(f'rw_pb{i}', [128, 512], F32) for i in range(8)]
    k.op('pool', lambda e: e.memset(zin[:], 0.0), [], [zin])
    def pv(pair, hh, rows, cols):
        return PB[2 * pair + hh // 4][0:rows, (hh % 4) * 128:(hh % 4) * 128 + cols]
    def pbank(pair, half, rows, cols):
        return PB[2 * pair + half][0:rows, :].rearrange("p (h i) -> p h i", i=128)[:, :, 0:cols]
    def ptl(pair, half):
        return PB[2 * pair + half]
    def fm(tile, h, n):
        return tile[(h % 2) * 64:(h % 2) * 64 + 64, h // 2, 0:n]
    def chunk(t0, n, carry, steps):
        if getattr(c, 'rstop', 99) <= 0: return
        zv = projT.h.ap()[off:off + 3328, :].rearrange("(f p) t -> p f t", p=128)
        zv2 = projT.h.ap()[off + 3328:off + 3360, :]
        if carry is None:
            k.dma('sp', zin[:, 0:26, 0:n + 1], zv[:, :, t0 - 1:t0 + n], [projT], [zin], zin)
            k.dma('sp', zin[0:32, 26, 0:n + 1], zv2[:, t0 - 1:t0 + n], [projT], [zin], zin)
        else:
            k.dma('sp', zin[:, 0:26, 1:n + 1], zv[:, :, t0:t0 + n], [projT], [zin], zin)
            k.dma('sp', zin[0:32, 26, 1:n + 1], zv2[:, t0:t0 + n], [projT], [zin], zin)
            if carry == 'zero':
                k.op('pool', lambda e: e.memset(zin[:, :, 0:1], 0.0), [], [zin])
            else:
                b = carry[1]
                k.op('pool', lambda e: e.tensor_copy(out=zin[:, :, 0:1], in_=shp[:, :, b:b + 1]), [shp], [zin])
        if getattr(c, 'rstop', 99) <= 0.5: return
        k.op('dve', lambda e: e.tensor_tensor(out=t24[:, :, 0:n], in0=zin[:, :, 0:n], in1=zin[:, :, 1:n + 1], op=ALU.subtract), [zin], [t24])
        k.op('pool', lambda e: e.tensor_tensor(out=t24[:, :, 0:n], in0=t24[:, :, 0:n], in1=mu[:, :, 0:1].broadcast_to([128, 27, n]), op=ALU.mult), [t24, mu], [t24])
        k.op('dve', lambda e: e.tensor_tensor(out=zm[:, :, 0:n], in0=t24[:, :, 0:n], in1=zin[:, :, 1:n + 1], op=ALU.add), [t24, zin], [zm])
        if getattr(c, 'rstop', 99) <= 1: return
        k.op('act', lambda e: e.activation(out=t24[0:64, 24, 0:n], in_=zm[0:64, 24, 0:n], func=AF.Tanh), [zm], [t24])
        k.op('act', lambda e: e.activation(out=t24[:, 25, 0:n], in_=zm[:, 25, 0:n], func=AF.Sigmoid), [zm], [t24])
        k.op('act', lambda e: e.activation(out=t24[0:32, 26, 0:n], in_=zm[0:32, 26, 0:n], func=AF.Sigmoid), [zm], [t24])
        for g in range(2):
            k.op('pe', lambda e: e.matmul(PB[g][0:n, :], lhsT=t24[0:64, 24, 0:n], rhs=w2[:, g * 512:(g + 1) * 512], start=True, stop=True), [t24, w2], [PB[g]])
            k.op('dve', lambda e: e.tensor_tensor(out=ldt[0:n, g * 512:(g + 1) * 512], in0=PB[g][0:n, :], in1=w0B[0:n, g * 512:(g + 1) * 512], op=ALU.add), [PB[g], w0B], [ldt])
        k.op('act', lambda e: e.activation(out=ldt[0:n, :], in_=ldt[0:n, :], func=AF.Sigmoid), [ldt], [ldt])
        k.op('dve', lambda e: e.tensor_scalar(out=ldt[0:n, :], in0=ldt[0:n, :], scalar1=-math.exp(-0.5), scalar2=None, op0=ALU.mult), [ldt], [ldt])
        if getattr(c, 'rstop', 99) <= 2: return
        for f in range(8):
            k.op('pe', lambda e: e.matmul(pv(1, f, 128, n), lhsT=ldt[0:n, f * 128:(f + 1) * 128], rhs=UI[0:n, 0:n], start=True, stop=True), [ldt, UI], [ptl(1, f // 4)])
            k.op('pe', lambda e: e.matmul(pv(2, f, 128, n), lhsT=ldt[0:n, f * 128:(f + 1) * 128], rhs=US[0:n, 0:n], start=True, stop=True), [ldt, US], [ptl(2, f // 4)])
        for half in range(2):
            hs = slice(4 * half, 4 * half + 4)
            k.op('act', lambda e: e.activation(out=EL[:, hs, 0:n], in_=pbank(1, half, 128, n), func=AF.Exp), [ptl(1, half)], [EL])
            k.op('act', lambda e: e.activation(out=ELi[:, hs, 0:n], in_=pbank(1, half, 128, n), func=AF.Exp, scale=-1.0), [ptl(1, half)], [ELi])
            k.op('act', lambda e: e.activation(out=ELx[:, hs, 0:n], in_=pbank(2, half, 128, n), func=AF.Exp), [ptl(2, half)], [ELx])
        if getattr(c, 'rstop', 99) <= 3: return
        for f in range(8):
            k.op('pe', lambda e: e.matmul(pv(0, f, 128, n), lhsT=a2[64:128, f * 128:(f + 1) * 128], rhs=zm[64:128, 24, 0:n], start=True, stop=True), [a2, zm], [ptl(0, f // 4)])
            k.op('pe', lambda e: e.matmul(pv(3, f, 128, n), lhsT=g2a[:, f * 128:(f + 1) * 128], rhs=t24[:, 25, 0:n], start=True, stop=False), [g2a, t24], [ptl(3, f // 4)])
            k.op('pe', lambda e: e.matmul(pv(3, f, 128, n), lhsT=g2b[:, f * 128:(f + 1) * 128], rhs=t24[0:32, 26, 0:n], start=False, stop=True), [g2b, t24], [ptl(3, f // 4)])
        for half in range(2):
            hs = slice(4 * half, 4 * half + 4)
            k.op('dve', lambda e: e.tensor_tensor(out=av[:, hs, 0:n], in0=pbank(0, half, 128, n), in1=par[:, hs, 0:1].broadcast_to([128, 4, n]), op=ALU.add), [ptl(0, half), par], [av])
            k.op('act', lambda e: e.copy(out=gate[:, hs, 0:n], in_=pbank(3, half, 128, n)), [ptl(3, half)], [gate])
        k.op('act', lambda e: e.activation(out=av[:, :, 0:n], in_=av[:, :, 0:n], func=AF.Sigmoid), [av], [av])
        R_ = zm[:, 0:8, 0:n]; K_ = zm[:, 8:16, 0:n]; V_ = zm[:, 16:24, 0:n]
        def pb(j):
            return par[:, :, j:j + 1].broadcast_to([128, 8, n])
        if getattr(c, 'rstop', 99) <= 4: return
        k.op('dve', lambda e: e.tensor_tensor(out=kk[:, :, 0:n], in0=K_, in1=pb(1), op=ALU.mult), [zm, par], [kk])
        k.op('act', lambda e: e.activation(out=t24[:, 0:8, 0:n], in_=kk[:, :, 0:n], func=AF.Square), [kk], [t24])
        for half in range(2):
            k.op('pe', lambda e: e.matmul(pbank(0, half, 128, n), lhsT=BD[:, :], rhs=t24[:, 4 * half:4 * half + 4, 0:n], start=True, stop=True), [BD, t24], [ptl(0, half)])
        for half in range(2):
            hs = slice(4 * half, 4 * half + 4)
            k.op('act', lambda e: e.activation(out=t24[:, hs, 0:n], in_=pbank(0, half, 128, n), func=AF.Sqrt, bias=c.eps6[:, 0:1], scale=1.0), [ptl(0, half), c.eps6], [t24])
        k.op('dve', lambda e: e.reciprocal(out=t24[:, 0:8, 0:n], in_=t24[:, 0:8, 0:n]), [t24], [t24])
        k.op('dve', lambda e: e.tensor_tensor(out=kk[:, :, 0:n], in0=kk[:, :, 0:n], in1=t24[:, 0:8, 0:n], op=ALU.mult), [kk, t24], [kk])
        k.op('dve', lambda e: e.scalar_tensor_tensor(out=t24[:, 8:16, 0:n], in0=av[:, :, 0:n], scalar=-1.0, in1=pb(2), op0=ALU.add, op1=ALU.mult), [av, par], [t24])
        k.op('dve', lambda e: e.scalar_tensor_tensor(out=t24[:, 8:16, 0:n], in0=t24[:, 8:16, 0:n], scalar=1.0, in1=K_, op0=ALU.add, op1=ALU.mult), [t24, zm], [t24])
        KM = t24[:, 8:16, 0:n]
        k.op('dve', lambda e: e.tensor_tensor(out=t24[:, 16:24, 0:n], in0=R_, in1=KM, op=ALU.mult), [zm, t24], [t24])
        k.op('pool', lambda e: e.tensor_tensor(out=t24[:, 16:24, 0:n], in0=t24[:, 16:24, 0:n], in1=pb(3), op=ALU.mult), [t24, par], [t24])
        for half in range(2):
            k.op('pe', lambda e: e.matmul(pbank(3, half, 128, n), lhsT=BD[:, :], rhs=t24[:, 16 + 4 * half:16 + 4 * half + 4, 0:n], start=True, stop=True), [BD, t24], [ptl(3, half)])
        for half in range(2):
            hs = slice(4 * half, 4 * half + 4)
            k.op('dve', lambda e: e.tensor_tensor(out=bon[:, hs, 0:n], in0=pbank(3, half, 128, n), in1=zm[:, 16 + 4 * half:16 + 4 * half + 4, 0:n], op=ALU.mult), [ptl(3, half), zm], [bon])
        if getattr(c, 'rstop', 99) <= 5: return
        k.op('dve', lambda e: e.tensor_tensor(out=rt_[:, :, 0:n], in0=R_, in1=EL[:, :, 0:n], op=ALU.mult), [zm, EL], [rt_])
        k.op('pool', lambda e: e.tensor_tensor(out=kt_[:, :, 0:n], in0=KM, in1=ELi[:, :, 0:n], op=ALU.mult), [t24, ELi], [kt_])
        k.op('dve', lambda e: e.tensor_tensor(out=bt_[:, :, 0:n], in0=kk[:, :, 0:n], in1=av[:, :, 0:n], op=ALU.mult), [kk, av], [bt_])
        k.op('pool', lambda e: e.tensor_tensor(out=bt_[:, :, 0:n], in0=bt_[:, :, 0:n], in1=ELi[:, :, 0:n], op=ALU.mult), [bt_, ELi], [bt_])
        k.op('dve', lambda e: e.scalar_tensor_tensor(out=at_[:, :, 0:n], in0=kk[:, :, 0:n], scalar=-1.0, in1=ELx[:, :, 0:n], op0=ALU.mult, op1=ALU.mult), [kk, ELx], [at_])
        if getattr(c, 'rstop', 99) <= 6: return
        for (src, dst, pr, eng) in ((zm, Vtok, 0, 'dve'), (kt_, Ktok, 1, 'act'), (bt_, Btok, 2, 'dve')):
            for f in range(8):
                sap = src[:, 16 + f, 0:n] if src is zm else src[:, f, 0:n]
                k.op('pe', lambda e: e.transpose(out=pv(pr, f, n, 128), in_=sap, identity=c.idf[:, :]), [src, c.idf], [ptl(pr, f // 4)])
            for half in range(2):
                hs = slice(4 * half, 4 * half + 4)
                if eng == 'dve':
                    k.op('dve', lambda e: e.tensor_copy(out=dst[0:n, hs, :], in_=pbank(pr, half, n, 128)), [ptl(pr, half)], [dst])
                else:
                    k.op('act', lambda e: e.copy(out=dst[0:n, hs, :], in_=pbank(pr, half, n, 128)), [ptl(pr, half)], [dst])
        if getattr(c, 'rstop', 99) <= 7: return
        for hg in range(4):
            fts = slice(2 * hg, 2 * hg + 2)
            P0 = PB[0]; P1 = PB[1]; P2 = PB[2]; P3 = PB[3]; P4 = PB[4]; P5 = PB[5]; P6 = PB[6]; P7 = PB[7]
            def v4(P, rows, cols):
                return P[0:rows, :].rearrange("p (h i) -> p h i", i=128)[:, :, 0:cols]
            pm4 = PM[:, :].unsqueeze(1).unsqueeze(3).broadcast_to([128, 2, 2, n])
            k.op('dve', lambda e: e.tensor_tensor(out=btM[:, :, 0:n].rearrange("p (f w) t -> p f w t", w=2), in0=bt_[:, fts, 0:n].unsqueeze(2).broadcast_to([128, 2, 2, n]), in1=pm4, op=ALU.mult), [bt_, PM], [btM])
            k.op('pool', lambda e: e.tensor_tensor(out=ktM[:, :, 0:n].rearrange("p (f w) t -> p f w t", w=2), in0=kt_[:, fts, 0:n].unsqueeze(2).broadcast_to([128, 2, 2, n]), in1=pm4, op=ALU.mult), [kt_, PM], [ktM])
            for (ltM, rt, dst, msk, P) in ((btM, at_, NT, US, P4), (ktM, at_, Aak, US, P5), (btM, rt_, Arb, UI, P6), (ktM, rt_, Ark, UI, P7)):
                for hh in range(4):
                    f = 2 * hg + hh // 2
                    k.op('pe', lambda e: e.matmul(P[0:n, hh * 128:hh * 128 + n], lhsT=ltM[:, hh, 0:n], rhs=rt[:, f, 0:n], start=True, stop=True), [ltM, rt], [P])
                k.op('dve', lambda e: e.tensor_tensor(out=dst[0:n, :, 0:n], in0=v4(P, n, n), in1=msk[0:n, 0:n].unsqueeze(1).broadcast_to([n, 4, n]), op=ALU.mult), [P, msk], [dst])
            for hh in range(4):
                k.op('pe', lambda e: e.transpose(out=P0[0:n, hh * 128:hh * 128 + n], in_=NT[0:n, hh, 0:n], identity=c.idf[0:n, 0:n]), [NT, c.idf], [P0])
            k.op('act', lambda e: e.copy(out=Nm[0:n, :, 0:n], in_=v4(P0, n, n)), [P0], [Nm])
            k.op('dve', lambda e: e.tensor_tensor(out=PT[0:n, :, 0:n], in0=NT[0:n, :, 0:n], in1=c.idf[0:n, 0:n].unsqueeze(1).broadcast_to([n, 4, n]), op=ALU.add), [NT, c.idf], [PT])
            X, XT, X2, XT2 = Nm, NT, Nm2, NT2
            for it in range(steps):
                lastit = (it == steps - 1)
                for hh in range(4):
                    k.op('pe', lambda e: e.matmul(P1[0:n, hh * 128:hh * 128 + n], lhsT=XT[0:n, hh, 0:n], rhs=X[0:n, hh, 0:n], start=True, stop=True), [XT, X], [P1])
                    if not lastit:
                        k.op('pe', lambda e: e.matmul(P2[0:n, hh * 128:hh * 128 + n], lhsT=X[0:n, hh, 0:n], rhs=XT[0:n, hh, 0:n], start=True, stop=True), [XT, X], [P2])
                k.op('act', lambda e: e.copy(out=X2[0:n, :, 0:n], in_=v4(P1, n, n)), [P1], [X2])
                if not lastit:
                    k.op('dve', lambda e: e.tensor_copy(out=XT2[0:n, :, 0:n], in_=v4(P2, n, n)), [P2], [XT2])
                for hh in range(4):
                    k.op('pe', lambda e: e.matmul(P3[0:n, hh * 128:hh * 128 + n], lhsT=X2[0:n, hh, 0:n], rhs=PT[0:n, hh, 0:n], start=True, stop=True), [X2, PT], [P3])
                k.op('dve', lambda e: e.tensor_tensor(out=PT[0:n, :, 0:n], in0=PT[0:n, :, 0:n], in1=v4(P3, n, n), op=ALU.add), [PT, P3], [PT])
                X, X2 = X2, X
                XT, XT2 = XT2, XT
            if getattr(c, 'rstop', 99) <= 9: return
            for fl in range(2):
                f = 2 * hg + fl
                k.op('pe', lambda e: e.matmul(P4[0:n, fl * 128:fl * 128 + 128], lhsT=at_[:, f, 0:n], rhs=Zbd[:, f, :], start=True, stop=False), [at_, Zbd], [P4])
                for two in range(2):
                    hh = 2 * fl + two
                    k.op('pe', lambda e: e.matmul(P4[0:n, fl * 128 + two * 64:fl * 128 + two * 64 + 64], lhsT=Aak[0:n, hh, 0:n], rhs=Vtok[0:n, f, two * 64:two * 64 + 64], start=False, stop=(two == 1)), [Aak, Vtok], [P4])
            k.op('dve', lambda e: e.tensor_copy(out=X1s[0:n, :, :], in_=P4[0:n, 0:256].rearrange("p (h e) -> p h e", e=64)), [P4], [X1s])
            for hh in range(4):
                k.op('pe', lambda e: e.matmul(P5[0:n, hh * 64:hh * 64 + 64], lhsT=PT[0:n, hh, 0:n], rhs=X1s[0:n, hh, :], start=True, stop=True), [PT, X1s], [P5])
            k.op('act', lambda e: e.copy(out=Uu[0:n, :, :], in_=P5[0:n, 0:256].rearrange("p (f e) -> p f e", e=128)), [P5], [Uu])
            for two in range(2):
                k.op('dve', lambda e: e.tensor_copy(out=Upad[0:n, :, :].rearrange("p (f w) e -> p f w e", w=2)[:, :, two, two * 64:two * 64 + 64], in_=P5[0:n, 0:256].rearrange("p (f w e) -> p f w e", w=2, e=64)[:, :, two, :]), [P5], [Upad])
                k.op('pool', lambda e: e.tensor_copy(out=Vpad[0:n, :, :].rearrange("p (f w) e -> p f w e", w=2)[:, :, two, two * 64:two * 64 + 64], in_=Vtok[0:n, fts, two * 64:two * 64 + 64]), [Vtok], [Vpad])
            for fl in range(2):
                f = 2 * hg + fl
                po = P6[:, fl * 128:fl * 128 + n]
                k.op('pe', lambda e: e.matmul(po, lhsT=Zbd[:, f, :], rhs=rt_[:, f, 0:n], start=True, stop=False), [Zbd, rt_], [P6])
                for two in range(2):
                    hh = 2 * fl + two
                    k.op('pe', lambda e: e.matmul(po, lhsT=Upad[0:n, hh, :], rhs=Arb[0:n, hh, 0:n], start=False, stop=False), [Upad, Arb], [P6])
                    k.op('pe', lambda e: e.matmul(po, lhsT=Vpad[0:n, hh, :], rhs=Ark[0:n, hh, 0:n], start=False, stop=(two == 1)), [Vpad, Ark], [P6])
            k.op('dve', lambda e: e.tensor_copy(out=yT[:, fts, 0:n], in_=P6[:, 0:256].rearrange("p (f t) -> p f t", t=128)[:, :, 0:n]), [P6], [yT])
            for fl in range(2):
                f = 2 * hg + fl
                pz = P7[:, fl * 128:fl * 128 + 128]
                k.op('pe', lambda e: e.matmul(pz, lhsT=Btok[0:n, f, :], rhs=Uu[0:n, fl, :], start=True, stop=False), [Btok, Uu], [P7])
                k.op('pe', lambda e: e.matmul(pz, lhsT=Ktok[0:n, f, :], rhs=Vtok[0:n, f, :], start=False, stop=True), [Ktok, Vtok], [P7])
            k.op('dve', lambda e: e.tensor_tensor(out=Wt[:, :, :], in0=P7[:, 0:256].rearrange("p (f e) -> p f e", e=128), in1=BD[:, :].unsqueeze(1).broadcast_to([128, 2, 128]), op=ALU.mult), [P7, BD], [Wt])
            k.op('dve', lambda e: e.tensor_tensor(out=Zbd[:, fts, :], in0=Zbd[:, fts, :], in1=Wt[:, :, :], op=ALU.add), [Zbd, Wt], [Zbd])
            k.op('pool', lambda e: e.tensor_tensor(out=Zbd[:, fts, :], in0=Zbd[:, fts, :], in1=EL[:, fts, n - 1:n].broadcast_to([128, 2, 128]), op=ALU.mult), [Zbd, EL], [Zbd])
        if getattr(c, 'rstop', 99) <= 10: return
        for half in range(2):
            k.op('pe', lambda e: e.matmul(pbank(0, half, 128, n), lhsT=BD[:, :], rhs=yT[:, 4 * half:4 * half + 4, 0:n], start=True, stop=True), [BD, yT], [ptl(0, half)])
        for half in range(2):
            hs = slice(4 * half, 4 * half + 4)
            k.op('dve', lambda e: e.scalar_tensor_tensor(out=yT[:, hs, 0:n], in0=pbank(0, half, 128, n), scalar=-1.0 / 64, in1=yT[:, hs, 0:n], op0=ALU.mult, op1=ALU.add), [ptl(0, half), yT], [yT])
        k.op('act', lambda e: e.activation(out=t24[:, 0:8, 0:n], in_=yT[:, :, 0:n], func=AF.Square), [yT], [t24])
        for half in range(2):
            k.op('pe', lambda e: e.matmul(pbank(1, half, 128, n), lhsT=BD[:, :], rhs=t24[:, 4 * half:4 * half + 4, 0:n], start=True, stop=True), [BD, t24], [ptl(1, half)])
        for half in range(2):
            hs = slice(4 * half, 4 * half + 4)
            k.op('act', lambda e: e.activation(out=t24[:, hs, 0:n], in_=pbank(1, half, 128, n), func=AF.Sqrt, bias=c.epsgn[:, 0:1], scale=1.0 / 64), [ptl(1, half), c.epsgn], [t24])
        k.op('dve', lambda e: e.reciprocal(out=t24[:, 0:8, 0:n], in_=t24[:, 0:8, 0:n]), [t24], [t24])
        k.op('dve', lambda e: e.tensor_tensor(out=yT[:, :, 0:n], in0=yT[:, :, 0:n], in1=t24[:, 0:8, 0:n], op=ALU.mult), [yT, t24], [yT])
        k.op('pool', lambda e: e.tensor_tensor(out=yT[:, :, 0:n], in0=yT[:, :, 0:n], in1=pb(4), op=ALU.mult), [yT, par], [yT])
        k.op('dve', lambda e: e.tensor_tensor(out=yT[:, :, 0:n], in0=yT[:, :, 0:n], in1=pb(5), op=ALU.add), [yT, par], [yT])
        k.op('dve', lambda e: e.tensor_tensor(out=yT[:, :, 0:n], in0=yT[:, :, 0:n], in1=bon[:, :, 0:n], op=ALU.add), [yT, bon], [yT])
        k.op('dve', lambda e: e.tensor_tensor(out=ost[:, :, 0:n], in0=yT[:, :, 0:n], in1=gate[:, :, 0:n], op=ALU.mult), [yT, gate], [ost])
        k.dma('sp', mixT.h.ap().rearrange("k p t -> p k t")[:, kc0:kc0 + 8, t0:t0 + n], ost[:, :, 0:n], [ost], [mixT], ost)
    def load_Z(b):
        for two in range(2):
            k.dma('sp', zio[two * 64:two * 64 + 64, :, two * 64:two * 64 + 64], st_S[1][b].rearrange("(f w) v kk -> w v f kk", w=2)[two], [st_S[0]], [zio], zio)
        for f in range(8):
            k.op('pe', lambda e: e.transpose(out=PB[f // 4][:, (f % 4) * 128:(f % 4) * 128 + 128], in_=zio[:, f, :], identity=c.idf[:, :]), [zio, c.idf], [PB[f // 4]])
        for g in range(2):
            k.op('dve', lambda e: e.tensor_copy(out=Zbd[:, 4 * g:4 * g + 4, :], in_=PB[g][:, :].rearrange("p (f e) -> p f e", e=128)), [PB[g]], [Zbd])
    def store_Z(dst):
        for f in range(8):
            k.op('pe', lambda e: e.transpose(out=PB[f // 4][:, (f % 4) * 128:(f % 4) * 128 + 128], in_=Zbd[:, f, :], identity=c.idf[:, :]), [Zbd, c.idf], [PB[f // 4]])
        for g in range(2):
            k.op('dve', lambda e: e.tensor_copy(out=zio[:, 4 * g:4 * g + 4, :], in_=PB[g][:, :].rearrange("p (f e) -> p f e", e=128)), [PB[g]], [zio])
        for two in range(2):
            k.dma('sp', dst[1].rearrange("(f w) v kk -> w v f kk", w=2)[two], zio[two * 64:two * 64 + 64, :, two * 64:two * 64 + 64], [zio], [dst[0]], zio)
    k.op('pool', lambda e: e.memset(zio[:], 0.0), [], [zio])
    k.op('pool', lambda e: e.memset(Upad[:], 0.0), [], [Upad])
    k.op('pool', lambda e: e.memset(Vpad[:], 0.0), [], [Vpad])
    k.op('pool', lambda e: e.memset(Zbd[:], 0.0), [], [Zbd])
    for t0 in range(0, T, 128):
        chunk(t0, 128, 'zero' if t0 == 0 else None, 6)
        if t0 + 128 == T:
            k.op('pool', lambda e: e.tensor_copy(out=sho[:, :, 0:1], in_=zin[:, :, 128:129]), [zin], [sho])
    if not getattr(c, 'rskip_store', False):
        store_Z(pS_out)
    for b in range(0 if getattr(c, 'skip_sample', False) else NS):
        load_Z(b)
        chunk(T + 8 * b, 8, ('state', b), 2)
        k.op('pool', lambda e: e.tensor_copy(out=sho[:, :, b + 1:b + 2], in_=zin[:, :, 8:9]), [zin], [sho])
        store_Z((sS_out[0], sS_out[1][b]))
    k.phase_end()
    k.phase_begin()
    Rt = k.sb('rw_shrt', [NSH, 27 * 128], F32)
    PP = [k.ps(f'rw_shp{i}', [128, 512], F32) for i in range(4)]
    for f0 in range(0, 27, 4):
        nf = min(4, 27 - f0)
        P = PP[(f0 // 4) % 4]
        for f in range(nf):
            k.op('pe', lambda e: e.transpose(out=P[0:NSH, f * 128:(f + 1) * 128], in_=sho[:, f0 + f, :], identity=c.idf[:, :]), [sho, c.idf], [P])
        k.op('dve', lambda e: e.tensor_copy(out=Rt[0:NSH, f0 * 128:(f0 + nf) * 128], in_=P[0:NSH, 0:nf * 128]), [P], [Rt])
    k.dma('sp', pshift_out[1].unsqueeze(0), Rt[0:1, 0:3360], [Rt], [pshift_out[0]], Rt)
    for b in range(NS):
        k.dma('sp', sshift_out[1][b].unsqueeze(0), Rt[b + 1:b + 2, 0:3360], [Rt], [sshift_out[0]], Rt)
    k.phase_end()
def run_mixers(k, c, li, projT, mixT):
    IN, OUT = c.IN, c.OUT
    def I(nm): return (IN[nm], IN[nm].h.ap()[li])
    def O(nm): return (OUT[nm], OUT[nm].h.ap()[li])
    k.scope_begin()
    gdn_phase(k, c, li, projT, mixT, SEQ, NB_S, IN, O('p_gdn'), O('p_gdn_conv'), O('s_gdn'), O('s_gdn_conv'), I('st_gdn'), I('st_gdn_conv'),
              OFF_GDN_QKV, OFF_GDN_Z, OFF_GDN_B, kc0=0)
    k.scope_end()
    k.scope_begin()
    rwkv_phase(k, c, li, projT, mixT, SEQ, NB_S, IN, O('p_rwkv'), O('p_rwkv_shift'), O('s_rwkv'), O('s_rwkv_shift'), I('st_rwkv'), I('st_rwkv_shift'),
               OFF_RWKV, kc0=8)
    k.scope_end()
    k.scope_begin()
    ssd_phase(k, c, li, projT, mixT, SEQ, NB_S, IN, O('p_ssm'), O('p_ssm_conv'), O('s_ssm'), O('s_ssm_conv'), I('st_ssm'), I('st_ssm_conv'),
              OFF_SSM_Z, OFF_SSM_XBC, OFF_SSM_DT, kc0=16)
    k.scope_end()
    k.scope_begin()
    swa_phase(k, c, li, projT, mixT, SEQ, NB_S, OFF_SWA, O('p_swa_k'), O('p_swa_v'), O('s_swa_k'), O('s_swa_v'), I('c_swa_k'), I('c_swa_v'), kc0=24)
    k.scope_end()
_OUT_ORDER = ['y_prompt', 'y_sample', 'p_gdn', 'p_gdn_conv', 'p_rwkv', 'p_rwkv_shift', 'p_ssm', 'p_ssm_conv', 'p_swa_k', 'p_swa_v',
              'p_ffn_conv', 's_gdn', 's_gdn_conv', 's_rwkv', 's_rwkv_shift', 's_ssm', 's_ssm_conv', 's_swa_k', 's_swa_v', 's_ffn_conv']
MIXERS = True

def kernel(**inp):
    f32 = np.float32
    A = {k_: np.asarray(v) for k_, v in inp.items()}
    shared = {}
    for nm in ['norm_mix_pre', 'norm_mix_post', 'norm_ffn_pre', 'norm_ffn_post', 'w_in', 'w_out', 'gdn_conv_w', 'gdn_A_log', 'gdn_dt_bias',
               'gdn_norm_w', 'rwkv_mu', 'rwkv_w0', 'rwkv_w2', 'rwkv_a0', 'rwkv_a2', 'rwkv_g2', 'rwkv_k_k', 'rwkv_k_a', 'rwkv_ln_w', 'rwkv_ln_b',
               'ssm_conv_w', 'ssm_conv_b', 'ssm_dt_bias', 'ssm_A_log', 'ssm_D', 'ssm_norm_w', 'ffn_w_up', 'ffn_conv_w', 'ffn_conv_b', 'ffn_w_down']:
        shared[nm] = np.ascontiguousarray(A[nm], dtype=f32)
    shared['rwkv_r_k'] = np.ascontiguousarray(A['rwkv_r_k'].reshape(DEPTH, 1024), dtype=f32)
    maps = []
    for cidx in range(4):
        m = dict(shared)
        b0 = 2 * cidx
        m['xin'] = np.ascontiguousarray(np.concatenate([A['x_prompt'][cidx], A['x_sample'][b0:b0 + 2].reshape(16, D_MODEL)], axis=0), dtype=f32)
        m['st_gdn'] = np.ascontiguousarray(A['state_gdn'][:, b0:b0 + 2]); m['st_gdn_conv'] = np.ascontiguousarray(A['state_gdn_conv'][:, b0:b0 + 2])
        m['st_rwkv'] = np.ascontiguousarray(A['state_rwkv'][:, b0:b0 + 2]); m['st_rwkv_shift'] = np.ascontiguousarray(A['state_rwkv_shift'][:, b0:b0 + 2])
        m['st_ssm'] = np.ascontiguousarray(A['state_ssm'][:, b0:b0 + 2]); m['st_ssm_conv'] = np.ascontiguousarray(A['state_ssm_conv'][:, b0:b0 + 2])
        m['c_swa_k'] = np.ascontiguousarray(A['cache_swa_k'][:, b0:b0 + 2].reshape(DEPTH, 2, 2048, 1024))
        m['c_swa_v'] = np.ascontiguousarray(A['cache_swa_v'][:, b0:b0 + 2].reshape(DEPTH, 2, 2048, 1024))
        m['st_ffn_conv'] = np.ascontiguousarray(A['state_ffn_conv'][:, b0:b0 + 2])
        maps.append(m)
    maps = maps + maps
    nc = bass.Bass("TRN2", target_bir_lowering=False)
    with ExitStack() as es:
        build_program(nc, es, mixers=MIXERS)
    res = run_bass_kernel_spmd(nc, maps, core_ids=list(range(8)))
    R = res.results
    L = DEPTH
    out = {}
    out['y_prompt'] = np.stack([R[c_]['y'][:SEQ] for c_ in range(4)])
    out['y_sample'] = np.concatenate([R[c_]['y'][SEQ:].reshape(2, 8, D_MODEL) for c_ in range(4)], axis=0)
    def pst(nm, shape):
        return np.stack([R[c_][nm].reshape((L,) + shape) for c_ in range(4)], axis=1)
    def sst(nm, shape):
        return np.concatenate([R[c_][nm].reshape((L, 2) + shape) for c_ in range(4)], axis=1)
    out['p_gdn'] = pst('p_gdn', (8, 128, 128)); out['p_gdn_conv'] = pst('p_gdn_conv', (3, 3072)); out['p_rwkv'] = pst('p_rwkv', (16, 64, 64))
    out['p_rwkv_shift'] = pst('p_rwkv_shift', (3360,)); out['p_ssm'] = pst('p_ssm', (16, 64, 128)); out['p_ssm_conv'] = pst('p_ssm_conv', (3, 1536))
    out['p_swa_k'] = pst('p_swa_k', (2048, 8, 128)); out['p_swa_v'] = pst('p_swa_v', (2048, 8, 128)); out['p_ffn_conv'] = pst('p_ffn_conv', (2, D_FF))
    out['s_gdn'] = sst('s_gdn', (8, 128, 128)); out['s_gdn_conv'] = sst('s_gdn_conv', (3, 3072)); out['s_rwkv'] = sst('s_rwkv', (16, 64, 64))
    out['s_rwkv_shift'] = sst('s_rwkv_shift', (3360,)); out['s_ssm'] = sst('s_ssm', (16, 64, 128)); out['s_ssm_conv'] = sst('s_ssm_conv', (3, 1536))
    out['s_swa_k'] = sst('s_swa_k', (8, 8, 128)); out['s_swa_v'] = sst('s_swa_v', (8, 8, 128)); out['s_ffn_conv'] = sst('s_ffn_conv', (2, D_FF))
    return tuple(np.ascontiguousarray(out[n_], dtype=f32) for n_ in _OUT_ORDER)
```
